# Optimizing a Trainium2 kernel written in Bass

```python
import math
import jax, jax.numpy as jnp
from jax import lax
import numpy as np

D_MODEL = 4096
BATCH = 4
SEQ = 4096
DEPTH = 1
DEC_BATCH = 32
DEC_SEQ = 64
PAST_LEN = 2048

CHUNK = 64
Q_BLOCK = 128
N_HEADS = 8
N_KV_HEADS = 4
GROUP = N_HEADS // N_KV_HEADS
HEAD_DIM = 128
V_DIM = 2 * HEAD_DIM
ATTN_WIDTH = N_HEADS * V_DIM
LRU_WIDTH = D_MODEL - ATTN_WIDTH
LRU_BLOCKS = 16
LRU_BLOCK = LRU_WIDTH // LRU_BLOCKS
CONV_WIDTH = 4
C_LRU = 8.0
ROT_DIM = HEAD_DIM // 4
ROPE_THETA = 500000.0
D_FF = 4 * D_MODEL
EPS = 1e-6
Q_COLS = N_HEADS * 2 * HEAD_DIM
K_COLS = N_KV_HEADS * 2 * HEAD_DIM
V_COLS = N_KV_HEADS * V_DIM
IN_COLS = Q_COLS + K_COLS + V_COLS + 2 * LRU_WIDTH

kernel_name = 'hymba_diffattn_rglru_stream_step'


def lambda_init_fn(layer_idx):
    return 0.8 - 0.6 * math.exp(-0.3 * layer_idx)


def rms_norm(x, g):
    xf = x.astype(jnp.float32)
    y = xf * lax.rsqrt(jnp.mean(xf * xf, axis=-1, keepdims=True) + EPS)
    return (y * g.astype(jnp.float32)).astype(x.dtype)


def rope_partial(x, pos):
    half = ROT_DIM // 2
    T = x.shape[1]
    inv = ROPE_THETA ** (-jnp.arange(half, dtype=jnp.float32) / half)
    ang = pos.astype(jnp.float32)[:, None] * inv[None, :]
    bshape = (1, T) + (1,) * (x.ndim - 3) + (half,)
    cos = jnp.cos(ang).reshape(bshape)
    sin = jnp.sin(ang).reshape(bshape)
    xf = x.astype(jnp.float32)
    x1 = xf[..., :half]
    x2 = xf[..., half:ROT_DIM]
    out = jnp.concatenate([x1 * cos - x2 * sin, x2 * cos + x1 * sin, xf[..., ROT_DIM:]], axis=-1)
    return out.astype(x.dtype)


def diff_attention(q, k, v, q_pos, k_pos, lam):
    s = jnp.einsum('bqhgcd,bkhcd->bhgcqk', q, k).astype(jnp.float32) * (HEAD_DIM ** -0.5)
    chunk_end = (q_pos // CHUNK + 1) * CHUNK
    visible = k_pos[None, :] < chunk_end[:, None]
    s = jnp.where(visible, s, -1e30)
    p = jax.nn.softmax(s, axis=-1)
    a = p[:, :, :, 0] - lam * p[:, :, :, 1]
    return jnp.einsum('bhgqk,bkhe->bqhge', a.astype(v.dtype), v)


def prompt_attention(q, k, v, lam):
    B, T = q.shape[0], q.shape[1]
    nb = T // Q_BLOCK
    qb = q.reshape((B, nb, Q_BLOCK) + q.shape[2:]).swapaxes(0, 1)
    k_pos = jnp.arange(T)

    def one_block(args):
        q_i, i = args
        q_pos = i * Q_BLOCK + jnp.arange(Q_BLOCK)
        return diff_attention(q_i, k, v, q_pos, k_pos, lam)

    ob = lax.map(one_block, (qb, jnp.arange(nb)))
    return ob.swapaxes(0, 1).reshape((B, T) + ob.shape[3:])


def causal_conv(x, prev, w, b):
    T = x.shape[1]
    xp = jnp.concatenate([prev.astype(x.dtype), x], axis=1)
    acc = b.astype(x.dtype)
    for j in range(CONV_WIDTH):
        acc = acc + xp[:, j:j + T] * w[j]
    return acc, xp[:, -(CONV_WIDTH - 1):]


def rg_lru(x, h0, ga_w, ga_b, gx_w, gx_b, lru_lam):
    B, T, _ = x.shape
    xb = x.reshape(B, T, LRU_BLOCKS, LRU_BLOCK)
    r = jax.nn.sigmoid((jnp.einsum('btnc,ncd->btnd', xb, ga_w) + ga_b).astype(jnp.float32)).reshape(B, T, LRU_WIDTH)
    i = jax.nn.sigmoid((jnp.einsum('btnc,ncd->btnd', xb, gx_w) + gx_b).astype(jnp.float32)).reshape(B, T, LRU_WIDTH)
    log_a = -C_LRU * r * jax.nn.softplus(-lru_lam.astype(jnp.float32))
    a = jnp.exp(log_a)
    u = jnp.sqrt(-jnp.expm1(2.0 * log_a)) * (i * x.astype(jnp.float32))

    def step(h, au):
        a_t, u_t = au
        h = a_t * h + u_t
        return h, h

    hT, hs = lax.scan(step, h0.astype(jnp.float32), (a.swapaxes(0, 1), u.swapaxes(0, 1)))
    return hs.swapaxes(0, 1).astype(x.dtype), hT


def trunk_layer(x, pos, k_past, v_past, conv_prev, h0, lp, lam_init):
    (g_mix, w_in, conv_w, conv_b, ga_w, ga_b, gx_w, gx_b, lru_lam,
     lq1, lk1, lq2, lk2, subln_g, w_out, g_mlp, w_up, w_down) = lp
    B, T, _ = x.shape
    xn = rms_norm(x, g_mix)
    proj = xn @ w_in
    o1 = Q_COLS
    o2 = o1 + K_COLS
    o3 = o2 + V_COLS
    o4 = o3 + LRU_WIDTH
    q = rope_partial(proj[..., :o1].reshape(B, T, N_KV_HEADS, GROUP, 2, HEAD_DIM), pos)
    k = rope_partial(proj[..., o1:o2].reshape(B, T, N_KV_HEADS, 2, HEAD_DIM), pos)
    v = proj[..., o2:o3].reshape(B, T, N_KV_HEADS, V_DIM)
    xr = proj[..., o3:o4]
    gr = proj[..., o4:]
    f32 = jnp.float32
    lam = (jnp.exp(jnp.sum(lq1.astype(f32) * lk1.astype(f32)))
           - jnp.exp(jnp.sum(lq2.astype(f32) * lk2.astype(f32))) + lam_init)
    if k_past is None:
        o = prompt_attention(q, k, v, lam)
    else:
        P = k_past.shape[1]
        k_all = jnp.concatenate([k_past.reshape(B, P, N_KV_HEADS, 2, HEAD_DIM).astype(k.dtype), k], axis=1)
        v_all = jnp.concatenate([v_past.astype(v.dtype), v], axis=1)
        o = diff_attention(q, k_all, v_all, pos, jnp.arange(P + T), lam)
    o = (rms_norm(o, subln_g) * (1.0 - lam_init)).reshape(B, T, ATTN_WIDTH)
    xc, conv_state = causal_conv(xr, conv_prev, conv_w, conv_b)
    hs, hT = rg_lru(xc, h0, ga_w, ga_b, gx_w, gx_b, lru_lam)
    y_lru = hs * jax.nn.gelu(gr)
    h = x + jnp.concatenate([o, y_lru], axis=-1) @ w_out
    z = rms_norm(h, g_mlp) @ w_up
    h = h + jnp.square(jax.nn.relu(z)) @ w_down
    return h, k.reshape(B, T, N_KV_HEADS, 2 * HEAD_DIM), v, conv_state, hT.astype(x.dtype)


def setup_inputs(seed: int = 0) -> dict:
    key = jax.random.key(seed)
    ks = jax.random.split(key, 26)
    f32 = jnp.float32

    def nrm(k, shape, s):
        return jax.random.normal(k, shape, f32) * s

    u = jax.random.uniform(ks[14], (DEPTH, LRU_WIDTH), f32, 0.9, 0.999)
    a = u ** (1.0 / C_LRU)
    lru_lambda = jnp.log(a) - jnp.log1p(-a)
    return {
        'x_prompt': nrm(ks[0], (BATCH, SEQ, D_MODEL), 1.0),
        'x_sample': nrm(ks[1], (DEC_BATCH, DEC_SEQ, D_MODEL), 1.0),
        'cache_k': nrm(ks[2], (DEPTH, DEC_BATCH, PAST_LEN, N_KV_HEADS, 2 * HEAD_DIM), 1.0),
        'cache_v': nrm(ks[3], (DEPTH, DEC_BATCH, PAST_LEN, N_KV_HEADS, V_DIM), 1.0),
        'state_conv': nrm(ks[4], (DEPTH, DEC_BATCH, CONV_WIDTH - 1, LRU_WIDTH), 1.0),
        'state_lru': nrm(ks[5], (DEPTH, DEC_BATCH, LRU_WIDTH), 0.5),
        'norm_mix': 1.0 + nrm(ks[6], (DEPTH, D_MODEL), 0.02),
        'w_in': nrm(ks[7], (DEPTH, D_MODEL, IN_COLS), D_MODEL ** -0.5),
        'conv_w': nrm(ks[8], (DEPTH, CONV_WIDTH, LRU_WIDTH), CONV_WIDTH ** -0.5),
        'conv_b': nrm(ks[9], (DEPTH, LRU_WIDTH), 0.01),
        'gate_a_w': nrm(ks[10], (DEPTH, LRU_BLOCKS, LRU_BLOCK, LRU_BLOCK), LRU_BLOCK ** -0.5),
        'gate_a_b': nrm(ks[11], (DEPTH, LRU_BLOCKS, LRU_BLOCK), 0.01),
        'gate_x_w': nrm(ks[12], (DEPTH, LRU_BLOCKS, LRU_BLOCK, LRU_BLOCK), LRU_BLOCK ** -0.5),
        'gate_x_b': nrm(ks[13], (DEPTH, LRU_BLOCKS, LRU_BLOCK), 0.01),
        'lru_lambda': lru_lambda,
        'lambda_q1': nrm(ks[15], (DEPTH, HEAD_DIM), 0.1),
        'lambda_k1': nrm(ks[16], (DEPTH, HEAD_DIM), 0.1),
        'lambda_q2': nrm(ks[17], (DEPTH, HEAD_DIM), 0.1),
        'lambda_k2': nrm(ks[18], (DEPTH, HEAD_DIM), 0.1),
        'subln_g': 1.0 + nrm(ks[19], (DEPTH, V_DIM), 0.02),
        'w_out': nrm(ks[20], (DEPTH, ATTN_WIDTH + LRU_WIDTH, D_MODEL), (ATTN_WIDTH + LRU_WIDTH) ** -0.5),
        'norm_mlp': 1.0 + nrm(ks[21], (DEPTH, D_MODEL), 0.02),
        'w_up': nrm(ks[22], (DEPTH, D_MODEL, D_FF), D_MODEL ** -0.5),
        'w_down': nrm(ks[23], (DEPTH, D_FF, D_MODEL), D_FF ** -0.5),
        'norm_final': 1.0 + nrm(ks[24], (D_MODEL,), 0.02),
    }


def reference(x_prompt, x_sample, cache_k, cache_v, state_conv, state_lru,
              norm_mix, w_in, conv_w, conv_b, gate_a_w, gate_a_b, gate_x_w, gate_x_b,
              lru_lambda, lambda_q1, lambda_k1, lambda_q2, lambda_k2, subln_g,
              w_out, norm_mlp, w_up, w_down, norm_final):
    Bp, Tp, _ = x_prompt.shape
    Bs, Ts, _ = x_sample.shape
    P = cache_k.shape[2]
    pos_p = jnp.arange(Tp)
    pos_s = P + jnp.arange(Ts)
    hp = x_prompt
    hs = x_sample
    kp_l, vp_l, cp_l, lp_l = [], [], [], []
    ks_l, vs_l, cs_l, ls_l = [], [], [], []
    for l in range(DEPTH):
        lp = (norm_mix[l], w_in[l], conv_w[l], conv_b[l], gate_a_w[l], gate_a_b[l],
              gate_x_w[l], gate_x_b[l], lru_lambda[l], lambda_q1[l], lambda_k1[l],
              lambda_q2[l], lambda_k2[l], subln_g[l], w_out[l], norm_mlp[l], w_up[l], w_down[l])
        lam_init = lambda_init_fn(l)
        conv0 = jnp.zeros((Bp, CONV_WIDTH - 1, LRU_WIDTH), x_prompt.dtype)
        h0 = jnp.zeros((Bp, LRU_WIDTH), jnp.float32)
        hp, kp, vp, cp, lpr = trunk_layer(hp, pos_p, None, None, conv0, h0, lp, lam_init)
        hs, kn, vn, cn, lsn = trunk_layer(hs, pos_s, cache_k[l], cache_v[l], state_conv[l], state_lru[l], lp, lam_init)
        kp_l.append(kp); vp_l.append(vp); cp_l.append(cp); lp_l.append(lpr)
        ks_l.append(kn); vs_l.append(vn); cs_l.append(cn); ls_l.append(lsn)
    y_prompt = rms_norm(hp, norm_final)
    y_sample = rms_norm(hs, norm_final)
    k_prompt = jnp.stack(kp_l)
    v_prompt = jnp.stack(vp_l)
    conv_prompt = jnp.stack(cp_l)
    lru_prompt = jnp.stack(lp_l)
    k_sample = jnp.stack(ks_l)
    v_sample = jnp.stack(vs_l)
    conv_sample = jnp.stack(cs_l)
    lru_sample = jnp.stack(ls_l)
    return (y_prompt, y_sample, k_prompt, v_prompt, conv_prompt, lru_prompt,
            k_sample, v_sample, conv_sample, lru_sample)
```

```python
import math
import os
from contextlib import ExitStack

import numpy as np
import concourse.bass as bass
import concourse.mybir as mybir
from concourse.bass_utils import run_bass_kernel_spmd

F32 = mybir.dt.float32
BF16 = mybir.dt.bfloat16
AF = mybir.ActivationFunctionType
ALU = mybir.AluOpType

D = 4096
NPRE = 2048
NMAIN = 2048
NSMP = 256
NQ = NMAIN + NSMP
NK = NPRE + NMAIN + NSMP
PAST = 2048
EPS = 1e-6
LAM_INIT = 0.2
SCALE = 128 ** -0.5
NSLOT = 3
SAME_ENGINE_SYNC = True
GELU_C = 2.0 * math.sqrt(2.0 / math.pi)


class Buf:
    __slots__ = ("name", "w", "r")

    def __init__(self, name=""):
        self.name = name
        self.w = None
        self.r = {}


class DSem:
    def __init__(self, sem):
        self.sem = sem
        self.count = 0


class Ent:
    __slots__ = ("fn", "waits", "inc", "val", "dsem")

    def __init__(self, fn):
        self.fn = fn
        self.waits = []
        self.inc = False
        self.val = None
        self.dsem = None


class _Rec:
    def __getattr__(self, name):
        def f(*a, **kw):
            self.__dict__["call"] = (name, a, kw)
        return f


COMPUTE = ("pe", "dve", "act", "pool")
ENGS = ("pe", "dve", "act", "pool", "sp")


class Trk:
    def __init__(self, nc, csem):
        self.nc = nc
        self.csem = csem
        self.base = {e: 0 for e in COMPUTE}
        self.dsems = []
        self.epoch = 0
        self._reset()

    def _reset(self):
        self.streams = {e: [] for e in ENGS}
        self.waited = {e: {p: -1 for p in COMPUTE} for e in ENGS}
        self.waitedD = {e: {} for e in ENGS}

    def new_dsem(self, sem):
        d = DSem(sem)
        self.dsems.append(d)
        return d

    def _add_wait(self, eng, ent, ref):
        if ref is None or ref[0] != self.epoch:
            return
        if ref[1] == "c":
            _, _, peng, idx = ref
            if peng == eng and (eng in ("pe", "sp") or not SAME_ENGINE_SYNC):
                return
            if idx <= self.waited[eng][peng]:
                return
            self.waited[eng][peng] = idx
            self.streams[peng][idx].inc = True
            ent.waits.append(("c", peng, idx))
        else:
            ds = ref[2]
            cnt = ds.count
            if self.waitedD[eng].get(ds, 0) >= cnt:
                return
            self.waitedD[eng][ds] = cnt
            ent.waits.append(("d", ds, cnt))

    def _record(self, eng, ent, reads, writes, ref_maker):
        deps = []
        for b in reads:
            if b.w is not None:
                deps.append(b.w)
        for b in writes:
            if b.w is not None:
                deps.append(b.w)
            deps.extend(b.r.values())
        for d in deps:
            self._add_wait(eng, ent, d)
        idx = len(self.streams[eng])
        self.streams[eng].append(ent)
        ref = ref_maker(idx)
        key = ref[2] if ref[1] == "d" else eng
        for b in reads:
            b.r[key] = ref
        for b in writes:
            b.w = ref
            b.r = {}
        return ref

    def op(self, eng, fn, reads=(), writes=()):
        rec = _Rec()
        fn(rec)
        name, args, kw = rec.call
        ent = Ent(lambda e: getattr(e, name)(*args, **kw))
        return self._record(eng, ent, reads, writes,
                            lambda idx: (self.epoch, "c", eng, idx))

    def dma(self, eng, out, in_, reads, writes, dsem):
        ent = Ent(lambda e: e.dma_start(out=out, in_=in_))
        ent.dsem = dsem
        ref = self._record(eng, ent, reads, writes,
                           lambda idx: (self.epoch, "d", dsem))
        dsem.count += 16
        return ref

    def barrier(self):
        last = {}
        for e in COMPUTE:
            st = self.streams[e]
            for i in range(len(st) - 1, -1, -1):
                if st[i].dsem is None:
                    last[e] = i
                    break
        for eng in ENGS:
            ent = Ent(lambda e: e.nop())
            for p, i in last.items():
                if p != eng and i > self.waited[eng][p]:
                    self.streams[p][i].inc = True
                    ent.waits.append(("c", p, i))
            for ds in self.dsems:
                if ds.count > 0 and self.waitedD[eng].get(ds, 0) < ds.count:
                    ent.waits.append(("d", ds, ds.count))
            self.streams[eng].append(ent)
        self.epoch += 1

    def replay(self, block):
        nc = self.nc
        for e in COMPUTE:
            c = self.base[e]
            for ent in self.streams[e]:
                if ent.inc:
                    c += 1
                    ent.val = c
            self.base[e] = c
        streams = self.streams
        csem = self.csem

        def run(engname):
            def f(engobj):
                for ent in streams[engname]:
                    for w in ent.waits:
                        if w[0] == "c":
                            engobj.wait_ge(csem[w[1]], streams[w[1]][w[2]].val)
                        else:
                            engobj.wait_ge(w[1].sem, w[2])
                    ins = ent.fn(engobj)
                    if ent.dsem is not None:
                        ins.then_inc(ent.dsem.sem, 16)
                    elif ent.inc:
                        ins.then_inc(csem[engname], 1)
            return f

        block.tensor(run("pe"))
        block.vector(run("dve"))
        block.scalar(run("act"))
        block.gpsimd(run("pool"))
        block.sync(run("sp"))
        self._reset()


def build_program(upto=3):
    nc = bass.Bass("TRN2", target_bir_lowering=False)

    def din(name, shape, dt=F32):
        return nc.dram_tensor(name, list(shape), dt, kind="ExternalInput").ap()

    def dout(name, shape, dt=F32):
        return nc.dram_tensor(name, list(shape), dt, kind="ExternalOutput").ap()

    def dscr(name, shape, dt):
        return nc.dram_tensor(name, list(shape), dt).ap()

    xm = din("xm", [NMAIN, D])
    xpre = din("xpre", [NPRE, D])
    xs = din("xs", [NSMP, D])
    ck = din("ck", [4, PAST, 1024])
    cv = din("cv", [4, PAST, 1024])
    constT_d = din("constT", [128, 16 * 24])
    gvec_d = din("gvec", [128, 64])
    gfin_d = din("gfin", [128, D])
    lamv_d = din("lamv", [128, 512])
    subg_d = din("subg", [128, 256])
    valid_d = din("valid", [128, 1])
    ident_d = din("ident", [128, 128])
    cs_pre = din("cs_pre", [NPRE, 128])
    cs_main = din("cs_main", [NMAIN, 128])
    cs_smp = din("cs_smp", [NSMP, 128])
    w_in = din("w_in", [D, 8192])
    w_out = din("w_out", [D, D] if upto >= 3 else [128, 128])
    w_up = din("w_up", [D, 16384] if upto >= 3 else [128, 128])
    w_down = din("w_down", [16384, D] if upto >= 3 else [128, 128])
    ga_w = din("ga_w", [16, 128, 128])
    gx_w = din("gx_w", [16, 128, 128])

    y_main = dout("y_main", [NMAIN, D])
    y_smp = dout("y_smp", [NSMP, D])
    k_main = dout("k_main", [NMAIN, 1024])
    v_main = dout("v_main", [NMAIN, 1024])
    k_smp = dout("k_smp", [NSMP, 1024])
    v_smp = dout("v_smp", [NSMP, 1024])
    st_main = dout("st_main", [128, 64])
    st_smp = dout("st_smp", [128, 256])

    qT_s = dscr("qT_s", [16, 128, NQ], BF16)
    kT_s = dscr("kT_s", [8, 128, NK], BF16)
    v_s = dscr("v_s", [NK, 1024], BF16)
    mixT_s = dscr("mixT_s", [32, 128, NQ], BF16)

    top = ExitStack()
    sems = {e: top.enter_context(nc.semaphore("c_" + e)) for e in COMPUTE}
    T = Trk(nc, sems)
    nds = [0]

    def DS():
        nds[0] += 1
        return T.new_dsem(top.enter_context(nc.semaphore("d%d" % nds[0])))

    uniq = [0]

    def sb(stack, name, shape, dt):
        uniq[0] += 1
        return stack.enter_context(nc.sbuf_tensor("sb%d_%s" % (uniq[0], name), list(shape), dt))

    def ps(stack, name, shape, dt):
        uniq[0] += 1
        return stack.enter_context(nc.psum_tensor("ps%d_%s" % (uniq[0], name), list(shape), dt))

    slots = [sb(top, "wslot%d" % i, [128, 8192], BF16) for i in range(NSLOT)]
    slot_b = [Buf("slot%d" % i) for i in range(NSLOT)]
    slot_ds = [DS() for _ in range(NSLOT)]
    ident = sb(top, "ident", [128, 128], BF16)
    gvec = sb(top, "gvec", [128, 64], F32)
    constT = sb(top, "constT", [128, 16, 24], F32)
    lamv = sb(top, "lamv", [128, 4, 128], F32)
    subg = sb(top, "subg", [128, 256], F32)
    sc = sb(top, "scal", [128, 16], F32)
    c12 = sb(top, "c12", [128, 2, 16], F32)
    B_const = Buf("consts")
    ds_const = DS()
    ds_const_p = DS()
    VALID = sc[:, 0:1]
    PREB = sc[:, 1:2]
    NLAM = sc[:, 3:4]

    plan = []
    IN_MAIN = ([("q", i) for i in range(4)] + [("k", i) for i in range(2)] + [("v", i) for i in range(2)]
               + [x for i in range(4) for x in (("xr", i), ("gr", i))])
    IN_PRE = [("k", 0), ("k", 1), ("v", 0), ("v", 1)] + [("xr", i) for i in range(4)]
    COL0 = {"q": 0, "k": 2048, "v": 3072, "xr": 4096, "gr": 6144}
    s1_tiles = ([("pre", i) for i in range(4)] + [("main", i) for i in range(4)] + [("smp", 0)])
    for kind, _ in s1_tiles:
        for (typ, bi) in (IN_PRE if kind == "pre" else IN_MAIN):
            for kh in range(2):
                plan.append(("in", COL0[typ] + bi * 512, kh))
    s3_tiles = [("main", i) for i in range(4)] + [("smp", 0)]
    for _ in s3_tiles:
        for cb in range(8):
            for kh in range(2):
                plan.append(("out", cb * 512, kh))
        for sbk in range(64):
            plan.append(("up", sbk))
            if sbk > 0:
                plan.append(("down", sbk - 1))
        plan.append(("down", 63))
    wptr = {"issued": 0, "used": 0}
    n_s1_blocks = len([p for p in plan if p[0] == "in"])
    plan_limit = len(plan) if upto >= 3 else n_s1_blocks

    def w_issue(i):
        d = plan[i]
        s = i % NSLOT
        if d[0] == "in" or d[0] == "out":
            w = w_in if d[0] == "in" else w_out
            src = w[d[2] * 2048:(d[2] + 1) * 2048, d[1]:d[1] + 512].rearrange("(kc p) c -> p kc c", p=128)
            dst = slots[s][:].rearrange("p (kc c) -> p kc c", kc=16)
        elif d[0] == "up":
            src = w_up[:, d[1] * 256:(d[1] + 1) * 256].rearrange("(kc p) c -> p kc c", p=128)
            dst = slots[s][:].rearrange("p (kc c) -> p kc c", kc=32)
        else:
            src = w_down[d[1] * 256:(d[1] + 1) * 256, :].rearrange("(f p) c -> p f c", p=128)
            dst = slots[s][:].rearrange("p (f c) -> p f c", f=2)
        T.dma("pool", dst, src, (), (slot_b[s],), slot_ds[s])

    def w_next(desc):
        i = wptr["used"]
        if os.environ.get("DBG_TILES") is None:
            assert plan[i] == desc, (plan[i], desc)
        else:
            plan[i] = desc
        while wptr["issued"] < min(plan_limit, i + NSLOT):
            w_issue(wptr["issued"])
            wptr["issued"] += 1
        wptr["used"] += 1
        s = i % NSLOT
        return slots[s], slot_b[s]

    st = ExitStack()
    xbuf = [sb(st, "xbuf%d" % i, [128, D], F32) for i in range(2)]
    xbuf_b = [Buf("xbuf%d" % i) for i in range(2)]
    xbuf_ds = [DS() for _ in range(2)]
    xsb = sb(st, "xsb", [128, D], BF16)
    xsb_b = Buf("xsb")
    xnT = sb(st, "xnT", [128, 32, 512], BF16)
    xnT_b = [Buf("xnT%d" % g) for g in range(4)]
    ssq = sb(st, "ssq", [128, 4], F32)
    ssq_b = Buf("ssq")
    cst = sb(st, "cst", [128, 4, 128], F32)
    cst_b = Buf("cst")
    cst_ds = DS()
    qk32 = [sb(st, "qk32_%d" % i, [128, 512], F32) for i in range(2)]
    qk32_b = [Buf() for _ in range(2)]
    qk32_ds = [DS() for _ in range(2)]
    rtmp = sb(st, "rtmp", [128, 4, 4, 16], F32)
    rtmp_b = Buf()
    qkbf = [sb(st, "qkbf%d" % i, [128, 512], BF16) for i in range(4)]
    qkbf_b = [Buf() for _ in range(4)]
    qkT_st = [sb(st, "qkTst%d" % i, [128, 24, 128], BF16) for i in range(4)]
    qkT_b = [Buf() for _ in range(4)]
    qkT_ds = [DS() for _ in range(4)]
    v32, v32_b, v32_ds = qk32, qk32_b, qk32_ds
    vbf_ds = [DS() for _ in range(4)]
    gaw = sb(st, "gaw", [128, 16, 128], BF16)
    gxw = sb(st, "gxw", [128, 16, 128], BF16)
    HL = sb(st, "HL", [128, 16, 4, 3], F32)
    HST = sb(st, "HST", [128, 16, 4], F32)
    HL_b = [Buf() for _ in range(16)]
    HST_b = [Buf() for _ in range(16)]
    stout = sb(st, "stout", [128, 16, 4], F32)
    stout_s = sb(st, "stout_s", [128, 16, 4, 4], F32)
    stout_b = Buf()
    stout_ds = DS()

    def L(name, shape, dt=F32):
        return sb(st, name, shape, dt), Buf(name)

    xh, xh_b = L("xh", [128, 4 * 131 + 4])
    xc4 = sb(st, "xc4", [128, 4, 512], F32)
    xc4_b = [Buf() for _ in range(4)]
    xcb4 = sb(st, "xcb4", [128, 4, 512], BF16)
    xcb4_b = [Buf() for _ in range(4)]
    rr, rr_b = L("rr", [128, 512])
    ii, ii_b = L("ii", [128, 512])
    aa, aa_b = L("aa", [128, 512])
    mm_, mm_b = L("mm", [128, 512])
    uu, uu_b = L("uu", [128, 512])
    hs4 = sb(st, "hs4", [128, 4, 512], F32)
    hs4_b = [Buf() for _ in range(4)]
    gr32, gr32_b = rr, rr_b
    g1, g1_b = ii, ii_b
    g2, g2_b = aa, aa_b
    ylb = [sb(st, "ylb%d" % i, [128, 512], BF16) for i in range(2)]
    ylb_b = [Buf() for _ in range(2)]
    ylb_ds = [DS() for _ in range(2)]

    ptx = [ps(st, "ptx%d" % i, [128, 1024], BF16) for i in range(2)]
    ptx_b = [Buf() for _ in range(2)]
    pp = [ps(st, "pp%d" % i, [128, 512], F32) for i in range(4)]
    pp_b = [Buf() for _ in range(4)]
    pg = [ps(st, "pg%d" % i, [128, 512], F32) for i in range(2)]
    pg_b = [Buf() for _ in range(2)]

    T.dma("sp", gvec[:], gvec_d[:, :], (), (B_const,), ds_const)
    T.dma("sp", constT[:].rearrange("p c k -> p (c k)"), constT_d[:, :], (), (B_const,), ds_const)
    T.dma("sp", lamv[:].rearrange("p a d -> p (a d)"), lamv_d[:, :], (), (B_const,), ds_const)
    T.dma("sp", subg[:], subg_d[:, :], (), (B_const,), ds_const)
    T.dma("sp", sc[:, 0:1], valid_d[:, :], (), (B_const,), ds_const)
    T.dma("pool", ident[:], ident_d[:, :], (), (B_const,), ds_const_p)
    T.dma("pool", gaw[:], ga_w.rearrange("n c d -> c n d"), (), (B_const,), ds_const_p)
    T.dma("pool", gxw[:], gx_w.rearrange("n c d -> c n d"), (), (B_const,), ds_const_p)
    CB = (B_const,)
    T.op("dve", lambda e: e.tensor_scalar(out=sc[:, 1:2], in0=sc[:, 0:1], scalar1=-1.0, scalar2=30000.0,
                                          op0=ALU.add, op1=ALU.mult), CB, CB)
    T.op("dve", lambda e: e.tensor_tensor(out=lamv[:, 0, :], in0=lamv[:, 0, :], in1=lamv[:, 1, :], op=ALU.mult), CB, CB)
    T.op("dve", lambda e: e.tensor_tensor(out=lamv[:, 2, :], in0=lamv[:, 2, :], in1=lamv[:, 3, :], op=ALU.mult), CB, CB)
    T.op("dve", lambda e: e.tensor_reduce(out=sc[:, 4:5], in_=lamv[:, 0, :], axis=mybir.AxisListType.X, op=ALU.add), CB, CB)
    T.op("dve", lambda e: e.tensor_reduce(out=sc[:, 5:6], in_=lamv[:, 2, :], axis=mybir.AxisListType.X, op=ALU.add), CB, CB)
    T.op("act", lambda e: e.activation(out=sc[:, 4:6], in_=sc[:, 4:6], func=AF.Exp), CB, CB)
    T.op("dve", lambda e: e.tensor_tensor(out=sc[:, 2:3], in0=sc[:, 4:5], in1=sc[:, 5:6], op=ALU.subtract), CB, CB)
    T.op("dve", lambda e: e.tensor_scalar(out=sc[:, 2:3], in0=sc[:, 2:3], scalar1=LAM_INIT, scalar2=None, op0=ALU.add), CB, CB)
    T.op("dve", lambda e: e.tensor_scalar(out=sc[:, 3:4], in0=sc[:, 2:3], scalar1=-1.0, scalar2=None, op0=ALU.mult), CB, CB)
    T.op("dve", lambda e: e.tensor_scalar(out=subg[:], in0=subg[:], scalar1=1.0 - LAM_INIT, scalar2=None, op0=ALU.mult), CB, CB)
    T.op("act", lambda e: e.activation(out=c12[:, 0, :], in_=constT[:, :, 7], func=AF.Exp, scale=-1.0), CB, CB)
    T.op("act", lambda e: e.activation(out=c12[:, 0, :], in_=c12[:, 0, :], func=AF.Ln, bias=1.0), CB, CB)
    T.op("dve", lambda e: e.tensor_scalar(out=c12[:, 1, :], in0=c12[:, 0, :], scalar1=-16.0, scalar2=None, op0=ALU.mult), CB, CB)
    T.op("dve", lambda e: e.tensor_scalar(out=c12[:, 0, :], in0=c12[:, 0, :], scalar1=-8.0, scalar2=None, op0=ALU.mult), CB, CB)
    for n in range(16):
        T.op("dve", lambda e, n=n: e.memset(HL[:, n, 0, :], 0.0), (), (HL_b[n],))
        T.op("dve", lambda e, n=n: e.memset(HST[:, n, 0:1], 0.0), (), (HST_b[n],))

    def tile_info(kind, ti):
        if kind == "pre":
            return dict(x=xpre, r0=ti * 512, NT=512, cs=cs_pre, qc=None, kc=ti * 512, ko=None, vo=None)
        if kind == "main":
            return dict(x=xm, r0=ti * 512, NT=512, cs=cs_main, qc=ti * 512, kc=NPRE + ti * 512, ko=k_main, vo=v_main)
        return dict(x=xs, r0=0, NT=256, cs=cs_smp, qc=NMAIN, kc=NPRE + NMAIN, ko=k_smp, vo=v_smp)

    xloads = []
    for kind, ti in s1_tiles:
        inf = tile_info(kind, ti)
        for g in range(inf["NT"] // 128):
            xloads.append((inf["x"], inf["r0"] + g * 128))
    xl = {"i": 0}

    def x_prefetch(upto):
        while xl["i"] <= min(upto, len(xloads) - 1):
            i = xl["i"]
            src, r = xloads[i]
            T.dma("sp", xbuf[i % 2][:], src[r:r + 128, :], (), (xbuf_b[i % 2],), xbuf_ds[i % 2])
            xl["i"] += 1

    def rstd_from_ss(ss_ap, n, bufs):
        T.op("dve", lambda e: e.tensor_scalar(out=ss_ap, in0=ss_ap, scalar1=1.0 / n, scalar2=EPS,
                                              op0=ALU.mult, op1=ALU.add), bufs, bufs)
        T.op("act", lambda e: e.activation(out=ss_ap, in_=ss_ap, func=AF.Sqrt), bufs, bufs)
        T.op("dve", lambda e: e.reciprocal(out=ss_ap, in_=ss_ap), bufs, bufs)

    def norm_transpose(src_tile, src_b, dstT, dst_b, g, gcol0, ptxs, ptxs_b, junk, junk_b, ssq_ap, ssq_buf):
        T.op("act", lambda e: e.activation(out=junk[:], in_=src_tile, func=AF.Square, accum_out=ssq_ap),
             (src_b,), (junk_b, ssq_buf))
        rstd_from_ss(ssq_ap, float(D), (ssq_buf,))
        T.op("dve", lambda e: e.tensor_scalar_mul(out=junk[:], in0=src_tile, scalar1=ssq_ap),
             (src_b, ssq_buf), (junk_b,))
        for q4 in range(4):
            pt = ptxs[q4 % 2]
            ptb = ptxs_b[q4 % 2]
            for j in range(8):
                kc = q4 * 8 + j
                T.op("pe", lambda e, kc=kc, j=j, pt=pt: e.transpose(out=pt[:, j * 128:(j + 1) * 128],
                                                                    in_=junk[:, kc * 128:(kc + 1) * 128], identity=ident[:]),
                     (junk_b, B_const), (ptb,))
            gsl = gvec[:, gcol0 + q4 * 8:gcol0 + q4 * 8 + 8].unsqueeze(2).to_broadcast([128, 8, 128])
            T.op("dve", lambda e, pt=pt, q4=q4, gsl=gsl: e.tensor_tensor(
                out=dstT[:, q4 * 8:(q4 + 1) * 8, g * 128:(g + 1) * 128],
                in0=pt[:].rearrange("p (k t) -> p k t", k=8), in1=gsl, op=ALU.mult),
                (ptb, B_const), (dst_b,))

    deferred = []

    def flush_deferred():
        while deferred:
            deferred.pop(0)()

    first_main = [True]
    gcount = 0
    qkT_i = 0
    alt = {"qk": 0, "v": 0, "y": 0}
    dbg_tiles = int(os.environ.get("DBG_TILES", "99"))
    XR = int(os.environ.get("DBG_XR", "99"))
    dbg_blocks = int(os.environ.get("DBG_BLOCKS", "99"))
    for kind, ti in s1_tiles[:dbg_tiles]:
        inf = tile_info(kind, ti)
        NT = inf["NT"]
        NG = NT // 128
        nseg = 4 if kind == "smp" else 1
        Lseg = NT // nseg
        blocks = (IN_PRE if kind == "pre" else IN_MAIN)[:dbg_blocks]
        if kind == "main" and first_main[0]:
            first_main[0] = False
            for n in range(16):
                T.op("dve", lambda e, n=n: e.tensor_scalar_mul(out=HL[:, n, 0, :], in0=HL[:, n, 0, :], scalar1=VALID),
                     (HL_b[n], B_const), (HL_b[n],))
                T.op("dve", lambda e, n=n: e.tensor_scalar_mul(out=HST[:, n, 0:1], in0=HST[:, n, 0:1], scalar1=VALID),
                     (HST_b[n], B_const), (HST_b[n],))
        if kind == "smp":
            for n in range(16):
                T.op("dve", lambda e, n=n: e.tensor_copy(out=stout[:, n, 0:3], in_=HL[:, n, 0, :]), (HL_b[n],), (stout_b,))
                T.op("dve", lambda e, n=n: e.tensor_copy(out=stout[:, n, 3:4], in_=HST[:, n, 0:1]), (HST_b[n],), (stout_b,))
            T.dma("sp", st_main[:, :], stout[:].rearrange("p c k -> p (c k)"), (stout_b,), (), stout_ds)
            for n in range(16):
                T.op("dve", lambda e, n=n: e.tensor_copy(
                    out=HL[:, n, :, :], in_=constT[:, n, 8:20].rearrange("p (s j) -> p s j", s=4)), CB, (HL_b[n],))
                T.op("dve", lambda e, n=n: e.tensor_copy(out=HST[:, n, :], in_=constT[:, n, 20:24]), CB, (HST_b[n],))
        T.dma("sp", cst[:, 0:NG, :], inf["cs"][inf["r0"]:inf["r0"] + NT, :].rearrange("(g p) c -> p g c", p=128),
              (), (cst_b,), cst_ds)
        for g in range(NG):
            x_prefetch(gcount + 1)
            xb = xbuf[gcount % 2]
            norm_transpose(xb[:], xbuf_b[gcount % 2], xnT, xnT_b[g], g, 0, ptx, ptx_b, xsb, xsb_b,
                           ssq[:, g:g + 1], ssq_b)
            gcount += 1
        for (typ, bi) in blocks:
            col0 = COL0[typ] + bi * 512
            for kh in range(2):
                slot, slb = w_next(("in", col0, kh))
                sl3 = slot[:].rearrange("p (kc c) -> p kc c", kc=16)
                if typ in ("q", "k", "v"):
                    for g in range(NG):
                        for kc in range(16):
                            T.op("pe", lambda e, g=g, kc=kc, kh=kh, sl3=sl3: e.matmul(
                                pp[g][:, :], lhsT=xnT[:, kh * 16 + kc, g * 128:(g + 1) * 128], rhs=sl3[:, kc, :],
                                start=(kh == 0 and kc == 0), stop=(kh == 1 and kc == 15)),
                                (xnT_b[g], slb), (pp_b[g],))
                else:
                    for cc in range(4):
                        for kc in range(16):
                            T.op("pe", lambda e, cc=cc, kc=kc, kh=kh, sl3=sl3: e.matmul(
                                pp[cc][:, 0:NT], lhsT=sl3[:, kc, cc * 128:(cc + 1) * 128], rhs=xnT[:, kh * 16 + kc, 0:NT],
                                start=(kh == 0 and kc == 0), stop=(kh == 1 and kc == 15)),
                                tuple(xnT_b[:NG]) + (slb,), (pp_b[cc],))
            flush_deferred()
            if typ in ("q", "k"):
                sub0 = (bi * 4) if typ == "q" else (16 + bi * 4)
                for g in range(NG):
                    a_ = alt["qk"] % 2
                    alt["qk"] += 1
                    q32, q32b = qk32[a_], qk32_b[a_]
                    T.op("act", lambda e, g=g, q32=q32: e.activation(out=q32[:], in_=pp[g][:, :], func=AF.Copy),
                         (pp_b[g],), (q32b,))
                    q3 = q32[:].rearrange("p (s d) -> p s d", s=4)
                    x1 = q3[:, :, 0:16]
                    x2 = q3[:, :, 16:32]
                    cos = cst[:, g, 0:64].rearrange("p (s d) -> p s d", s=4)
                    sin = cst[:, g, 64:128].rearrange("p (s d) -> p s d", s=4)
                    rb = (q32b, cst_b)
                    T.op("dve", lambda e, x1=x1, cos=cos: e.tensor_tensor(out=rtmp[:, 0], in0=x1, in1=cos, op=ALU.mult), rb, (rtmp_b,))
                    T.op("dve", lambda e, x2=x2, sin=sin: e.tensor_tensor(out=rtmp[:, 1], in0=x2, in1=sin, op=ALU.mult), rb, (rtmp_b,))
                    T.op("dve", lambda e, x2=x2, cos=cos: e.tensor_tensor(out=rtmp[:, 2], in0=x2, in1=cos, op=ALU.mult), rb, (rtmp_b,))
                    T.op("dve", lambda e, x1=x1, sin=sin: e.tensor_tensor(out=rtmp[:, 3], in0=x1, in1=sin, op=ALU.mult), rb, (rtmp_b,))
                    T.op("dve", lambda e, x1=x1: e.tensor_tensor(out=x1, in0=rtmp[:, 0], in1=rtmp[:, 1], op=ALU.subtract),
                         (rtmp_b,), (q32b,))
                    T.op("dve", lambda e, x2=x2: e.tensor_tensor(out=x2, in0=rtmp[:, 2], in1=rtmp[:, 3], op=ALU.add),
                         (rtmp_b,), (q32b,))
                    if typ == "k" and inf["ko"] is not None:
                        r0 = inf["r0"] + g * 128
                        T.dma("sp", inf["ko"][r0:r0 + 128, bi * 512:(bi + 1) * 512], q32[:], (q32b,), (), qk32_ds[a_])
                    qb, qbb = qkbf[g], qkbf_b[g]
                    T.op("act", lambda e, q32=q32, qb=qb: e.activation(out=qb[:], in_=q32[:], func=AF.Copy), (q32b,), (qbb,))

                    def tr(g=g, qb=qb, qbb=qbb, sub0=sub0, typ=typ, bi=bi, last=(typ == "k" and bi == 1)):
                        pt, ptb = ptx[g % 2], ptx_b[g % 2]
                        for s in range(4):
                            T.op("pe", lambda e, s=s: e.transpose(out=pt[:, s * 128:(s + 1) * 128],
                                                                  in_=qb[:, s * 128:(s + 1) * 128], identity=ident[:]),
                                 (qbb, B_const), (ptb,))
                        st_t, st_b = qkT_st[g], qkT_b[g]
                        T.op("act", lambda e: e.activation(out=st_t[:, sub0:sub0 + 4, :],
                                                           in_=pt[:, 0:512].rearrange("p (s t) -> p s t", s=4), func=AF.Copy),
                             (ptb,), (st_b,))
                        if last:
                            if inf["qc"] is not None:
                                c0 = inf["qc"] + g * 128
                                T.dma("sp", qT_s[:, :, c0:c0 + 128].rearrange("s d t -> d s t"), st_t[:, 0:16, :],
                                      (st_b,), (), qkT_ds[g])
                            c0 = inf["kc"] + g * 128
                            T.dma("sp", kT_s[:, :, c0:c0 + 128].rearrange("s d t -> d s t"), st_t[:, 16:24, :],
                                  (st_b,), (), qkT_ds[g])
                    deferred.append(tr)
            elif typ == "v":
                for g in range(NG):
                    a_ = alt["v"] % 2
                    alt["v"] += 1
                    T.op("act", lambda e, g=g, a_=a_: e.activation(out=v32[a_][:], in_=pp[g][:, :], func=AF.Copy),
                         (pp_b[g],), (v32_b[a_],))
                    T.op("dve", lambda e, g=g, a_=a_: e.tensor_copy(out=qkbf[g][:], in_=v32[a_][:]),
                         (v32_b[a_],), (qkbf_b[g],))
                    if inf["vo"] is not None:
                        r0 = inf["r0"] + g * 128
                        T.dma("sp", inf["vo"][r0:r0 + 128, bi * 512:(bi + 1) * 512], v32[a_][:], (v32_b[a_],), (), v32_ds[a_])
                    r0 = inf["kc"] + g * 128
                    T.dma("sp", v_s[r0:r0 + 128, bi * 512:(bi + 1) * 512], qkbf[g][:], (qkbf_b[g],), (), vbf_ds[g])
            elif typ == "xr":
                W = 3 + Lseg
                xh3 = xh[:, 0:nseg * W].rearrange("p (s w) -> p s w", s=nseg)
                for cc in range(4):
                    n = bi * 4 + cc

                    def v3(t):
                        return t[:, 0:NT].rearrange("p (s l) -> p s l", s=nseg)

                    T.op("act", lambda e, cc=cc: e.activation(out=xh3[:, :, 3:W], in_=v3(pp[cc]), func=AF.Copy),
                         (pp_b[cc],), (xh_b,))
                    if XR < 1:
                        continue
                    T.op("dve", lambda e, n=n: e.tensor_copy(out=xh3[:, :, 0:3], in_=HL[:, n, 0:nseg, :]), (HL_b[n],), (xh_b,))
                    T.op("dve", lambda e, n=n: e.tensor_copy(out=HL[:, n, 0:nseg, :], in_=xh3[:, :, W - 3:W]), (xh_b,), (HL_b[n],))
                    if kind == "smp":
                        T.op("dve", lambda e, n=n: e.tensor_copy(out=stout_s[:, n, :, 0:3], in_=xh3[:, :, W - 3:W]),
                             (xh_b,), (stout_b,))
                    if XR < 2:
                        continue
                    cw = constT[:, n, :]
                    xc, xc_b, xcb, xcb_b = xc4[:, cc, :], xc4_b[cc], xcb4[:, cc, :], xcb4_b[cc]
                    T.op("dve", lambda e, cw=cw, xc=xc: e.tensor_scalar(out=v3(xc), in0=xh3[:, :, 3:W], scalar1=cw[:, 3:4],
                                                                 scalar2=cw[:, 4:5], op0=ALU.mult, op1=ALU.add),
                         (xh_b, B_const), (xc_b,))
                    for j in range(3):
                        T.op("dve", lambda e, cw=cw, j=j, xc=xc: e.scalar_tensor_tensor(
                            out=v3(xc), in0=xh3[:, :, j:j + Lseg], scalar=cw[:, j:j + 1], in1=v3(xc),
                            op0=ALU.mult, op1=ALU.add), (xh_b, xc_b, B_const), (xc_b,))
                    T.op("act", lambda e, xc=xc, xcb=xcb: e.activation(out=xcb[:, 0:NT], in_=xc[:, 0:NT], func=AF.Copy), (xc_b,), (xcb_b,))

                    if XR < 3:
                        continue
                    def gates(n=n, cc=cc, cw=cw, xc=xc, xc_b=xc_b, xcb=xcb, xcb_b=xcb_b):
                        T.op("pe", lambda e: e.matmul(pg[0][:, 0:NT], lhsT=gaw[:, n, :], rhs=xcb[:, 0:NT], start=True, stop=True),
                             (xcb_b, B_const), (pg_b[0],))
                        T.op("pe", lambda e: e.matmul(pg[1][:, 0:NT], lhsT=gxw[:, n, :], rhs=xcb[:, 0:NT], start=True, stop=True),
                             (xcb_b, B_const), (pg_b[1],))
                        T.op("act", lambda e: e.activation(out=rr[:, 0:NT], in_=pg[0][:, 0:NT], func=AF.Sigmoid, bias=cw[:, 5:6]),
                             (pg_b[0], B_const), (rr_b,))
                        T.op("act", lambda e: e.activation(out=ii[:, 0:NT], in_=pg[1][:, 0:NT], func=AF.Sigmoid, bias=cw[:, 6:7]),
                             (pg_b[1], B_const), (ii_b,))
                        if XR < 4:
                            return
                        T.op("act", lambda e: e.activation(out=aa[:, 0:NT], in_=rr[:, 0:NT], func=AF.Exp, scale=c12[:, 0, n:n + 1]),
                             (rr_b, B_const), (aa_b,))
                        T.op("act", lambda e: e.activation(out=mm_[:, 0:NT], in_=rr[:, 0:NT], func=AF.Exp, scale=c12[:, 1, n:n + 1]),
                             (rr_b, B_const), (mm_b,))
                        T.op("act", lambda e: e.activation(out=mm_[:, 0:NT], in_=mm_[:, 0:NT], func=AF.Sqrt, bias=1.0, scale=-1.0),
                             (mm_b,), (mm_b,))
                        T.op("dve", lambda e: e.tensor_tensor(out=uu[:, 0:NT], in0=ii[:, 0:NT], in1=xc[:, 0:NT], op=ALU.mult),
                             (ii_b, xc_b), (uu_b,))
                        T.op("dve", lambda e: e.tensor_tensor(out=uu[:, 0:NT], in0=uu[:, 0:NT], in1=mm_[:, 0:NT], op=ALU.mult),
                             (uu_b, mm_b), (uu_b,))
                        if XR < 5:
                            return
                        for s in range(nseg):
                            T.op("dve", lambda e, s=s: e.tensor_tensor_scan(
                                out=hs4[:, cc, s * Lseg:(s + 1) * Lseg], data0=aa[:, s * Lseg:(s + 1) * Lseg],
                                data1=uu[:, s * Lseg:(s + 1) * Lseg], initial=HST[:, n, s:s + 1], op0=ALU.mult, op1=ALU.add),
                                (aa_b, uu_b, HST_b[n]), (hs4_b[cc],))
                        if XR < 6:
                            return
                        T.op("dve", lambda e: e.tensor_copy(
                            out=HST[:, n, 0:nseg],
                            in_=hs4[:, cc, 0:NT].rearrange("p (s l) -> p s l", s=nseg)[:, :, Lseg - 1]),
                            (hs4_b[cc],), (HST_b[n],))
                        if kind == "smp":
                            T.op("dve", lambda e: e.tensor_copy(out=stout_s[:, n, :, 3], in_=HST[:, n, :]), (HST_b[n],), (stout_b,))
                    deferred.append(gates)
            else:
                for cc in range(4):
                    n = bi * 4 + cc
                    T.op("act", lambda e, cc=cc: e.activation(out=gr32[:, 0:NT], in_=pp[cc][:, 0:NT], func=AF.Copy),
                         (pp_b[cc],), (gr32_b,))
                    T.op("act", lambda e: e.activation(out=g1[:, 0:NT], in_=gr32[:, 0:NT], func=AF.Square), (gr32_b,), (g1_b,))
                    T.op("dve", lambda e: e.tensor_scalar(out=g1[:, 0:NT], in0=g1[:, 0:NT], scalar1=0.044715, scalar2=1.0,
                                                          op0=ALU.mult, op1=ALU.add), (g1_b,), (g1_b,))
                    T.op("dve", lambda e: e.tensor_tensor(out=g1[:, 0:NT], in0=g1[:, 0:NT], in1=gr32[:, 0:NT], op=ALU.mult),
                         (g1_b, gr32_b), (g1_b,))
                    T.op("act", lambda e: e.activation(out=g2[:, 0:NT], in_=g1[:, 0:NT], func=AF.Sigmoid, scale=GELU_C),
                         (g1_b,), (g2_b,))
                    T.op("dve", lambda e: e.tensor_tensor(out=g2[:, 0:NT], in0=g2[:, 0:NT], in1=gr32[:, 0:NT], op=ALU.mult),
                         (g2_b, gr32_b), (g2_b,))

                    def ymul(n=n, cc=cc):
                        a_ = alt["y"] % 2
                        alt["y"] += 1
                        T.op("dve", lambda e: e.tensor_tensor(out=ylb[a_][:, 0:NT], in0=g2[:, 0:NT], in1=hs4[:, cc, 0:NT], op=ALU.mult),
                             (g2_b, hs4_b[cc]), (ylb_b[a_],))
                        T.dma("sp", mixT_s[16 + n, :, inf["qc"]:inf["qc"] + NT], ylb[a_][:, 0:NT], (ylb_b[a_],), (), ylb_ds[a_])
                    flush_deferred()
                    ymul()
        flush_deferred()
    T.dma("sp", st_smp[:, :], stout_s[:].rearrange("p c s k -> p (c s k)"), (stout_b,), (), stout_ds)
    T.barrier()
    with nc.Block() as block:
        T.replay(block)
    st.close()
    if upto == 1:
        top.close()
        return nc

    st = ExitStack()
    kTt = [sb(st, "kTt%d" % i, [128, 2, 4096], BF16) for i in range(2)]
    vat = [sb(st, "vat%d" % i, [128, 32, 258], BF16) for i in range(2)]
    qTt = [sb(st, "qTt%d" % i, [128, 4, 2048], BF16) for i in range(2)]
    kvq_b = [(Buf(), Buf(), Buf()) for _ in range(2)]
    kvq_ds = [DS() for _ in range(2)]
    kvq_dsp = [DS() for _ in range(2)]
    kcb = sb(st, "kcb", [128, 16, 256], BF16)
    kcb_b = Buf()
    kcb_ds = DS()
    pT = [sb(st, "pT%d" % i, [128, 2, 256], BF16) for i in range(2)]
    pT_b = [Buf() for _ in range(2)]
    o32 = sb(st, "o32", [128, 4, 260], F32)
    o32_b = Buf()
    d32 = sb(st, "d32", [128, 2, 256], F32)
    d32_b = Buf()
    rs = sb(st, "rs", [128, 8], F32)
    rs_b = Buf()
    onb = sb(st, "onb", [128, 2, 256], BF16)
    onb_b = Buf()
    junk2 = sb(st, "junk2", [128, 256], F32)
    junk2_b = Buf()
    oT_st = [sb(st, "oTst%d" % i, [128, 2, 256], BF16) for i in range(2)]
    oT_b = [Buf() for _ in range(2)]
    oT_ds = [DS() for _ in range(2)]
    psc = [ps(st, "psc%d" % i, [128, 512], F32) for i in range(2)]
    psc_b = [Buf() for _ in range(2)]
    po = [ps(st, "po%d" % i, [128, 512], F32) for i in range(4)]
    po_b = [Buf() for _ in range(4)]
    pt2 = [ps(st, "pt2_%d" % i, [128, 1024], BF16) for i in range(2)]
    pt2_b = [Buf() for _ in range(2)]
    for i in range(2):
        T.op("dve", lambda e, i=i: e.memset(vat[i][:, :, 256:258], 1.0), (), (kvq_b[i][1],))
    acount = {"o": 0, "blk": 0}

    def attn_block(kT_of, v_of, q_of, bias_of, nblocks, band0, NS, kbuf, qbuf, out_store):
        QW = NS * 128
        items = []
        for j in range(nblocks):
            spj = (j - band0) if band0 is not None else -1
            col0 = 128 * spj if spj > 0 else 0
            items.append((j, spj, col0))

        def qk(it):
            j, spj, col0 = it
            pb = acount["blk"] % 2
            for c in range(2):
                kt = kT_of(c, j)
                nk = kt.shape[1]
                T.op("pe", lambda e, c=c, kt=kt, nk=nk, col0=col0, pb=pb: e.matmul(
                    psc[pb][0:nk, c * 256 + col0:c * 256 + QW], lhsT=kt, rhs=q_of(c, col0), start=True, stop=True),
                    tuple(kbuf) + (qbuf,), (psc_b[pb],))
            nk = kT_of(0, j).shape[1]
            bj = bias_of(j)
            T.op("act", lambda e, pb=pb, nk=nk, col0=col0, bj=bj: e.activation(
                out=pT[pb][0:nk, :, col0:QW],
                in_=psc[pb][0:nk, :].rearrange("p (c t) -> p c t", c=2)[:, :, col0:QW],
                func=AF.Exp, scale=SCALE, bias=bj), (psc_b[pb], B_const), (pT_b[pb],))
            if spj >= 0:
                T.op("dve", lambda e, pb=pb, spj=spj: e.memset(pT[pb][64:128, :, spj * 128:spj * 128 + 64], 0.0),
                     (), (pT_b[pb],))
            acount["blk"] += 1
            return pb

        def pv(it, pb):
            j, spj, col0 = it
            vv, nk = v_of(j)
            for c in range(2):
                for s in range(max(spj, 0), NS):
                    last = (band0 + s) if band0 is not None else nblocks - 1
                    T.op("pe", lambda e, c=c, s=s, vv=vv, nk=nk, pb=pb, j=j, last=last: e.matmul(
                        po[c * 2 + s][:, 0:257], lhsT=pT[pb][0:nk, c, s * 128:(s + 1) * 128], rhs=vv,
                        start=(j == 0), stop=(j == last)), (pT_b[pb],) + tuple(kbuf), (po_b[c * 2 + s],))

        prev = None
        for it in items:
            pb = qk(it)
            if prev is not None:
                pv(*prev)
            prev = (it, pb)
        pv(*prev)
        for c in range(2):
            for s in range(NS):
                i4 = c * 2 + s
                if i4 % 2 == 0:
                    T.op("act", lambda e, i4=i4: e.activation(out=o32[:, i4, 0:257], in_=po[i4][:, 0:257], func=AF.Copy),
                         (po_b[i4],), (o32_b,))
                else:
                    T.op("dve", lambda e, i4=i4: e.tensor_copy(out=o32[:, i4, 0:257], in_=po[i4][:, 0:257]),
                         (po_b[i4],), (o32_b,))
        ob = (o32_b, rs_b)
        for s in range(NS):
            T.op("dve", lambda e, s=s: e.reciprocal(out=rs[:, 0:1], in_=o32[:, s, 256:257]), (o32_b,), (rs_b,))
            T.op("dve", lambda e, s=s: e.reciprocal(out=rs[:, 1:2], in_=o32[:, 2 + s, 256:257]), (o32_b,), (rs_b,))
            T.op("dve", lambda e: e.tensor_tensor(out=rs[:, 1:2], in0=rs[:, 1:2], in1=NLAM, op=ALU.mult), (rs_b, B_const), (rs_b,))
            T.op("dve", lambda e, s=s: e.tensor_scalar_mul(out=d32[:, s, :], in0=o32[:, s, 0:256], scalar1=rs[:, 0:1]), ob, (d32_b,))
            T.op("dve", lambda e, s=s: e.scalar_tensor_tensor(out=d32[:, s, :], in0=o32[:, 2 + s, 0:256], scalar=rs[:, 1:2],
                                                              in1=d32[:, s, :], op0=ALU.mult, op1=ALU.add),
                 (o32_b, rs_b, d32_b), (d32_b,))
            T.op("act", lambda e, s=s: e.activation(out=junk2[:], in_=d32[:, s, :], func=AF.Square, accum_out=rs[:, 2:3]),
                 (d32_b,), (junk2_b, rs_b))
            rstd_from_ss(rs[:, 2:3], 256.0, (rs_b,))
            T.op("dve", lambda e, s=s: e.scalar_tensor_tensor(out=onb[:, s, :], in0=d32[:, s, :], scalar=rs[:, 2:3], in1=subg[:],
                                                              op0=ALU.mult, op1=ALU.mult), (d32_b, rs_b, B_const), (onb_b,))
        oi = acount["o"] % 2
        acount["o"] += 1
        for s in range(NS):
            for eh in range(2):
                T.op("pe", lambda e, s=s, eh=eh: e.transpose(out=pt2[oi][:, (eh * 2 + s) * 128:(eh * 2 + s + 1) * 128],
                                                             in_=onb[:, s, eh * 128:(eh + 1) * 128], identity=ident[:]),
                     (onb_b, B_const), (pt2_b[oi],))
        T.op("act", lambda e: e.activation(out=oT_st[oi][:, :, 0:QW],
                                           in_=pt2[oi][:, 0:512].rearrange("p (a t) -> p a t", a=2)[:, :, 0:QW], func=AF.Copy),
             (pt2_b[oi],), (oT_b[oi],))
        out_store(oT_st[oi], oT_b[oi], oT_ds[oi])

    for h in range(4):
        ab = h % 2
        kb, vb, qb = kvq_b[ab]
        T.dma("sp", kTt[ab][:], kT_s[h * 2:h * 2 + 2, :, 0:4096].rearrange("c d t -> d c t"), (), (kb,), kvq_ds[ab])
        T.dma("sp", vat[ab][:, :, 0:256], v_s[0:4096, h * 256:(h + 1) * 256].rearrange("(g p) e -> p g e", p=128),
              (), (vb,), kvq_ds[ab])
        T.dma("sp", qTt[ab][:], qT_s[h * 4:h * 4 + 4, :, 0:2048].rearrange("s d t -> d s t"), (), (qb,), kvq_ds[ab])
        kvb = Buf()
        for g in range(2):
            for Q in range(8):
                A = NPRE + Q * 256
                nfull = A // 128

                def kT_of(c, j, ab=ab):
                    return kTt[ab][:, c, j * 128:(j + 1) * 128]

                def v_of(j, ab=ab):
                    return vat[ab][:, j, 0:257], 128

                def q_of(c, col0, ab=ab, g=g, Q=Q):
                    return qTt[ab][:, g * 2 + c, Q * 256 + col0:(Q + 1) * 256]

                def bias_of(j):
                    return PREB if j < 16 else 0.0

                def out_store(ot, otb, ods, h=h, g=g, Q=Q):
                    ch = (h * 2 + g) * 2
                    T.dma("sp", mixT_s[ch:ch + 2, :, Q * 256:(Q + 1) * 256].rearrange("a d t -> d a t"), ot[:, :, :],
                          (otb,), (), ods)

                attn_block(kT_of, v_of, q_of, bias_of, nfull + 2, nfull, 2, (kb, vb), qb, out_store)
    T.barrier()
    with nc.Block() as block:
        T.replay(block)

    kTs = kTt
    for sq in range(4):
        for h in range(4):
            ab = (sq * 4 + h) % 2
            kb, vb, qb = kvq_b[ab]
            T.dma("pool", kcb[:], ck[sq, :, h * 256:(h + 1) * 256].rearrange("(g p) e -> p g e", p=128), (), (kcb_b,), kcb_ds)
            for gg in range(16):
                for c in range(2):
                    idx = gg * 2 + c
                    pti = (idx // 8) % 2
                    T.op("pe", lambda e, gg=gg, c=c, idx=idx, pti=pti: e.transpose(
                        out=pt2[pti][:, (idx % 8) * 128:(idx % 8 + 1) * 128], in_=kcb[:, gg, c * 128:(c + 1) * 128],
                        identity=ident[:]), (kcb_b, B_const), (pt2_b[pti],))
                if gg % 4 == 3:
                    pti = ((gg * 2) // 8) % 2
                    g0 = gg - 3
                    T.op("dve", lambda e, pti=pti, g0=g0, ab=ab: e.tensor_copy(
                        out=kTs[ab][:, :, g0 * 128:(g0 + 4) * 128].rearrange("p c (g t) -> p g c t", g=4),
                        in_=pt2[pti][:].rearrange("p (g c t) -> p g c t", g=4, c=2)), (pt2_b[pti],), (kb,))
            T.dma("sp", kTs[ab][:, :, 2048:2112],
                  kT_s[h * 2:h * 2 + 2, :, NPRE + NMAIN + sq * 64:NPRE + NMAIN + sq * 64 + 64].rearrange("c d t -> d c t"),
                  (), (kb,), kvq_ds[ab])
            T.dma("pool", vat[ab][:, 0:16, 0:256], cv[sq, :, h * 256:(h + 1) * 256].rearrange("(g p) e -> p g e", p=128),
                  (), (vb,), kvq_dsp[ab])
            r0 = NPRE + NMAIN + sq * 64
            T.dma("sp", vat[ab][0:64, 16, 0:256], v_s[r0:r0 + 64, h * 256:(h + 1) * 256], (), (vb,), kvq_ds[ab])
            c0 = NMAIN + sq * 64
            for g_ in range(2):
                for c_ in range(2):
                    T.dma("sp", qTt[ab][:, c_, g_ * 64:(g_ + 1) * 64], qT_s[h * 4 + g_ * 2 + c_, :, c0:c0 + 64],
                          (), (qb,), kvq_ds[ab])

            def kT_of(c, j, ab=ab):
                return kTs[ab][:, c, j * 128:min((j + 1) * 128, 2112)]

            def v_of(j, ab=ab):
                nk = 128 if j < 16 else 64
                return vat[ab][0:nk, j, 0:257], nk

            def q_of(c, col0, ab=ab):
                return qTt[ab][:, c, 0:128]

            def bias_of(j):
                return 0.0

            def out_store(ot, otb, ods, h=h, sq=sq):
                for g in range(2):
                    ch = (h * 2 + g) * 2
                    c0 = NMAIN + sq * 64
                    T.dma("sp", mixT_s[ch:ch + 2, :, c0:c0 + 64].rearrange("a d t -> d a t"), ot[:, :, g * 64:(g + 1) * 64],
                          (otb,), (), ods)

            attn_block(kT_of, v_of, q_of, bias_of, 17, None, 1, (kb, vb), qb, out_store)
    T.barrier()
    with nc.Block() as block:
        T.replay(block)
    st.close()
    if upto == 2:
        top.close()
        return nc

    st = ExitStack()
    hh = sb(st, "hh", [128, 4, D], F32)
    hh_b = [Buf() for _ in range(4)]
    hh_ds = [DS() for _ in range(4)]
    actT = sb(st, "actT", [128, 32, 512], BF16)
    actT_b = [Buf() for _ in range(4)]
    actT_ds = DS()
    hsb = sb(st, "hsb", [128, D], BF16)
    hsb_b = Buf()
    gfin = sb(st, "gfin", [128, D], F32)
    ssq3 = sb(st, "ssq3", [128, 8], F32)
    ssq3_b = Buf()
    z32 = [sb(st, "z32_%d" % i, [128, 512], F32) for i in range(2)]
    z32_b = [Buf() for _ in range(2)]
    zT = [sb(st, "zT%d" % i, [128, 2, 512], BF16) for i in range(2)]
    zT_b = [Buf() for _ in range(2)]
    ptx = [ps(st, "ptx3_%d" % i, [128, 1024], BF16) for i in range(2)]
    ptx_b = [Buf() for _ in range(2)]
    pp = [ps(st, "pp3_%d" % i, [128, 512], F32) for i in range(4)]
    pp_b = [Buf() for _ in range(4)]
    pz = [ps(st, "pz%d" % i, [128, 512], F32) for i in range(2)]
    pz_b = [Buf() for _ in range(2)]
    T.dma("sp", gfin[:], gfin_d[:, :], (), (B_const,), ds_const)

    pcount = 0
    for kind, ti in s3_tiles:
        if kind == "main":
            xsrc, r0, NT, c0, ydst = xm, ti * 512, 512, ti * 512, y_main
        else:
            xsrc, r0, NT, c0, ydst = xs, 0, 256, NMAIN, y_smp
        NG = NT // 128
        for g in range(NG):
            T.dma("sp", actT[:, :, g * 128:(g + 1) * 128],
                  mixT_s[:, :, c0 + g * 128:c0 + (g + 1) * 128].rearrange("k d t -> d k t"), (), (actT_b[g],), actT_ds)
            T.dma("sp", hh[:, g, :], xsrc[r0 + g * 128:r0 + (g + 1) * 128, :], (), (hh_b[g],), hh_ds[g])
        for cb in range(8):
            for kh in range(2):
                slot, slb = w_next(("out", cb * 512, kh))
                sl3 = slot[:].rearrange("p (kc c) -> p kc c", kc=16)
                for g in range(NG):
                    for kc in range(16):
                        T.op("pe", lambda e, g=g, kc=kc, kh=kh, sl3=sl3: e.matmul(
                            pp[g][:, :], lhsT=actT[:, kh * 16 + kc, g * 128:(g + 1) * 128], rhs=sl3[:, kc, :],
                            start=(kh == 0 and kc == 0), stop=(kh == 1 and kc == 15)), (actT_b[g], slb), (pp_b[g],))
            for g in range(NG):
                T.op("dve", lambda e, g=g, cb=cb: e.tensor_tensor(out=hh[:, g, cb * 512:(cb + 1) * 512],
                                                                  in0=pp[g][:, :], in1=hh[:, g, cb * 512:(cb + 1) * 512], op=ALU.add),
                     (pp_b[g], hh_b[g]), (hh_b[g],))
        for g in range(NG):
            norm_transpose(hh[:, g, :], hh_b[g], actT, actT_b[g], g, 32, ptx, ptx_b, hsb, hsb_b, ssq3[:, g:g + 1], ssq3_b)
        pend = []

        def down(sbk, zi):
            nonlocal pcount
            slot_d, sld_b = w_next(("down", sbk))
            sd3 = slot_d[:].rearrange("p (f c) -> p f c", f=2)
            for g in range(NG):
                for cb in range(8):
                    bk = pcount % 4
                    pcount += 1
                    for f in range(2):
                        T.op("pe", lambda e, g=g, cb=cb, f=f, bk=bk: e.matmul(
                            pp[bk][:, :], lhsT=zT[zi][:, f, g * 128:(g + 1) * 128], rhs=sd3[:, f, cb * 512:(cb + 1) * 512],
                            start=(f == 0), stop=(f == 1)), (zT_b[zi], sld_b), (pp_b[bk],))
                    T.op("dve", lambda e, g=g, cb=cb, bk=bk: e.tensor_tensor(
                        out=hh[:, g, cb * 512:(cb + 1) * 512], in0=pp[bk][:, :], in1=hh[:, g, cb * 512:(cb + 1) * 512], op=ALU.add),
                        (pp_b[bk], hh_b[g]), (hh_b[g],))

        for sbk in range(64):
            slot_u, slu_b = w_next(("up", sbk))
            su3 = slot_u[:].rearrange("p (kc c) -> p kc c", kc=32)
            zi = sbk % 2
            for f in range(2):
                for kc in range(32):
                    T.op("pe", lambda e, f=f, kc=kc, su3=su3: e.matmul(
                        pz[f][:, 0:NT], lhsT=su3[:, kc, f * 128:(f + 1) * 128], rhs=actT[:, kc, 0:NT],
                        start=(kc == 0), stop=(kc == 31)), tuple(actT_b[:NG]) + (slu_b,), (pz_b[f],))
            for f in range(2):
                T.op("act", lambda e, f=f: e.activation(out=z32[f][:, 0:NT], in_=pz[f][:, 0:NT], func=AF.Relu),
                     (pz_b[f],), (z32_b[f],))
                T.op("act", lambda e, f=f, zi=zi: e.activation(out=zT[zi][:, f, 0:NT], in_=z32[f][:, 0:NT], func=AF.Square),
                     (z32_b[f],), (zT_b[zi],))
            if pend:
                down(*pend.pop(0))
            pend.append((sbk, zi))
        down(*pend.pop(0))
        for g in range(NG):
            T.op("act", lambda e, g=g: e.activation(out=hsb[:], in_=hh[:, g, :], func=AF.Square, accum_out=ssq3[:, 4 + g:5 + g]),
                 (hh_b[g],), (hsb_b, ssq3_b))
            rstd_from_ss(ssq3[:, 4 + g:5 + g], float(D), (ssq3_b,))
            for hf in range(2):
                T.op("dve", lambda e, g=g, hf=hf: e.scalar_tensor_tensor(
                    out=hh[:, g, hf * 2048:(hf + 1) * 2048], in0=hh[:, g, hf * 2048:(hf + 1) * 2048], scalar=ssq3[:, 4 + g:5 + g],
                    in1=gfin[:, hf * 2048:(hf + 1) * 2048], op0=ALU.mult, op1=ALU.mult), (hh_b[g], ssq3_b, B_const), (hh_b[g],))
            T.dma("sp", ydst[r0 + g * 128:r0 + (g + 1) * 128, :], hh[:, g, :], (hh_b[g],), (), hh_ds[g])
    T.barrier()
    with nc.Block() as block:
        T.replay(block)
    st.close()
    top.close()
    return nc


_NC_CACHE = {}


def _rope_table(pos):
    half = 16
    inv = (np.float32(500000.0) ** (-np.arange(half, dtype=np.float32) / np.float32(half))).astype(np.float32)
    ang = pos.astype(np.float32)[:, None] * inv[None, :]
    cos = np.cos(ang).astype(np.float32)
    sin = np.sin(ang).astype(np.float32)
    return np.ascontiguousarray(np.concatenate([np.tile(cos, (1, 4)), np.tile(sin, (1, 4))], axis=1))


def kernel(x_prompt, x_sample, cache_k, cache_v, state_conv, state_lru,
           norm_mix, w_in, conv_w, conv_b, gate_a_w, gate_a_b, gate_x_w, gate_x_b,
           lru_lambda, lambda_q1, lambda_k1, lambda_q2, lambda_k2, subln_g,
           w_out, norm_mlp, w_up, w_down, norm_final):
    f32 = np.float32
    A = lambda a: np.ascontiguousarray(np.asarray(a, dtype=f32))
    x_prompt = A(x_prompt); x_sample = A(x_sample)
    cache_k = A(cache_k); cache_v = A(cache_v)
    state_conv = A(state_conv); state_lru = A(state_lru)
    upto = _NC_CACHE.get("upto", 3)
    if "nc" not in _NC_CACHE:
        _NC_CACHE["nc"] = build_program(upto)
    nc = _NC_CACHE["nc"]

    gvec = np.concatenate([A(norm_mix)[0].reshape(32, 128).T, A(norm_mlp)[0].reshape(32, 128).T], axis=1)
    gfin = np.tile(A(norm_final).reshape(1, D), (128, 1))
    lamv = np.tile(np.concatenate([A(lambda_q1)[0], A(lambda_k1)[0], A(lambda_q2)[0], A(lambda_k2)[0]]).reshape(1, 512), (128, 1))
    subg = np.tile(A(subln_g)[0].reshape(1, 256), (128, 1))
    ident = np.eye(128, dtype=f32)
    shared = dict(
        gvec=A(gvec), gfin=A(gfin), lamv=A(lamv), subg=A(subg), ident=ident,
        w_in=A(w_in)[0],
        w_out=A(w_out)[0] if upto >= 3 else np.zeros((128, 128), f32),
        w_up=A(w_up)[0] if upto >= 3 else np.zeros((128, 128), f32),
        w_down=A(w_down)[0] if upto >= 3 else np.zeros((128, 128), f32),
        ga_w=A(gate_a_w)[0], gx_w=A(gate_x_w)[0],
        cs_pre=_rope_table(np.arange(NPRE)),
        cs_smp=_rope_table(np.tile(PAST + np.arange(64), 4)),
    )
    in_maps = []
    for c in range(8):
        b, half = c // 2, c % 2
        rows = np.concatenate([
            A(conv_w)[0], A(conv_b), A(gate_a_b)[0].reshape(1, 2048), A(gate_x_b)[0].reshape(1, 2048), A(lru_lambda),
            state_conv[0, 4 * c:4 * c + 4].reshape(12, 2048), state_lru[0, 4 * c:4 * c + 4]], axis=0)
        constT = rows.reshape(24, 16, 128).transpose(2, 1, 0).reshape(128, 16 * 24)
        m = dict(shared)
        m.update(
            xm=x_prompt[b, half * 2048:(half + 1) * 2048],
            xpre=x_prompt[b, 0:2048],
            xs=x_sample[4 * c:4 * c + 4].reshape(256, D),
            ck=cache_k[0, 4 * c:4 * c + 4].reshape(4, PAST, 1024),
            cv=cache_v[0, 4 * c:4 * c + 4].reshape(4, PAST, 1024),
            constT=A(constT),
            valid=np.full((128, 1), float(half), dtype=f32),
            cs_main=_rope_table(half * 2048 + np.arange(NMAIN)),
        )
        in_maps.append({k: np.ascontiguousarray(v) for k, v in m.items()})
    if os.environ.get("DBG_CORES"):
        res = run_bass_kernel_spmd(nc, [in_maps[1]], core_ids=[0])
        R = [res.results[0]] * 8
    else:
        res = run_bass_kernel_spmd(nc, in_maps, core_ids=list(range(8)))
        R = res.results

    y_prompt = np.empty((4, 4096, D), f32)
    k_prompt = np.empty((1, 4, 4096, 4, 256), f32)
    v_prompt = np.empty((1, 4, 4096, 4, 256), f32)
    conv_prompt = np.empty((1, 4, 3, 2048), f32)
    lru_prompt = np.empty((1, 4, 2048), f32)
    y_sample = np.empty((32, 64, D), f32)
    k_sample = np.empty((1, 32, 64, 4, 256), f32)
    v_sample = np.empty((1, 32, 64, 4, 256), f32)
    conv_sample = np.empty((1, 32, 3, 2048), f32)
    lru_sample = np.empty((1, 32, 2048), f32)
    for c in range(8):
        b, half = c // 2, c % 2
        r = R[c]
        sl = slice(half * 2048, (half + 1) * 2048)
        y_prompt[b, sl] = r["y_main"]
        k_prompt[0, b, sl] = r["k_main"].reshape(2048, 4, 256)
        v_prompt[0, b, sl] = r["v_main"].reshape(2048, 4, 256)
        if half == 1:
            stm = r["st_main"].reshape(128, 16, 4)
            full = stm.transpose(2, 1, 0).reshape(4, 2048)
            conv_prompt[0, b] = full[0:3]
            lru_prompt[0, b] = full[3]
        y_sample[4 * c:4 * c + 4] = r["y_smp"].reshape(4, 64, D)
        k_sample[0, 4 * c:4 * c + 4] = r["k_smp"].reshape(4, 64, 4, 256)
        v_sample[0, 4 * c:4 * c + 4] = r["v_smp"].reshape(4, 64, 4, 256)
        sts = r["st_smp"].reshape(128, 16, 4, 4)
        fulls = sts.transpose(2, 3, 1, 0).reshape(4, 4, 2048)
        conv_sample[0, 4 * c:4 * c + 4] = fulls[:, 0:3]
        lru_sample[0, 4 * c:4 * c + 4] = fulls[:, 3]
    return (y_prompt, y_sample, k_prompt, v_prompt, conv_prompt, lru_prompt,
            k_sample, v_sample, conv_sample, lru_sample)
```

```python
import math
import os
from contextlib import ExitStack

import numpy as np
import concourse.bass as bass
import concourse.mybir as mybir
from concourse.bass_utils import run_bass_kernel_spmd

F32 = mybir.dt.float32
BF16 = mybir.dt.bfloat16
AF = mybir.ActivationFunctionType
ALU = mybir.AluOpType

D = 4096
NPRE = 2048
NMAIN = 2048
NSMP = 256
NQ = NMAIN + NSMP
NK = NPRE + NMAIN + NSMP
PAST = 2048
EPS = 1e-6
LAM_INIT = 0.2
SCALE = 128 ** -0.5
NSLOT = 3
SAME_ENGINE_SYNC = True
GELU_C = 2.0 * math.sqrt(2.0 / math.pi)


class Buf:
    __slots__ = ("name", "w", "r")

    def __init__(self, name=""):
        self.name = name
        self.w = None
        self.r = {}


class DSem:
    def __init__(self, sem):
        self.sem = sem
        self.count = 0


class Ent:
    __slots__ = ("fn", "waits", "inc", "val", "dsem")

    def __init__(self, fn):
        self.fn = fn
        self.waits = []
        self.inc = False
        self.val = None
        self.dsem = None


class _Rec:
    def __getattr__(self, name):
        def f(*a, **kw):
            self.__dict__["call"] = (name, a, kw)
        return f


COMPUTE = ("pe", "dve", "act", "pool")
ENGS = ("pe", "dve", "act", "pool", "sp")


class Trk:
    def __init__(self, nc, csem):
        self.nc = nc
        self.csem = csem
        self.base = {e: 0 for e in COMPUTE}
        self.dsems = []
        self.epoch = 0
        self._reset()

    def _reset(self):
        self.streams = {e: [] for e in ENGS}
        self.waited = {e: {p: -1 for p in COMPUTE} for e in ENGS}
        self.waitedD = {e: {} for e in ENGS}

    def new_dsem(self, sem):
        d = DSem(sem)
        self.dsems.append(d)
        return d

    def _add_wait(self, eng, ent, ref):
        if ref is None or ref[0] != self.epoch:
            return
        if ref[1] == "c":
            _, _, peng, idx = ref
            if peng == eng and (eng in ("pe", "sp") or not SAME_ENGINE_SYNC):
                return
            if idx <= self.waited[eng][peng]:
                return
            self.waited[eng][peng] = idx
            self.streams[peng][idx].inc = True
            ent.waits.append(("c", peng, idx))
        else:
            ds = ref[2]
            cnt = ds.count
            if self.waitedD[eng].get(ds, 0) >= cnt:
                return
            self.waitedD[eng][ds] = cnt
            ent.waits.append(("d", ds, cnt))

    def _record(self, eng, ent, reads, writes, ref_maker):
        deps = []
        for b in reads:
            if b.w is not None:
                deps.append(b.w)
        for b in writes:
            if b.w is not None:
                deps.append(b.w)
            deps.extend(b.r.values())
        for d in deps:
            self._add_wait(eng, ent, d)
        idx = len(self.streams[eng])
        self.streams[eng].append(ent)
        ref = ref_maker(idx)
        key = ref[2] if ref[1] == "d" else eng
        for b in reads:
            b.r[key] = ref
        for b in writes:
            b.w = ref
            b.r = {}
        return ref

    def op(self, eng, fn, reads=(), writes=()):
        rec = _Rec()
        fn(rec)
        name, args, kw = rec.call
        ent = Ent(lambda e: getattr(e, name)(*args, **kw))
        return self._record(eng, ent, reads, writes,
                            lambda idx: (self.epoch, "c", eng, idx))

    def dma(self, eng, out, in_, reads, writes, dsem):
        ent = Ent(lambda e: e.dma_start(out=out, in_=in_))
        ent.dsem = dsem
        ref = self._record(eng, ent, reads, writes,
                           lambda idx: (self.epoch, "d", dsem))
        dsem.count += 16
        return ref

    def barrier(self):
        last = {}
        for e in COMPUTE:
            st = self.streams[e]
            for i in range(len(st) - 1, -1, -1):
                if st[i].dsem is None:
                    last[e] = i
                    break
        for eng in ENGS:
            ent = Ent(lambda e: e.nop())
            for p, i in last.items():
                if p != eng and i > self.waited[eng][p]:
                    self.streams[p][i].inc = True
                    ent.waits.append(("c", p, i))
            for ds in self.dsems:
                if ds.count > 0 and self.waitedD[eng].get(ds, 0) < ds.count:
                    ent.waits.append(("d", ds, ds.count))
            self.streams[eng].append(ent)
        self.epoch += 1

    def replay(self, block):
        nc = self.nc
        for e in COMPUTE:
            c = self.base[e]
            for ent in self.streams[e]:
                if ent.inc:
                    c += 1
                    ent.val = c
            self.base[e] = c
        streams = self.streams
        csem = self.csem

        def run(engname):
            def f(engobj):
                for ent in streams[engname]:
                    for w in ent.waits:
                        if w[0] == "c":
                            engobj.wait_ge(csem[w[1]], streams[w[1]][w[2]].val)
                        else:
                            engobj.wait_ge(w[1].sem, w[2])
                    ins = ent.fn(engobj)
                    if ent.dsem is not None:
                        ins.then_inc(ent.dsem.sem, 16)
                    elif ent.inc:
                        ins.then_inc(csem[engname], 1)
            return f

        block.tensor(run("pe"))
        block.vector(run("dve"))
        block.scalar(run("act"))
        block.gpsimd(run("pool"))
        block.sync(run("sp"))
        self._reset()


def build_program(upto=3):
    nc = bass.Bass("TRN2", target_bir_lowering=False)

    def din(name, shape, dt=F32):
        return nc.dram_tensor(name, list(shape), dt, kind="ExternalInput").ap()

    def dout(name, shape, dt=F32):
        return nc.dram_tensor(name, list(shape), dt, kind="ExternalOutput").ap()

    def dscr(name, shape, dt):
        return nc.dram_tensor(name, list(shape), dt).ap()

    xm = din("xm", [NMAIN, D])
    xpre = din("xpre", [NPRE, D])
    xs = din("xs", [NSMP, D])
    ck = din("ck", [4, PAST, 1024])
    cv = din("cv", [4, PAST, 1024])
    constT_d = din("constT", [128, 16 * 24])
    gvec_d = din("gvec", [128, 64])
    gfin_d = din("gfin", [128, D])
    lamv_d = din("lamv", [128, 512])
    subg_d = din("subg", [128, 256])
    valid_d = din("valid", [128, 1])
    ident_d = din("ident", [128, 128])
    cs_pre = din("cs_pre", [NPRE, 128])
    cs_main = din("cs_main", [NMAIN, 128])
    cs_smp = din("cs_smp", [NSMP, 128])
    w_in = din("w_in", [D, 8192])
    w_out = din("w_out", [D, D] if upto >= 3 else [128, 128])
    w_up = din("w_up", [D, 16384] if upto >= 3 else [128, 128])
    w_down = din("w_down", [16384, D] if upto >= 3 else [128, 128])
    ga_w = din("ga_w", [16, 128, 128])
    gx_w = din("gx_w", [16, 128, 128])

    y_main = dout("y_main", [NMAIN, D])
    y_smp = dout("y_smp", [NSMP, D])
    k_main = dout("k_main", [NMAIN, 1024])
    v_main = dout("v_main", [NMAIN, 1024])
    k_smp = dout("k_smp", [NSMP, 1024])
    v_smp = dout("v_smp", [NSMP, 1024])
    st_main = dout("st_main", [128, 64])
    st_smp = dout("st_smp", [128, 256])

    qT_s = dscr("qT_s", [16, 128, NQ], BF16)
    kT_s = dscr("kT_s", [8, 128, NK], BF16)
    v_s = dscr("v_s", [NK, 1024], BF16)
    mixT_s = dscr("mixT_s", [32, 128, NQ], BF16)

    top = ExitStack()
    sems = {e: top.enter_context(nc.semaphore("c_" + e)) for e in COMPUTE}
    T = Trk(nc, sems)
    nds = [0]

    def DS():
        nds[0] += 1
        return T.new_dsem(top.enter_context(nc.semaphore("d%d" % nds[0])))

    uniq = [0]

    def sb(stack, name, shape, dt):
        uniq[0] += 1
        return stack.enter_context(nc.sbuf_tensor("sb%d_%s" % (uniq[0], name), list(shape), dt))

    def ps(stack, name, shape, dt):
        uniq[0] += 1
        return stack.enter_context(nc.psum_tensor("ps%d_%s" % (uniq[0], name), list(shape), dt))

    MAXSLOT = 4
    slot_b = [Buf("slot%d" % i) for i in range(MAXSLOT)]
    slot_ds = [DS() for _ in range(MAXSLOT)]
    WS = {"slots": None, "n": 0, "plan": None, "issued": 0, "used": 0}

    def ws_begin(stack, n, plan_):
        WS["slots"] = [sb(stack, "wslot%d" % i, [128, 8192], BF16) for i in range(n)]
        WS["n"] = n
        WS["plan"] = plan_
        WS["issued"] = 0
        WS["used"] = 0
    ident = sb(top, "ident", [128, 128], BF16)
    gvec = sb(top, "gvec", [128, 64], F32)
    constT = sb(top, "constT", [128, 16, 24], F32)
    lamv = sb(top, "lamv", [128, 4, 128], F32)
    subg = sb(top, "subg", [128, 256], F32)
    sc = sb(top, "scal", [128, 16], F32)
    c12 = sb(top, "c12", [128, 2, 16], F32)
    B_const = Buf("consts")
    ds_const = DS()
    ds_const_p = DS()
    VALID = sc[:, 0:1]
    PREB = sc[:, 1:2]
    NLAM = sc[:, 3:4]

    IN_MAIN = ([("q", i) for i in range(4)] + [("k", i) for i in range(2)] + [("v", i) for i in range(2)]
               + [x for i in range(4) for x in (("xr", i), ("gr", i))])
    IN_PRE = [("k", 0), ("k", 1), ("v", 0), ("v", 1)] + [("xr", i) for i in range(4)]
    COL0 = {"q": 0, "k": 2048, "v": 3072, "xr": 4096, "gr": 6144}
    s1_tiles = ([("pre", i) for i in range(4)] + [("main", i) for i in range(4)] + [("smp", 0)])
    plan1 = []
    for kind, _ in s1_tiles:
        for (typ, bi) in (IN_PRE if kind == "pre" else IN_MAIN):
            for kh in range(2):
                plan1.append(("in", COL0[typ] + bi * 512, kh))
    s3_tiles = [("main", i) for i in range(4)] + [("smp", 0)]
    NSB = 32
    plan3 = []
    for _ in s3_tiles:
        for cb in range(8):
            for kh in range(2):
                plan3.append(("out", cb * 512, kh))
        for sbk in range(NSB):
            plan3.append(("upA", sbk))
            plan3.append(("upB", sbk))
            if sbk > 0:
                plan3.append(("dA", sbk - 1))
                plan3.append(("dB", sbk - 1))
        plan3.append(("dA", NSB - 1))
        plan3.append(("dB", NSB - 1))

    def w_issue(i):
        d = WS["plan"][i]
        s_ = i % WS["n"]
        slot = WS["slots"][s_]
        if d[0] == "in" or d[0] == "out":
            w = w_in if d[0] == "in" else w_out
            src = w[d[2] * 2048:(d[2] + 1) * 2048, d[1]:d[1] + 512].rearrange("(kc p) c -> p kc c", p=128)
            dst = slot[:].rearrange("p (kc c) -> p kc c", kc=16)
        elif d[0] in ("upA", "upB"):
            r0_ = 0 if d[0] == "upA" else 2048
            src = w_up[r0_:r0_ + 2048, d[1] * 512:(d[1] + 1) * 512].rearrange("(kc p) c -> p kc c", p=128)
            dst = slot[:].rearrange("p (kc c) -> p kc c", kc=16)
        else:
            r0_ = d[1] * 512 + (0 if d[0] == "dA" else 256)
            src = w_down[r0_:r0_ + 256, :].rearrange("(f p) c -> p f c", p=128)
            dst = slot[:].rearrange("p (f c) -> p f c", f=2)
        T.dma("pool", dst, src, (), (slot_b[s_],), slot_ds[s_])

    def w_next(desc):
        i = WS["used"]
        plan_ = WS["plan"]
        if os.environ.get("DBG_TILES") is None:
            assert plan_[i] == desc, (plan_[i], desc)
        else:
            plan_[i] = desc
        while WS["issued"] < min(len(plan_), i + 3):
            w_issue(WS["issued"])
            WS["issued"] += 1
        WS["used"] += 1
        s_ = i % WS["n"]
        return WS["slots"][s_], slot_b[s_]

    st = ExitStack()
    ws_begin(st, 3, plan1)
    xbuf = [sb(st, "xbuf%d" % i, [128, D], F32) for i in range(2)]
    xbuf_b = [Buf("xbuf%d" % i) for i in range(2)]
    xbuf_ds = [DS() for _ in range(2)]
    xsb = sb(st, "xsb", [128, D], BF16)
    xsb_b = Buf("xsb")
    xnT = sb(st, "xnT", [128, 32, 512], BF16)
    xnT_b = [Buf("xnT%d" % g) for g in range(4)]
    ssq = sb(st, "ssq", [128, 4], F32)
    ssq_b = Buf("ssq")
    cst = sb(st, "cst", [128, 4, 128], F32)
    cst_b = Buf("cst")
    cst_ds = DS()
    qk32 = [sb(st, "qk32_%d" % i, [128, 512], F32) for i in range(4)]
    qk32_b = [Buf() for _ in range(4)]
    qk32_ds = [DS() for _ in range(4)]
    rtmp = sb(st, "rtmp", [128, 4, 4, 16], F32)
    rtmp_b = Buf()
    qkbf = [sb(st, "qkbf%d" % i, [128, 512], BF16) for i in range(4)]
    qkbf_b = [Buf() for _ in range(4)]
    qkT_st = [sb(st, "qkTst%d" % i, [128, 4, 128], BF16) for i in range(4)]
    qkT_b = [Buf() for _ in range(4)]
    qkT_ds = [DS() for _ in range(4)]
    v32, v32_b, v32_ds = qk32, qk32_b, qk32_ds
    vbf_ds = [DS() for _ in range(4)]
    gaw = sb(st, "gaw", [128, 16, 128], BF16)
    gxw = sb(st, "gxw", [128, 16, 128], BF16)
    HL = sb(st, "HL", [128, 16, 4, 3], F32)
    HST = sb(st, "HST", [128, 16, 4], F32)
    HL_b = [Buf() for _ in range(16)]
    HST_b = [Buf() for _ in range(16)]
    stout = sb(st, "stout", [128, 16, 4], F32)
    stout_s = sb(st, "stout_s", [128, 16, 4, 4], F32)
    stout_b = Buf()
    stout_ds = DS()

    def L(name, shape, dt=F32):
        return sb(st, name, shape, dt), Buf(name)

    xh4 = sb(st, "xh4", [128, 4, 528], F32)
    xh4_b = [Buf() for _ in range(4)]
    xc4 = sb(st, "xc4", [128, 4, 512], F32)
    xc4_b = [Buf() for _ in range(4)]
    xcb4 = sb(st, "xcb4", [128, 4, 512], BF16)
    xcb4_b = [Buf() for _ in range(4)]
    rr, rr_b = L("rr", [128, 512])
    ii, ii_b = L("ii", [128, 512])
    aa, aa_b = L("aa", [128, 512])
    mm_, mm_b = L("mm", [128, 512])
    uu, uu_b = L("uu", [128, 512])
    hs4 = sb(st, "hs4", [128, 4, 512], F32)
    hs4_b = [Buf() for _ in range(4)]
    gr4 = sb(st, "gr4", [128, 4, 512], F32)
    gr4_b = [Buf() for _ in range(4)]
    g1, g1_b = ii, ii_b
    g2, g2_b = aa, aa_b
    ylb = [sb(st, "ylb%d" % i, [128, 512], BF16) for i in range(2)]
    ylb_b = [Buf() for _ in range(2)]
    ylb_ds = [DS() for _ in range(2)]

    ptx = [ps(st, "ptx%d" % i, [128, 1024], BF16) for i in range(2)]
    ptx_b = [Buf() for _ in range(2)]
    pp = [ps(st, "pp%d" % i, [128, 512], F32) for i in range(4)]
    pp_b = [Buf() for _ in range(4)]
    pg = [ps(st, "pg%d" % i, [128, 512], F32) for i in range(2)]
    pg_b = [Buf() for _ in range(2)]

    T.dma("sp", gvec[:], gvec_d[:, :], (), (B_const,), ds_const)
    T.dma("sp", constT[:].rearrange("p c k -> p (c k)"), constT_d[:, :], (), (B_const,), ds_const)
    T.dma("sp", lamv[:].rearrange("p a d -> p (a d)"), lamv_d[:, :], (), (B_const,), ds_const)
    T.dma("sp", subg[:], subg_d[:, :], (), (B_const,), ds_const)
    T.dma("sp", sc[:, 0:1], valid_d[:, :], (), (B_const,), ds_const)
    T.dma("pool", ident[:], ident_d[:, :], (), (B_const,), ds_const_p)
    T.dma("pool", gaw[:], ga_w.rearrange("n c d -> c n d"), (), (B_const,), ds_const_p)
    T.dma("pool", gxw[:], gx_w.rearrange("n c d -> c n d"), (), (B_const,), ds_const_p)
    CB = (B_const,)
    T.op("dve", lambda e: e.tensor_scalar(out=sc[:, 1:2], in0=sc[:, 0:1], scalar1=-1.0, scalar2=30000.0,
                                          op0=ALU.add, op1=ALU.mult), CB, CB)
    T.op("dve", lambda e: e.tensor_tensor(out=lamv[:, 0, :], in0=lamv[:, 0, :], in1=lamv[:, 1, :], op=ALU.mult), CB, CB)
    T.op("dve", lambda e: e.tensor_tensor(out=lamv[:, 2, :], in0=lamv[:, 2, :], in1=lamv[:, 3, :], op=ALU.mult), CB, CB)
    T.op("dve", lambda e: e.tensor_reduce(out=sc[:, 4:5], in_=lamv[:, 0, :], axis=mybir.AxisListType.X, op=ALU.add), CB, CB)
    T.op("dve", lambda e: e.tensor_reduce(out=sc[:, 5:6], in_=lamv[:, 2, :], axis=mybir.AxisListType.X, op=ALU.add), CB, CB)
    T.op("act", lambda e: e.activation(out=sc[:, 4:6], in_=sc[:, 4:6], func=AF.Exp), CB, CB)
    T.op("dve", lambda e: e.tensor_tensor(out=sc[:, 2:3], in0=sc[:, 4:5], in1=sc[:, 5:6], op=ALU.subtract), CB, CB)
    T.op("dve", lambda e: e.tensor_scalar(out=sc[:, 2:3], in0=sc[:, 2:3], scalar1=LAM_INIT, scalar2=None, op0=ALU.add), CB, CB)
    T.op("dve", lambda e: e.tensor_scalar(out=sc[:, 3:4], in0=sc[:, 2:3], scalar1=-1.0, scalar2=None, op0=ALU.mult), CB, CB)
    T.op("dve", lambda e: e.tensor_scalar(out=subg[:], in0=subg[:], scalar1=1.0 - LAM_INIT, scalar2=None, op0=ALU.mult), CB, CB)
    T.op("act", lambda e: e.activation(out=c12[:, 0, :], in_=constT[:, :, 7], func=AF.Exp, scale=-1.0), CB, CB)
    T.op("act", lambda e: e.activation(out=c12[:, 0, :], in_=c12[:, 0, :], func=AF.Ln, bias=1.0), CB, CB)
    T.op("dve", lambda e: e.tensor_scalar(out=c12[:, 1, :], in0=c12[:, 0, :], scalar1=-16.0, scalar2=None, op0=ALU.mult), CB, CB)
    T.op("dve", lambda e: e.tensor_scalar(out=c12[:, 0, :], in0=c12[:, 0, :], scalar1=-8.0, scalar2=None, op0=ALU.mult), CB, CB)
    for n in range(16):
        T.op("dve", lambda e, n=n: e.memset(HL[:, n, 0, :], 0.0), (), (HL_b[n],))
        T.op("dve", lambda e, n=n: e.memset(HST[:, n, 0:1], 0.0), (), (HST_b[n],))

    def tile_info(kind, ti):
        if kind == "pre":
            return dict(x=xpre, r0=ti * 512, NT=512, cs=cs_pre, qc=None, kc=ti * 512, ko=None, vo=None)
        if kind == "main":
            return dict(x=xm, r0=ti * 512, NT=512, cs=cs_main, qc=ti * 512, kc=NPRE + ti * 512, ko=k_main, vo=v_main)
        return dict(x=xs, r0=0, NT=256, cs=cs_smp, qc=NMAIN, kc=NPRE + NMAIN, ko=k_smp, vo=v_smp)

    xloads = []
    for kind, ti in s1_tiles:
        inf = tile_info(kind, ti)
        for g in range(inf["NT"] // 128):
            xloads.append((inf["x"], inf["r0"] + g * 128))
    xl = {"i": 0}

    def x_prefetch(upto):
        while xl["i"] <= min(upto, len(xloads) - 1):
            i = xl["i"]
            src, r = xloads[i]
            T.dma("sp", xbuf[i % 2][:], src[r:r + 128, :], (), (xbuf_b[i % 2],), xbuf_ds[i % 2])
            xl["i"] += 1

    def rstd_from_ss(ss_ap, n, bufs):
        T.op("dve", lambda e: e.tensor_scalar(out=ss_ap, in0=ss_ap, scalar1=1.0 / n, scalar2=EPS,
                                              op0=ALU.mult, op1=ALU.add), bufs, bufs)
        T.op("act", lambda e: e.activation(out=ss_ap, in_=ss_ap, func=AF.Sqrt), bufs, bufs)
        T.op("dve", lambda e: e.reciprocal(out=ss_ap, in_=ss_ap), bufs, bufs)

    def norm_transpose(src_tile, src_b, dstT, dst_b, g, gcol0, ptxs, ptxs_b, junk, junk_b, ssq_ap, ssq_buf):
        T.op("act", lambda e: e.activation(out=junk[:], in_=src_tile, func=AF.Square, accum_out=ssq_ap),
             (src_b,), (junk_b, ssq_buf))
        rstd_from_ss(ssq_ap, float(D), (ssq_buf,))
        T.op("dve", lambda e: e.tensor_scalar_mul(out=junk[:], in0=src_tile, scalar1=ssq_ap),
             (src_b, ssq_buf), (junk_b,))
        for q4 in range(4):
            pt = ptxs[q4 % 2]
            ptb = ptxs_b[q4 % 2]
            for j in range(8):
                kc = q4 * 8 + j
                T.op("pe", lambda e, kc=kc, j=j, pt=pt: e.transpose(out=pt[:, j * 128:(j + 1) * 128],
                                                                    in_=junk[:, kc * 128:(kc + 1) * 128], identity=ident[:]),
                     (junk_b, B_const), (ptb,))
            gsl = gvec[:, gcol0 + q4 * 8:gcol0 + q4 * 8 + 8].unsqueeze(2).to_broadcast([128, 8, 128])
            T.op("dve", lambda e, pt=pt, q4=q4, gsl=gsl: e.tensor_tensor(
                out=dstT[:, q4 * 8:(q4 + 1) * 8, g * 128:(g + 1) * 128],
                in0=pt[:].rearrange("p (k t) -> p k t", k=8), in1=gsl, op=ALU.mult),
                (ptb, B_const), (dst_b,))

    deferred = []

    def flush_deferred():
        while deferred:
            deferred.pop(0)()

    first_main = [True]
    gcount = 0
    qkT_i = 0
    alt = {"qk": 0, "v": 0, "y": 0}
    dbg_tiles = int(os.environ.get("DBG_TILES", "99"))
    XR = int(os.environ.get("DBG_XR", "99"))
    dbg_blocks = int(os.environ.get("DBG_BLOCKS", "99"))
    for kind, ti in s1_tiles[:dbg_tiles]:
        inf = tile_info(kind, ti)
        NT = inf["NT"]
        NG = NT // 128
        nseg = 4 if kind == "smp" else 1
        Lseg = NT // nseg
        blocks = (IN_PRE if kind == "pre" else IN_MAIN)[:dbg_blocks]
        if kind == "main" and first_main[0]:
            first_main[0] = False
            for n in range(16):
                T.op("dve", lambda e, n=n: e.tensor_scalar_mul(out=HL[:, n, 0, :], in0=HL[:, n, 0, :], scalar1=VALID),
                     (HL_b[n], B_const), (HL_b[n],))
                T.op("dve", lambda e, n=n: e.tensor_scalar_mul(out=HST[:, n, 0:1], in0=HST[:, n, 0:1], scalar1=VALID),
                     (HST_b[n], B_const), (HST_b[n],))
        if kind == "smp":
            for n in range(16):
                T.op("dve", lambda e, n=n: e.tensor_copy(out=stout[:, n, 0:3], in_=HL[:, n, 0, :]), (HL_b[n],), (stout_b,))
                T.op("dve", lambda e, n=n: e.tensor_copy(out=stout[:, n, 3:4], in_=HST[:, n, 0:1]), (HST_b[n],), (stout_b,))
            T.dma("sp", st_main[:, :], stout[:].rearrange("p c k -> p (c k)"), (stout_b,), (), stout_ds)
            for n in range(16):
                T.op("dve", lambda e, n=n: e.tensor_copy(
                    out=HL[:, n, :, :], in_=constT[:, n, 8:20].rearrange("p (s j) -> p s j", s=4)), CB, (HL_b[n],))
                T.op("dve", lambda e, n=n: e.tensor_copy(out=HST[:, n, :], in_=constT[:, n, 20:24]), CB, (HST_b[n],))
        T.dma("sp", cst[:, 0:NG, :], inf["cs"][inf["r0"]:inf["r0"] + NT, :].rearrange("(g p) c -> p g c", p=128),
              (), (cst_b,), cst_ds)
        for g in range(NG):
            x_prefetch(gcount + 1)
            xb = xbuf[gcount % 2]
            norm_transpose(xb[:], xbuf_b[gcount % 2], xnT, xnT_b[g], g, 0, ptx, ptx_b, xsb, xsb_b,
                           ssq[:, g:g + 1], ssq_b)
            gcount += 1
        for (typ, bi) in blocks:
            col0 = COL0[typ] + bi * 512
            for kh in range(2):
                slot, slb = w_next(("in", col0, kh))
                sl3 = slot[:].rearrange("p (kc c) -> p kc c", kc=16)
                if typ in ("q", "k", "v"):
                    for g in range(NG):
                        for kc in range(16):
                            T.op("pe", lambda e, g=g, kc=kc, kh=kh, sl3=sl3: e.matmul(
                                pp[g][:, :], lhsT=xnT[:, kh * 16 + kc, g * 128:(g + 1) * 128], rhs=sl3[:, kc, :],
                                start=(kh == 0 and kc == 0), stop=(kh == 1 and kc == 15)),
                                (xnT_b[g], slb), (pp_b[g],))
                else:
                    for cc in range(4):
                        for kc in range(16):
                            T.op("pe", lambda e, cc=cc, kc=kc, kh=kh, sl3=sl3: e.matmul(
                                pp[cc][:, 0:NT], lhsT=sl3[:, kc, cc * 128:(cc + 1) * 128], rhs=xnT[:, kh * 16 + kc, 0:NT],
                                start=(kh == 0 and kc == 0), stop=(kh == 1 and kc == 15)),
                                tuple(xnT_b[:NG]) + (slb,), (pp_b[cc],))
            flush_deferred()
            if typ in ("q", "k"):
                sub0 = (bi * 4) if typ == "q" else (16 + bi * 4)
                for g in range(NG):
                    T.op("act", lambda e, g=g: e.activation(out=qk32[g][:], in_=pp[g][:, :], func=AF.Copy),
                         (pp_b[g],), (qk32_b[g],))
                for g in range(NG):
                    a_ = g
                    q32, q32b = qk32[g], qk32_b[g]
                    q3 = q32[:].rearrange("p (s d) -> p s d", s=4)
                    x1 = q3[:, :, 0:16]
                    x2 = q3[:, :, 16:32]
                    cos = cst[:, g, 0:64].rearrange("p (s d) -> p s d", s=4)
                    sin = cst[:, g, 64:128].rearrange("p (s d) -> p s d", s=4)
                    rb = (q32b, cst_b)
                    T.op("dve", lambda e, x1=x1, cos=cos: e.tensor_tensor(out=rtmp[:, 0], in0=x1, in1=cos, op=ALU.mult), rb, (rtmp_b,))
                    T.op("dve", lambda e, x2=x2, sin=sin: e.tensor_tensor(out=rtmp[:, 1], in0=x2, in1=sin, op=ALU.mult), rb, (rtmp_b,))
                    T.op("dve", lambda e, x2=x2, cos=cos: e.tensor_tensor(out=rtmp[:, 2], in0=x2, in1=cos, op=ALU.mult), rb, (rtmp_b,))
                    T.op("dve", lambda e, x1=x1, sin=sin: e.tensor_tensor(out=rtmp[:, 3], in0=x1, in1=sin, op=ALU.mult), rb, (rtmp_b,))
                    T.op("dve", lambda e, x1=x1: e.tensor_tensor(out=x1, in0=rtmp[:, 0], in1=rtmp[:, 1], op=ALU.subtract),
                         (rtmp_b,), (q32b,))
                    T.op("dve", lambda e, x2=x2: e.tensor_tensor(out=x2, in0=rtmp[:, 2], in1=rtmp[:, 3], op=ALU.add),
                         (rtmp_b,), (q32b,))
                    if typ == "k" and inf["ko"] is not None:
                        r0 = inf["r0"] + g * 128
                        T.dma("sp", inf["ko"][r0:r0 + 128, bi * 512:(bi + 1) * 512], q32[:], (q32b,), (), qk32_ds[a_])
                    qb, qbb = qkbf[g], qkbf_b[g]
                    T.op("act", lambda e, q32=q32, qb=qb: e.activation(out=qb[:], in_=q32[:], func=AF.Copy), (q32b,), (qbb,))

                    def tr(g=g, qb=qb, qbb=qbb, sub0=sub0, typ=typ, bi=bi, last=(typ == "k" and bi == 1)):
                        pt, ptb = ptx[g % 2], ptx_b[g % 2]
                        for s in range(4):
                            T.op("pe", lambda e, s=s: e.transpose(out=pt[:, s * 128:(s + 1) * 128],
                                                                  in_=qb[:, s * 128:(s + 1) * 128], identity=ident[:]),
                                 (qbb, B_const), (ptb,))
                        st_t, st_b = qkT_st[g], qkT_b[g]
                        T.op("act", lambda e: e.activation(out=st_t[:, :, :],
                                                           in_=pt[:, 0:512].rearrange("p (s t) -> p s t", s=4), func=AF.Copy),
                             (ptb,), (st_b,))
                        if typ == "q":
                            c0 = inf["qc"] + g * 128
                            T.dma("sp", qT_s[sub0:sub0 + 4, :, c0:c0 + 128].rearrange("s d t -> d s t"), st_t[:, :, :],
                                  (st_b,), (), qkT_ds[g])
                        else:
                            c0 = inf["kc"] + g * 128
                            T.dma("sp", kT_s[sub0 - 16:sub0 - 12, :, c0:c0 + 128].rearrange("s d t -> d s t"), st_t[:, :, :],
                                  (st_b,), (), qkT_ds[g])
                    deferred.append(tr)
            elif typ == "v":
                for g in range(NG):
                    T.op("act", lambda e, g=g: e.activation(out=v32[g][:], in_=pp[g][:, :], func=AF.Copy),
                         (pp_b[g],), (v32_b[g],))
                for g in range(NG):
                    a_ = g
                    T.op("dve", lambda e, g=g, a_=a_: e.tensor_copy(out=qkbf[g][:], in_=v32[a_][:]),
                         (v32_b[a_],), (qkbf_b[g],))
                    if inf["vo"] is not None:
                        r0 = inf["r0"] + g * 128
                        T.dma("sp", inf["vo"][r0:r0 + 128, bi * 512:(bi + 1) * 512], v32[a_][:], (v32_b[a_],), (), v32_ds[a_])
                    r0 = inf["kc"] + g * 128
                    T.dma("sp", v_s[r0:r0 + 128, bi * 512:(bi + 1) * 512], qkbf[g][:], (qkbf_b[g],), (), vbf_ds[g])
            elif typ == "xr":
                W = 3 + Lseg

                def v3(t):
                    return t[:, 0:NT].rearrange("p (s l) -> p s l", s=nseg)

                for cc in range(4):
                    xh3 = xh4[:, cc, 0:nseg * W].rearrange("p (s w) -> p s w", s=nseg)
                    T.op("act", lambda e, cc=cc, xh3=xh3: e.activation(out=xh3[:, :, 3:W], in_=v3(pp[cc]), func=AF.Copy),
                         (pp_b[cc],), (xh4_b[cc],))
                for cc in range(4):
                    n = bi * 4 + cc
                    xh3 = xh4[:, cc, 0:nseg * W].rearrange("p (s w) -> p s w", s=nseg)
                    xh_b = xh4_b[cc]
                    if XR < 1:
                        continue
                    T.op("dve", lambda e, n=n: e.tensor_copy(out=xh3[:, :, 0:3], in_=HL[:, n, 0:nseg, :]), (HL_b[n],), (xh_b,))
                    T.op("dve", lambda e, n=n: e.tensor_copy(out=HL[:, n, 0:nseg, :], in_=xh3[:, :, W - 3:W]), (xh_b,), (HL_b[n],))
                    if kind == "smp":
                        T.op("dve", lambda e, n=n: e.tensor_copy(out=stout_s[:, n, :, 0:3], in_=xh3[:, :, W - 3:W]),
                             (xh_b,), (stout_b,))
                    if XR < 2:
                        continue
                    cw = constT[:, n, :]
                    xc, xc_b, xcb, xcb_b = xc4[:, cc, :], xc4_b[cc], xcb4[:, cc, :], xcb4_b[cc]
                    T.op("dve", lambda e, cw=cw, xc=xc: e.tensor_scalar(out=v3(xc), in0=xh3[:, :, 3:W], scalar1=cw[:, 3:4],
                                                                 scalar2=cw[:, 4:5], op0=ALU.mult, op1=ALU.add),
                         (xh_b, B_const), (xc_b,))
                    for j in range(3):
                        T.op("dve", lambda e, cw=cw, j=j, xc=xc: e.scalar_tensor_tensor(
                            out=v3(xc), in0=xh3[:, :, j:j + Lseg], scalar=cw[:, j:j + 1], in1=v3(xc),
                            op0=ALU.mult, op1=ALU.add), (xh_b, xc_b, B_const), (xc_b,))
                    T.op("act", lambda e, xc=xc, xcb=xcb: e.activation(out=xcb[:, 0:NT], in_=xc[:, 0:NT], func=AF.Copy), (xc_b,), (xcb_b,))

                    if XR < 3:
                        continue
                    def gates(n=n, cc=cc, cw=cw, xc=xc, xc_b=xc_b, xcb=xcb, xcb_b=xcb_b):
                        T.op("pe", lambda e: e.matmul(pg[0][:, 0:NT], lhsT=gaw[:, n, :], rhs=xcb[:, 0:NT], start=True, stop=True),
                             (xcb_b, B_const), (pg_b[0],))
                        T.op("pe", lambda e: e.matmul(pg[1][:, 0:NT], lhsT=gxw[:, n, :], rhs=xcb[:, 0:NT], start=True, stop=True),
                             (xcb_b, B_const), (pg_b[1],))
                        T.op("act", lambda e: e.activation(out=rr[:, 0:NT], in_=pg[0][:, 0:NT], func=AF.Sigmoid, bias=cw[:, 5:6]),
                             (pg_b[0], B_const), (rr_b,))
                        T.op("act", lambda e: e.activation(out=ii[:, 0:NT], in_=pg[1][:, 0:NT], func=AF.Sigmoid, bias=cw[:, 6:7]),
                             (pg_b[1], B_const), (ii_b,))
                        if XR < 4:
                            return
                        T.op("act", lambda e: e.activation(out=aa[:, 0:NT], in_=rr[:, 0:NT], func=AF.Exp, scale=c12[:, 0, n:n + 1]),
                             (rr_b, B_const), (aa_b,))
                        T.op("act", lambda e: e.activation(out=mm_[:, 0:NT], in_=rr[:, 0:NT], func=AF.Exp, scale=c12[:, 1, n:n + 1]),
                             (rr_b, B_const), (mm_b,))
                        T.op("act", lambda e: e.activation(out=mm_[:, 0:NT], in_=mm_[:, 0:NT], func=AF.Sqrt, bias=1.0, scale=-1.0),
                             (mm_b,), (mm_b,))
                        T.op("dve", lambda e: e.tensor_tensor(out=uu[:, 0:NT], in0=ii[:, 0:NT], in1=xc[:, 0:NT], op=ALU.mult),
                             (ii_b, xc_b), (uu_b,))
                        T.op("dve", lambda e: e.tensor_tensor(out=uu[:, 0:NT], in0=uu[:, 0:NT], in1=mm_[:, 0:NT], op=ALU.mult),
                             (uu_b, mm_b), (uu_b,))
                        if XR < 5:
                            return
                        for s in range(nseg):
                            T.op("dve", lambda e, s=s: e.tensor_tensor_scan(
                                out=hs4[:, cc, s * Lseg:(s + 1) * Lseg], data0=aa[:, s * Lseg:(s + 1) * Lseg],
                                data1=uu[:, s * Lseg:(s + 1) * Lseg], initial=HST[:, n, s:s + 1], op0=ALU.mult, op1=ALU.add),
                                (aa_b, uu_b, HST_b[n]), (hs4_b[cc],))
                        if XR < 6:
                            return
                        T.op("dve", lambda e: e.tensor_copy(
                            out=HST[:, n, 0:nseg],
                            in_=hs4[:, cc, 0:NT].rearrange("p (s l) -> p s l", s=nseg)[:, :, Lseg - 1]),
                            (hs4_b[cc],), (HST_b[n],))
                        if kind == "smp":
                            T.op("dve", lambda e: e.tensor_copy(out=stout_s[:, n, :, 3], in_=HST[:, n, :]), (HST_b[n],), (stout_b,))
                    deferred.append(gates)
            else:
                for cc in range(4):
                    T.op("act", lambda e, cc=cc: e.activation(out=gr4[:, cc, 0:NT], in_=pp[cc][:, 0:NT], func=AF.Copy),
                         (pp_b[cc],), (gr4_b[cc],))
                for cc in range(4):
                    n = bi * 4 + cc
                    gr32, gr32_b = gr4[:, cc, :], gr4_b[cc]
                    T.op("act", lambda e: e.activation(out=g1[:, 0:NT], in_=gr32[:, 0:NT], func=AF.Square), (gr32_b,), (g1_b,))
                    T.op("dve", lambda e: e.tensor_scalar(out=g1[:, 0:NT], in0=g1[:, 0:NT], scalar1=0.044715, scalar2=1.0,
                                                          op0=ALU.mult, op1=ALU.add), (g1_b,), (g1_b,))
                    T.op("dve", lambda e: e.tensor_tensor(out=g1[:, 0:NT], in0=g1[:, 0:NT], in1=gr32[:, 0:NT], op=ALU.mult),
                         (g1_b, gr32_b), (g1_b,))
                    T.op("act", lambda e: e.activation(out=g2[:, 0:NT], in_=g1[:, 0:NT], func=AF.Sigmoid, scale=GELU_C),
                         (g1_b,), (g2_b,))
                    T.op("dve", lambda e: e.tensor_tensor(out=g2[:, 0:NT], in0=g2[:, 0:NT], in1=gr32[:, 0:NT], op=ALU.mult),
                         (g2_b, gr32_b), (g2_b,))

                    def ymul(n=n, cc=cc):
                        a_ = alt["y"] % 2
                        alt["y"] += 1
                        T.op("dve", lambda e: e.tensor_tensor(out=ylb[a_][:, 0:NT], in0=g2[:, 0:NT], in1=hs4[:, cc, 0:NT], op=ALU.mult),
                             (g2_b, hs4_b[cc]), (ylb_b[a_],))
                        T.dma("sp", mixT_s[16 + n, :, inf["qc"]:inf["qc"] + NT], ylb[a_][:, 0:NT], (ylb_b[a_],), (), ylb_ds[a_])
                    flush_deferred()
                    ymul()
        flush_deferred()
    T.dma("sp", st_smp[:, :], stout_s[:].rearrange("p c s k -> p (c s k)"), (stout_b,), (), stout_ds)
    T.barrier()
    with nc.Block() as block:
        T.replay(block)
    st.close()
    if upto == 1:
        top.close()
        return nc

    st = ExitStack()
    kTt = [sb(st, "kTt%d" % i, [128, 2, 4096], BF16) for i in range(2)]
    vat = [sb(st, "vat%d" % i, [128, 32, 258], BF16) for i in range(2)]
    qTt = [sb(st, "qTt%d" % i, [128, 4, 2048], BF16) for i in range(2)]
    kvq_b = [(Buf(), Buf(), Buf()) for _ in range(2)]
    kvq_ds = [DS() for _ in range(2)]
    kvq_dsp = [DS() for _ in range(2)]
    kcb = sb(st, "kcb", [128, 16, 256], BF16)
    kcb_b = Buf()
    kcb_ds = DS()
    pT = [sb(st, "pT%d" % i, [128, 2, 256], BF16) for i in range(2)]
    pT_b = [Buf() for _ in range(2)]
    o32 = sb(st, "o32", [128, 4, 260], F32)
    o32_b = Buf()
    d32 = sb(st, "d32", [128, 2, 256], F32)
    d32_b = Buf()
    rs = sb(st, "rs", [128, 8], F32)
    rs_b = Buf()
    onb = sb(st, "onb", [128, 2, 256], BF16)
    onb_b = Buf()
    junk2 = sb(st, "junk2", [128, 256], F32)
    junk2_b = Buf()
    oT_st = [sb(st, "oTst%d" % i, [128, 2, 256], BF16) for i in range(2)]
    oT_b = [Buf() for _ in range(2)]
    oT_ds = [DS() for _ in range(2)]
    psc = [ps(st, "psc%d" % i, [128, 512], F32) for i in range(2)]
    psc_b = [Buf() for _ in range(2)]
    po = [ps(st, "po%d" % i, [128, 512], F32) for i in range(4)]
    po_b = [Buf() for _ in range(4)]
    pt2 = [ps(st, "pt2_%d" % i, [128, 1024], BF16) for i in range(2)]
    pt2_b = [Buf() for _ in range(2)]
    for i in range(2):
        T.op("dve", lambda e, i=i: e.memset(vat[i][:, :, 256:258], 1.0), (), (kvq_b[i][1],))
    acount = {"o": 0, "blk": 0}

    def attn_block(kT_of, v_of, q_of, bias_of, nblocks, band0, NS, kbuf, qbuf, out_store):
        QW = NS * 128
        items = []
        for j in range(nblocks):
            spj = (j - band0) if band0 is not None else -1
            col0 = 128 * spj if spj > 0 else 0
            items.append((j, spj, col0))

        def qk(it):
            j, spj, col0 = it
            pb = acount["blk"] % 2
            for c in range(2):
                kt = kT_of(c, j)
                nk = kt.shape[1]
                T.op("pe", lambda e, c=c, kt=kt, nk=nk, col0=col0, pb=pb: e.matmul(
                    psc[pb][0:nk, c * 256 + col0:c * 256 + QW], lhsT=kt, rhs=q_of(c, col0), start=True, stop=True),
                    tuple(kbuf) + (qbuf,), (psc_b[pb],))
            nk = kT_of(0, j).shape[1]
            bj = bias_of(j)
            T.op("act", lambda e, pb=pb, nk=nk, col0=col0, bj=bj: e.activation(
                out=pT[pb][0:nk, :, col0:QW],
                in_=psc[pb][0:nk, :].rearrange("p (c t) -> p c t", c=2)[:, :, col0:QW],
                func=AF.Exp, scale=SCALE, bias=bj), (psc_b[pb], B_const), (pT_b[pb],))
            if spj >= 0:
                T.op("dve", lambda e, pb=pb, spj=spj: e.memset(pT[pb][64:128, :, spj * 128:spj * 128 + 64], 0.0),
                     (), (pT_b[pb],))
            acount["blk"] += 1
            return pb

        def pv(it, pb):
            j, spj, col0 = it
            vv, nk = v_of(j)
            for c in range(2):
                for s in range(max(spj, 0), NS):
                    last = (band0 + s) if band0 is not None else nblocks - 1
                    T.op("pe", lambda e, c=c, s=s, vv=vv, nk=nk, pb=pb, j=j, last=last: e.matmul(
                        po[c * 2 + s][:, 0:257], lhsT=pT[pb][0:nk, c, s * 128:(s + 1) * 128], rhs=vv,
                        start=(j == 0), stop=(j == last)), (pT_b[pb],) + tuple(kbuf), (po_b[c * 2 + s],))

        prev = None
        for it in items:
            pb = qk(it)
            if prev is not None:
                pv(*prev)
            prev = (it, pb)
        pv(*prev)
        for c in range(2):
            for s in range(NS):
                i4 = c * 2 + s
                if i4 % 2 == 0:
                    T.op("act", lambda e, i4=i4: e.activation(out=o32[:, i4, 0:257], in_=po[i4][:, 0:257], func=AF.Copy),
                         (po_b[i4],), (o32_b,))
                else:
                    T.op("dve", lambda e, i4=i4: e.tensor_copy(out=o32[:, i4, 0:257], in_=po[i4][:, 0:257]),
                         (po_b[i4],), (o32_b,))
        ob = (o32_b, rs_b)
        for s in range(NS):
            T.op("dve", lambda e, s=s: e.reciprocal(out=rs[:, 0:1], in_=o32[:, s, 256:257]), (o32_b,), (rs_b,))
            T.op("dve", lambda e, s=s: e.reciprocal(out=rs[:, 1:2], in_=o32[:, 2 + s, 256:257]), (o32_b,), (rs_b,))
            T.op("dve", lambda e: e.tensor_tensor(out=rs[:, 1:2], in0=rs[:, 1:2], in1=NLAM, op=ALU.mult), (rs_b, B_const), (rs_b,))
            T.op("dve", lambda e, s=s: e.tensor_scalar_mul(out=d32[:, s, :], in0=o32[:, s, 0:256], scalar1=rs[:, 0:1]), ob, (d32_b,))
            T.op("dve", lambda e, s=s: e.scalar_tensor_tensor(out=d32[:, s, :], in0=o32[:, 2 + s, 0:256], scalar=rs[:, 1:2],
                                                              in1=d32[:, s, :], op0=ALU.mult, op1=ALU.add),
                 (o32_b, rs_b, d32_b), (d32_b,))
            T.op("act", lambda e, s=s: e.activation(out=junk2[:], in_=d32[:, s, :], func=AF.Square, accum_out=rs[:, 2:3]),
                 (d32_b,), (junk2_b, rs_b))
            rstd_from_ss(rs[:, 2:3], 256.0, (rs_b,))
            T.op("dve", lambda e, s=s: e.scalar_tensor_tensor(out=onb[:, s, :], in0=d32[:, s, :], scalar=rs[:, 2:3], in1=subg[:],
                                                              op0=ALU.mult, op1=ALU.mult), (d32_b, rs_b, B_const), (onb_b,))
        oi = acount["o"] % 2
        acount["o"] += 1
        for s in range(NS):
            for eh in range(2):
                T.op("pe", lambda e, s=s, eh=eh: e.transpose(out=pt2[oi][:, (eh * 2 + s) * 128:(eh * 2 + s + 1) * 128],
                                                             in_=onb[:, s, eh * 128:(eh + 1) * 128], identity=ident[:]),
                     (onb_b, B_const), (pt2_b[oi],))
        T.op("act", lambda e: e.activation(out=oT_st[oi][:, :, 0:QW],
                                           in_=pt2[oi][:, 0:512].rearrange("p (a t) -> p a t", a=2)[:, :, 0:QW], func=AF.Copy),
             (pt2_b[oi],), (oT_b[oi],))
        out_store(oT_st[oi], oT_b[oi], oT_ds[oi])

    for h in range(4):
        ab = h % 2
        kb, vb, qb = kvq_b[ab]
        T.dma("sp", kTt[ab][:], kT_s[h * 2:h * 2 + 2, :, 0:4096].rearrange("c d t -> d c t"), (), (kb,), kvq_ds[ab])
        T.dma("sp", vat[ab][:, :, 0:256], v_s[0:4096, h * 256:(h + 1) * 256].rearrange("(g p) e -> p g e", p=128),
              (), (vb,), kvq_ds[ab])
        T.dma("sp", qTt[ab][:], qT_s[h * 4:h * 4 + 4, :, 0:2048].rearrange("s d t -> d s t"), (), (qb,), kvq_ds[ab])
        kvb = Buf()
        for g in range(2):
            for Q in range(8):
                A = NPRE + Q * 256
                nfull = A // 128

                def kT_of(c, j, ab=ab):
                    return kTt[ab][:, c, j * 128:(j + 1) * 128]

                def v_of(j, ab=ab):
                    return vat[ab][:, j, 0:257], 128

                def q_of(c, col0, ab=ab, g=g, Q=Q):
                    return qTt[ab][:, g * 2 + c, Q * 256 + col0:(Q + 1) * 256]

                def bias_of(j):
                    return PREB if j < 16 else 0.0

                def out_store(ot, otb, ods, h=h, g=g, Q=Q):
                    ch = (h * 2 + g) * 2
                    T.dma("sp", mixT_s[ch:ch + 2, :, Q * 256:(Q + 1) * 256].rearrange("a d t -> d a t"), ot[:, :, :],
                          (otb,), (), ods)

                attn_block(kT_of, v_of, q_of, bias_of, nfull + 2, nfull, 2, (kb, vb), qb, out_store)
    T.barrier()
    with nc.Block() as block:
        T.replay(block)

    kTs = kTt
    for sq in range(4):
        for h in range(4):
            ab = (sq * 4 + h) % 2
            kb, vb, qb = kvq_b[ab]
            T.dma("pool", kcb[:], ck[sq, :, h * 256:(h + 1) * 256].rearrange("(g p) e -> p g e", p=128), (), (kcb_b,), kcb_ds)
            for gg in range(16):
                for c in range(2):
                    idx = gg * 2 + c
                    pti = (idx // 8) % 2
                    T.op("pe", lambda e, gg=gg, c=c, idx=idx, pti=pti: e.transpose(
                        out=pt2[pti][:, (idx % 8) * 128:(idx % 8 + 1) * 128], in_=kcb[:, gg, c * 128:(c + 1) * 128],
                        identity=ident[:]), (kcb_b, B_const), (pt2_b[pti],))
                if gg % 4 == 3:
                    pti = ((gg * 2) // 8) % 2
                    g0 = gg - 3
                    T.op("dve", lambda e, pti=pti, g0=g0, ab=ab: e.tensor_copy(
                        out=kTs[ab][:, :, g0 * 128:(g0 + 4) * 128].rearrange("p c (g t) -> p g c t", g=4),
                        in_=pt2[pti][:].rearrange("p (g c t) -> p g c t", g=4, c=2)), (pt2_b[pti],), (kb,))
            T.dma("sp", kTs[ab][:, :, 2048:2112],
                  kT_s[h * 2:h * 2 + 2, :, NPRE + NMAIN + sq * 64:NPRE + NMAIN + sq * 64 + 64].rearrange("c d t -> d c t"),
                  (), (kb,), kvq_ds[ab])
            T.dma("pool", vat[ab][:, 0:16, 0:256], cv[sq, :, h * 256:(h + 1) * 256].rearrange("(g p) e -> p g e", p=128),
                  (), (vb,), kvq_dsp[ab])
            r0 = NPRE + NMAIN + sq * 64
            T.dma("sp", vat[ab][0:64, 16, 0:256], v_s[r0:r0 + 64, h * 256:(h + 1) * 256], (), (vb,), kvq_ds[ab])
            c0 = NMAIN + sq * 64
            for g_ in range(2):
                for c_ in range(2):
                    T.dma("sp", qTt[ab][:, c_, g_ * 64:(g_ + 1) * 64], qT_s[h * 4 + g_ * 2 + c_, :, c0:c0 + 64],
                          (), (qb,), kvq_ds[ab])

            def kT_of(c, j, ab=ab):
                return kTs[ab][:, c, j * 128:min((j + 1) * 128, 2112)]

            def v_of(j, ab=ab):
                nk = 128 if j < 16 else 64
                return vat[ab][0:nk, j, 0:257], nk

            def q_of(c, col0, ab=ab):
                return qTt[ab][:, c, 0:128]

            def bias_of(j):
                return 0.0

            def out_store(ot, otb, ods, h=h, sq=sq):
                for g in range(2):
                    ch = (h * 2 + g) * 2
                    c0 = NMAIN + sq * 64
                    T.dma("sp", mixT_s[ch:ch + 2, :, c0:c0 + 64].rearrange("a d t -> d a t"), ot[:, :, g * 64:(g + 1) * 64],
                          (otb,), (), ods)

            attn_block(kT_of, v_of, q_of, bias_of, 17, None, 1, (kb, vb), qb, out_store)
    T.barrier()
    with nc.Block() as block:
        T.replay(block)
    st.close()
    if upto == 2:
        top.close()
        return nc

    st = ExitStack()
    ws_begin(st, 4, plan3)
    hh = sb(st, "hh", [128, 4, D], F32)
    hh_b = [Buf() for _ in range(4)]
    hh_ds = [DS() for _ in range(4)]
    actT = sb(st, "actT", [128, 32, 512], BF16)
    actT_b = [Buf() for _ in range(4)]
    actT_ds = DS()
    hsb = sb(st, "hsb", [128, D], BF16)
    hsb_b = Buf()
    gfin = sb(st, "gfin", [128, D], F32)
    ssq3 = sb(st, "ssq3", [128, 8], F32)
    ssq3_b = Buf()
    z32 = [sb(st, "z32_%d" % i, [128, 512], F32) for i in range(2)]
    z32_b = [Buf() for _ in range(2)]
    zT = [sb(st, "zT%d" % i, [128, 4, 512], BF16) for i in range(2)]
    zT_b = [Buf() for _ in range(2)]
    ptx1 = ps(st, "ptx3", [128, 1024], BF16)
    ptx = [ptx1, ptx1]
    ptx1_b = Buf()
    ptx_b = [ptx1_b, ptx1_b]
    pp = [ps(st, "pp3_%d" % i, [128, 512], F32) for i in range(4)]
    pp_b = [Buf() for _ in range(4)]
    pd = [ps(st, "pd%d" % i, [128, 512], F32) for i in range(3)]
    pd_b = [Buf() for _ in range(3)]
    T.dma("sp", gfin[:], gfin_d[:, :], (), (B_const,), ds_const)

    pcount = 0
    for kind, ti in s3_tiles:
        if kind == "main":
            xsrc, r0, NT, c0, ydst = xm, ti * 512, 512, ti * 512, y_main
        else:
            xsrc, r0, NT, c0, ydst = xs, 0, 256, NMAIN, y_smp
        NG = NT // 128
        for g in range(NG):
            T.dma("sp", actT[:, :, g * 128:(g + 1) * 128],
                  mixT_s[:, :, c0 + g * 128:c0 + (g + 1) * 128].rearrange("k d t -> d k t"), (), (actT_b[g],), actT_ds)
            T.dma("sp", hh[:, g, :], xsrc[r0 + g * 128:r0 + (g + 1) * 128, :], (), (hh_b[g],), hh_ds[g])
        for cb in range(8):
            for kh in range(2):
                slot, slb = w_next(("out", cb * 512, kh))
                sl3 = slot[:].rearrange("p (kc c) -> p kc c", kc=16)
                for g in range(NG):
                    for kc in range(16):
                        T.op("pe", lambda e, g=g, kc=kc, kh=kh, sl3=sl3: e.matmul(
                            pp[g][:, :], lhsT=actT[:, kh * 16 + kc, g * 128:(g + 1) * 128], rhs=sl3[:, kc, :],
                            start=(kh == 0 and kc == 0), stop=(kh == 1 and kc == 15)), (actT_b[g], slb), (pp_b[g],))
            for g in range(NG):
                T.op("dve", lambda e, g=g, cb=cb: e.tensor_tensor(out=hh[:, g, cb * 512:(cb + 1) * 512],
                                                                  in0=pp[g][:, :], in1=hh[:, g, cb * 512:(cb + 1) * 512], op=ALU.add),
                     (pp_b[g], hh_b[g]), (hh_b[g],))
        for g in range(NG):
            norm_transpose(hh[:, g, :], hh_b[g], actT, actT_b[g], g, 32, ptx, ptx_b, hsb, hsb_b, ssq3[:, g:g + 1], ssq3_b)
        pend = []

        def down(sbk, zi):
            nonlocal pcount
            slot_a, sla_b = w_next(("dA", sbk))
            slot_bb, slbb_b = w_next(("dB", sbk))
            sd = [slot_a[:].rearrange("p (f c) -> p f c", f=2), slot_bb[:].rearrange("p (f c) -> p f c", f=2)]
            sdb = [sla_b, slbb_b]
            for g in range(NG):
                for cb in range(8):
                    bk = pcount % 3
                    pcount += 1
                    for f in range(4):
                        T.op("pe", lambda e, g=g, cb=cb, f=f, bk=bk: e.matmul(
                            pd[bk][:, :], lhsT=zT[zi][:, f, g * 128:(g + 1) * 128], rhs=sd[f // 2][:, f % 2, cb * 512:(cb + 1) * 512],
                            start=(f == 0), stop=(f == 3)), (zT_b[zi], sdb[f // 2]), (pd_b[bk],))
                    T.op("dve", lambda e, g=g, cb=cb, bk=bk: e.tensor_tensor(
                        out=hh[:, g, cb * 512:(cb + 1) * 512], in0=pd[bk][:, :], in1=hh[:, g, cb * 512:(cb + 1) * 512], op=ALU.add),
                        (pd_b[bk], hh_b[g]), (hh_b[g],))

        for sbk in range(NSB):
            zi = sbk % 2
            for hf, nm in ((0, "upA"), (1, "upB")):
                slot_u, slu_b = w_next((nm, sbk))
                su3 = slot_u[:].rearrange("p (kc c) -> p kc c", kc=16)
                for f in range(4):
                    for kc in range(16):
                        T.op("pe", lambda e, f=f, kc=kc, su3=su3, hf=hf: e.matmul(
                            pp[f][:, 0:NT], lhsT=su3[:, kc, f * 128:(f + 1) * 128], rhs=actT[:, hf * 16 + kc, 0:NT],
                            start=(hf == 0 and kc == 0), stop=(hf == 1 and kc == 15)), tuple(actT_b[:NG]) + (slu_b,), (pp_b[f],))
            for f in range(4):
                T.op("act", lambda e, f=f: e.activation(out=z32[f % 2][:, 0:NT], in_=pp[f][:, 0:NT], func=AF.Relu),
                     (pp_b[f],), (z32_b[f % 2],))
                T.op("act", lambda e, f=f, zi=zi: e.activation(out=zT[zi][:, f, 0:NT], in_=z32[f % 2][:, 0:NT], func=AF.Square),
                     (z32_b[f % 2],), (zT_b[zi],))
            if pend:
                down(*pend.pop(0))
            pend.append((sbk, zi))
        down(*pend.pop(0))
        for g in range(NG):
            T.op("act", lambda e, g=g: e.activation(out=hsb[:], in_=hh[:, g, :], func=AF.Square, accum_out=ssq3[:, 4 + g:5 + g]),
                 (hh_b[g],), (hsb_b, ssq3_b))
            rstd_from_ss(ssq3[:, 4 + g:5 + g], float(D), (ssq3_b,))
            for hf in range(2):
                T.op("dve", lambda e, g=g, hf=hf: e.scalar_tensor_tensor(
                    out=hh[:, g, hf * 2048:(hf + 1) * 2048], in0=hh[:, g, hf * 2048:(hf + 1) * 2048], scalar=ssq3[:, 4 + g:5 + g],
                    in1=gfin[:, hf * 2048:(hf + 1) * 2048], op0=ALU.mult, op1=ALU.mult), (hh_b[g], ssq3_b, B_const), (hh_b[g],))
            T.dma("sp", ydst[r0 + g * 128:r0 + (g + 1) * 128, :], hh[:, g, :], (hh_b[g],), (), hh_ds[g])
    T.barrier()
    with nc.Block() as block:
        T.replay(block)
    st.close()
    top.close()
    return nc


_NC_CACHE = {}


def _rope_table(pos):
    half = 16
    inv = (np.float32(500000.0) ** (-np.arange(half, dtype=np.float32) / np.float32(half))).astype(np.float32)
    ang = pos.astype(np.float32)[:, None] * inv[None, :]
    cos = np.cos(ang).astype(np.float32)
    sin = np.sin(ang).astype(np.float32)
    return np.ascontiguousarray(np.concatenate([np.tile(cos, (1, 4)), np.tile(sin, (1, 4))], axis=1))


def kernel(x_prompt, x_sample, cache_k, cache_v, state_conv, state_lru,
           norm_mix, w_in, conv_w, conv_b, gate_a_w, gate_a_b, gate_x_w, gate_x_b,
           lru_lambda, lambda_q1, lambda_k1, lambda_q2, lambda_k2, subln_g,
           w_out, norm_mlp, w_up, w_down, norm_final):
    f32 = np.float32
    A = lambda a: np.ascontiguousarray(np.asarray(a, dtype=f32))
    x_prompt = A(x_prompt); x_sample = A(x_sample)
    cache_k = A(cache_k); cache_v = A(cache_v)
    state_conv = A(state_conv); state_lru = A(state_lru)
    upto = _NC_CACHE.get("upto", 3)
    if "nc" not in _NC_CACHE:
        _NC_CACHE["nc"] = build_program(upto)
    nc = _NC_CACHE["nc"]

    gvec = np.concatenate([A(norm_mix)[0].reshape(32, 128).T, A(norm_mlp)[0].reshape(32, 128).T], axis=1)
    gfin = np.tile(A(norm_final).reshape(1, D), (128, 1))
    lamv = np.tile(np.concatenate([A(lambda_q1)[0], A(lambda_k1)[0], A(lambda_q2)[0], A(lambda_k2)[0]]).reshape(1, 512), (128, 1))
    subg = np.tile(A(subln_g)[0].reshape(1, 256), (128, 1))
    ident = np.eye(128, dtype=f32)
    shared = dict(
        gvec=A(gvec), gfin=A(gfin), lamv=A(lamv), subg=A(subg), ident=ident,
        w_in=A(w_in)[0],
        w_out=A(w_out)[0] if upto >= 3 else np.zeros((128, 128), f32),
        w_up=A(w_up)[0] if upto >= 3 else np.zeros((128, 128), f32),
        w_down=A(w_down)[0] if upto >= 3 else np.zeros((128, 128), f32),
        ga_w=A(gate_a_w)[0], gx_w=A(gate_x_w)[0],
        cs_pre=_rope_table(np.arange(NPRE)),
        cs_smp=_rope_table(np.tile(PAST + np.arange(64), 4)),
    )
    in_maps = []
    for c in range(8):
        b, half = c // 2, c % 2
        rows = np.concatenate([
            A(conv_w)[0], A(conv_b), A(gate_a_b)[0].reshape(1, 2048), A(gate_x_b)[0].reshape(1, 2048), A(lru_lambda),
            state_conv[0, 4 * c:4 * c + 4].reshape(12, 2048), state_lru[0, 4 * c:4 * c + 4]], axis=0)
        constT = rows.reshape(24, 16, 128).transpose(2, 1, 0).reshape(128, 16 * 24)
        m = dict(shared)
        m.update(
            xm=x_prompt[b, half * 2048:(half + 1) * 2048],
            xpre=x_prompt[b, 0:2048],
            xs=x_sample[4 * c:4 * c + 4].reshape(256, D),
            ck=cache_k[0, 4 * c:4 * c + 4].reshape(4, PAST, 1024),
            cv=cache_v[0, 4 * c:4 * c + 4].reshape(4, PAST, 1024),
            constT=A(constT),
            valid=np.full((128, 1), float(half), dtype=f32),
            cs_main=_rope_table(half * 2048 + np.arange(NMAIN)),
        )
        in_maps.append({k: np.ascontiguousarray(v) for k, v in m.items()})
    if os.environ.get("DBG_CORES"):
        res = run_bass_kernel_spmd(nc, [in_maps[1]], core_ids=[0])
        R = [res.results[0]] * 8
    else:
        res = run_bass_kernel_spmd(nc, in_maps, core_ids=list(range(8)))
        R = res.results

    y_prompt = np.empty((4, 4096, D), f32)
    k_prompt = np.empty((1, 4, 4096, 4, 256), f32)
    v_prompt = np.empty((1, 4, 4096, 4, 256), f32)
    conv_prompt = np.empty((1, 4, 3, 2048), f32)
    lru_prompt = np.empty((1, 4, 2048), f32)
    y_sample = np.empty((32, 64, D), f32)
    k_sample = np.empty((1, 32, 64, 4, 256), f32)
    v_sample = np.empty((1, 32, 64, 4, 256), f32)
    conv_sample = np.empty((1, 32, 3, 2048), f32)
    lru_sample = np.empty((1, 32, 2048), f32)
    for c in range(8):
        b, half = c // 2, c % 2
        r = R[c]
        sl = slice(half * 2048, (half + 1) * 2048)
        y_prompt[b, sl] = r["y_main"]
        k_prompt[0, b, sl] = r["k_main"].reshape(2048, 4, 256)
        v_prompt[0, b, sl] = r["v_main"].reshape(2048, 4, 256)
        if half == 1:
            stm = r["st_main"].reshape(128, 16, 4)
            full = stm.transpose(2, 1, 0).reshape(4, 2048)
            conv_prompt[0, b] = full[0:3]
            lru_prompt[0, b] = full[3]
        y_sample[4 * c:4 * c + 4] = r["y_smp"].reshape(4, 64, D)
        k_sample[0, 4 * c:4 * c + 4] = r["k_smp"].reshape(4, 64, 4, 256)
        v_sample[0, 4 * c:4 * c + 4] = r["v_smp"].reshape(4, 64, 4, 256)
        sts = r["st_smp"].reshape(128, 16, 4, 4)
        fulls = sts.transpose(2, 3, 1, 0).reshape(4, 4, 2048)
        conv_sample[0, 4 * c:4 * c + 4] = fulls[:, 0:3]
        lru_sample[0, 4 * c:4 * c + 4] = fulls[:, 3]
    return (y_prompt, y_sample, k_prompt, v_prompt, conv_prompt, lru_prompt,
            k_sample, v_sample, conv_sample, lru_sample)
```

```python
import math
import os
from contextlib import ExitStack

import numpy as np
import concourse.bass as bass
import concourse.mybir as mybir
from concourse.bass_utils import run_bass_kernel_spmd

F32 = mybir.dt.float32
BF16 = mybir.dt.bfloat16
AF = mybir.ActivationFunctionType
ALU = mybir.AluOpType

D = 4096
NPRE = 2048
NMAIN = 2048
NSMP = 256
NQ = NMAIN + NSMP
NK = NPRE + NMAIN + NSMP
PAST = 2048
EPS = 1e-6
LAM_INIT = 0.2
SCALE = 128 ** -0.5
NSLOT = 3
SAME_ENGINE_SYNC = True
GELU_C = 2.0 * math.sqrt(2.0 / math.pi)


class Buf:
    __slots__ = ("name", "w", "r")

    def __init__(self, name=""):
        self.name = name
        self.w = None
        self.r = {}


class DSem:
    def __init__(self, sem):
        self.sem = sem
        self.count = 0


class Ent:
    __slots__ = ("fn", "waits", "inc", "val", "dsem")

    def __init__(self, fn):
        self.fn = fn
        self.waits = []
        self.inc = False
        self.val = None
        self.dsem = None


class _Rec:
    def __getattr__(self, name):
        def f(*a, **kw):
            self.__dict__["call"] = (name, a, kw)
        return f


COMPUTE = ("pe", "dve", "act", "pool")
ENGS = ("pe", "dve", "act", "pool", "sp")


class Trk:
    def __init__(self, nc, csem):
        self.nc = nc
        self.csem = csem
        self.base = {e: 0 for e in COMPUTE}
        self.dsems = []
        self.epoch = 0
        self._reset()

    def _reset(self):
        self.streams = {e: [] for e in ENGS}
        self.waited = {e: {p: -1 for p in COMPUTE} for e in ENGS}
        self.waitedD = {e: {} for e in ENGS}

    def new_dsem(self, sem):
        d = DSem(sem)
        self.dsems.append(d)
        return d

    def _add_wait(self, eng, ent, ref):
        if ref is None or ref[0] != self.epoch:
            return
        if ref[1] == "c":
            _, _, peng, idx = ref
            if peng == eng and (eng in ("pe", "sp") or not SAME_ENGINE_SYNC):
                return
            if idx <= self.waited[eng][peng]:
                return
            self.waited[eng][peng] = idx
            self.streams[peng][idx].inc = True
            ent.waits.append(("c", peng, idx))
        else:
            ds = ref[2]
            cnt = ds.count
            if self.waitedD[eng].get(ds, 0) >= cnt:
                return
            self.waitedD[eng][ds] = cnt
            ent.waits.append(("d", ds, cnt))

    def _record(self, eng, ent, reads, writes, ref_maker):
        deps = []
        for b in reads:
            if b.w is not None:
                deps.append(b.w)
        for b in writes:
            if b.w is not None:
                deps.append(b.w)
            deps.extend(b.r.values())
        for d in deps:
            self._add_wait(eng, ent, d)
        idx = len(self.streams[eng])
        self.streams[eng].append(ent)
        ref = ref_maker(idx)
        key = ref[2] if ref[1] == "d" else eng
        for b in reads:
            b.r[key] = ref
        for b in writes:
            b.w = ref
            b.r = {}
        return ref

    def op(self, eng, fn, reads=(), writes=()):
        rec = _Rec()
        fn(rec)
        name, args, kw = rec.call
        ent = Ent(lambda e: getattr(e, name)(*args, **kw))
        return self._record(eng, ent, reads, writes,
                            lambda idx: (self.epoch, "c", eng, idx))

    def dma(self, eng, out, in_, reads, writes, dsem):
        ent = Ent(lambda e: e.dma_start(out=out, in_=in_))
        ent.dsem = dsem
        ref = self._record(eng, ent, reads, writes,
                           lambda idx: (self.epoch, "d", dsem))
        dsem.count += 16
        return ref

    def barrier(self):
        last = {}
        for e in COMPUTE:
            st = self.streams[e]
            for i in range(len(st) - 1, -1, -1):
                if st[i].dsem is None:
                    last[e] = i
                    break
        for eng in ENGS:
            ent = Ent(lambda e: e.nop())
            for p, i in last.items():
                if p != eng and i > self.waited[eng][p]:
                    self.streams[p][i].inc = True
                    ent.waits.append(("c", p, i))
            for ds in self.dsems:
                if ds.count > 0 and self.waitedD[eng].get(ds, 0) < ds.count:
                    ent.waits.append(("d", ds, ds.count))
            self.streams[eng].append(ent)
        self.epoch += 1

    def replay(self, block):
        nc = self.nc
        for e in COMPUTE:
            c = self.base[e]
            for ent in self.streams[e]:
                if ent.inc:
                    c += 1
                    ent.val = c
            self.base[e] = c
        streams = self.streams
        csem = self.csem

        def run(engname):
            def f(engobj):
                for ent in streams[engname]:
                    for w in ent.waits:
                        if w[0] == "c":
                            engobj.wait_ge(csem[w[1]], streams[w[1]][w[2]].val)
                        else:
                            engobj.wait_ge(w[1].sem, w[2])
                    ins = ent.fn(engobj)
                    if ent.dsem is not None:
                        ins.then_inc(ent.dsem.sem, 16)
                    elif ent.inc:
                        ins.then_inc(csem[engname], 1)
            return f

        block.tensor(run("pe"))
        block.vector(run("dve"))
        block.scalar(run("act"))
        block.gpsimd(run("pool"))
        block.sync(run("sp"))
        self._reset()


def build_program(upto=3):
    nc = bass.Bass("TRN2", target_bir_lowering=False)

    def din(name, shape, dt=F32):
        return nc.dram_tensor(name, list(shape), dt, kind="ExternalInput").ap()

    def dout(name, shape, dt=F32):
        return nc.dram_tensor(name, list(shape), dt, kind="ExternalOutput").ap()

    def dscr(name, shape, dt):
        return nc.dram_tensor(name, list(shape), dt).ap()

    xm = din("xm", [NMAIN, D])
    xpre = din("xpre", [NPRE, D])
    xs = din("xs", [NSMP, D])
    ck = din("ck", [4, PAST, 1024])
    cv = din("cv", [4, PAST, 1024])
    constT_d = din("constT", [128, 16 * 24])
    gvec_d = din("gvec", [128, 64])
    gfin_d = din("gfin", [128, D])
    lamv_d = din("lamv", [128, 512])
    subg_d = din("subg", [128, 256])
    valid_d = din("valid", [128, 1])
    ident_d = din("ident", [128, 128])
    cs_pre = din("cs_pre", [NPRE, 128])
    cs_main = din("cs_main", [NMAIN, 128])
    cs_smp = din("cs_smp", [NSMP, 128])
    w_in = din("w_in", [D, 8192])
    w_out = din("w_out", [D, D] if upto >= 3 else [128, 128])
    w_up = din("w_up", [D, 16384] if upto >= 3 else [128, 128])
    w_down = din("w_down", [16384, D] if upto >= 3 else [128, 128])
    ga_w = din("ga_w", [16, 128, 128])
    gx_w = din("gx_w", [16, 128, 128])

    y_main = dout("y_main", [NMAIN, D])
    y_smp = dout("y_smp", [NSMP, D])
    k_main = dout("k_main", [NMAIN, 1024])
    v_main = dout("v_main", [NMAIN, 1024])
    k_smp = dout("k_smp", [NSMP, 1024])
    v_smp = dout("v_smp", [NSMP, 1024])
    st_main = dout("st_main", [128, 64])
    st_smp = dout("st_smp", [128, 256])

    qT_s = dscr("qT_s", [16, 128, NQ], BF16)
    kT_s = dscr("kT_s", [8, 128, NK], BF16)
    v_s = dscr("v_s", [NK, 1024], BF16)
    mixT_s = dscr("mixT_s", [32, 128, NQ], BF16)

    top = ExitStack()
    sems = {e: top.enter_context(nc.semaphore("c_" + e)) for e in COMPUTE}
    T = Trk(nc, sems)
    nds = [0]

    def DS():
        nds[0] += 1
        return T.new_dsem(top.enter_context(nc.semaphore("d%d" % nds[0])))

    uniq = [0]

    def sb(stack, name, shape, dt):
        uniq[0] += 1
        return stack.enter_context(nc.sbuf_tensor("sb%d_%s" % (uniq[0], name), list(shape), dt))

    def ps(stack, name, shape, dt):
        uniq[0] += 1
        return stack.enter_context(nc.psum_tensor("ps%d_%s" % (uniq[0], name), list(shape), dt))

    MAXSLOT = 4
    slot_b = [Buf("slot%d" % i) for i in range(MAXSLOT)]
    slot_ds = [DS() for _ in range(MAXSLOT)]
    WS = {"slots": None, "n": 0, "plan": None, "issued": 0, "used": 0}

    def ws_begin(stack, n, plan_):
        WS["slots"] = [sb(stack, "wslot%d" % i, [128, 8192], BF16) for i in range(n)]
        WS["n"] = n
        WS["plan"] = plan_
        WS["issued"] = 0
        WS["used"] = 0
    ident = sb(top, "ident", [128, 128], BF16)
    gvec = sb(top, "gvec", [128, 64], F32)
    constT = sb(top, "constT", [128, 16, 24], F32)
    lamv = sb(top, "lamv", [128, 4, 128], F32)
    subg = sb(top, "subg", [128, 256], F32)
    sc = sb(top, "scal", [128, 16], F32)
    c12 = sb(top, "c12", [128, 2, 16], F32)
    B_const = Buf("consts")
    ds_const = DS()
    ds_const_p = DS()
    VALID = sc[:, 0:1]
    PREB = sc[:, 1:2]
    NLAM = sc[:, 3:4]

    IN_MAIN = ([("q", i) for i in range(4)] + [("k", i) for i in range(2)] + [("v", i) for i in range(2)]
               + [x for i in range(4) for x in (("xr", i), ("gr", i))])
    IN_PRE = [("k", 0), ("k", 1), ("v", 0), ("v", 1)] + [("xr", i) for i in range(4)]
    COL0 = {"q": 0, "k": 2048, "v": 3072, "xr": 4096, "gr": 6144}
    s1_tiles = ([("pre", i) for i in range(4)] + [("main", i) for i in range(4)] + [("smp", 0)])
    plan1 = []
    for kind, _ in s1_tiles:
        for (typ, bi) in (IN_PRE if kind == "pre" else IN_MAIN):
            for kh in range(2):
                plan1.append(("in", COL0[typ] + bi * 512, kh))
    s3_tiles = [("main", i) for i in range(4)] + [("smp", 0)]
    NSB = 32
    plan3 = []
    for _ in s3_tiles:
        for cb in range(8):
            for kh in range(2):
                plan3.append(("out", cb * 512, kh))
        for sbk in range(NSB):
            plan3.append(("upA", sbk))
            plan3.append(("upB", sbk))
            if sbk > 0:
                plan3.append(("dA", sbk - 1))
                plan3.append(("dB", sbk - 1))
        plan3.append(("dA", NSB - 1))
        plan3.append(("dB", NSB - 1))

    def w_issue(i):
        d = WS["plan"][i]
        s_ = i % WS["n"]
        slot = WS["slots"][s_]
        if d[0] == "in" or d[0] == "out":
            w = w_in if d[0] == "in" else w_out
            src = w[d[2] * 2048:(d[2] + 1) * 2048, d[1]:d[1] + 512].rearrange("(kc p) c -> p kc c", p=128)
            dst = slot[:].rearrange("p (kc c) -> p kc c", kc=16)
        elif d[0] in ("upA", "upB"):
            r0_ = 0 if d[0] == "upA" else 2048
            src = w_up[r0_:r0_ + 2048, d[1] * 512:(d[1] + 1) * 512].rearrange("(kc p) c -> p kc c", p=128)
            dst = slot[:].rearrange("p (kc c) -> p kc c", kc=16)
        else:
            r0_ = d[1] * 512 + (0 if d[0] == "dA" else 256)
            src = w_down[r0_:r0_ + 256, :].rearrange("(f p) c -> p f c", p=128)
            dst = slot[:].rearrange("p (f c) -> p f c", f=2)
        T.dma("pool", dst, src, (), (slot_b[s_],), slot_ds[s_])

    def w_next(desc):
        i = WS["used"]
        plan_ = WS["plan"]
        if os.environ.get("DBG_TILES") is None:
            assert plan_[i] == desc, (plan_[i], desc)
        else:
            plan_[i] = desc
        while WS["issued"] < min(len(plan_), i + 3):
            w_issue(WS["issued"])
            WS["issued"] += 1
        WS["used"] += 1
        s_ = i % WS["n"]
        return WS["slots"][s_], slot_b[s_]

    st = ExitStack()
    ws_begin(st, 3, plan1)
    xbuf = [sb(st, "xbuf%d" % i, [128, D], F32) for i in range(2)]
    xbuf_b = [Buf("xbuf%d" % i) for i in range(2)]
    xbuf_ds = [DS() for _ in range(2)]
    xsb = sb(st, "xsb", [128, D], BF16)
    xsb_b = Buf("xsb")
    xnT = sb(st, "xnT", [128, 32, 512], BF16)
    xnT_b = [Buf("xnT%d" % g) for g in range(4)]
    ssq = sb(st, "ssq", [128, 4], F32)
    ssq_b = Buf("ssq")
    cst = sb(st, "cst", [128, 4, 128], F32)
    cst_b = Buf("cst")
    cst_ds = DS()
    qk32 = [sb(st, "qk32_%d" % i, [128, 512], F32) for i in range(4)]
    qk32_b = [Buf() for _ in range(4)]
    qk32_ds = [DS() for _ in range(4)]
    rtmp = sb(st, "rtmp", [128, 4, 4, 16], F32)
    rtmp_b = Buf()
    qkbf = [sb(st, "qkbf%d" % i, [128, 512], BF16) for i in range(4)]
    qkbf_b = [Buf() for _ in range(4)]
    qkT_st = [sb(st, "qkTst%d" % i, [128, 4, 128], BF16) for i in range(4)]
    qkT_b = [Buf() for _ in range(4)]
    qkT_ds = [DS() for _ in range(4)]
    v32, v32_b, v32_ds = qk32, qk32_b, qk32_ds
    vbf_ds = [DS() for _ in range(4)]
    gaw = sb(st, "gaw", [128, 16, 128], BF16)
    gxw = sb(st, "gxw", [128, 16, 128], BF16)
    HL = sb(st, "HL", [128, 16, 4, 3], F32)
    HST = sb(st, "HST", [128, 16, 4], F32)
    HL_b = [Buf() for _ in range(16)]
    HST_b = [Buf() for _ in range(16)]
    stout = sb(st, "stout", [128, 16, 4], F32)
    stout_s = sb(st, "stout_s", [128, 16, 4, 4], F32)
    stout_b = Buf()
    stout_ds = DS()

    def L(name, shape, dt=F32):
        return sb(st, name, shape, dt), Buf(name)

    xh4 = sb(st, "xh4", [128, 4, 528], F32)
    xh4_b = [Buf() for _ in range(4)]
    xc4 = sb(st, "xc4", [128, 4, 512], F32)
    xc4_b = [Buf() for _ in range(4)]
    xcb4 = sb(st, "xcb4", [128, 4, 512], BF16)
    xcb4_b = [Buf() for _ in range(4)]
    rr, rr_b = L("rr", [128, 512])
    ii, ii_b = L("ii", [128, 512])
    aa, aa_b = L("aa", [128, 512])
    mm_, mm_b = L("mm", [128, 512])
    uu, uu_b = L("uu", [128, 512])
    hs4 = sb(st, "hs4", [128, 4, 512], F32)
    hs4_b = [Buf() for _ in range(4)]
    gr4 = sb(st, "gr4", [128, 4, 512], F32)
    gr4_b = [Buf() for _ in range(4)]
    g1, g1_b = ii, ii_b
    g2, g2_b = aa, aa_b
    ylb = [sb(st, "ylb%d" % i, [128, 512], BF16) for i in range(2)]
    ylb_b = [Buf() for _ in range(2)]
    ylb_ds = [DS() for _ in range(2)]

    ptx = [ps(st, "ptx%d" % i, [128, 1024], BF16) for i in range(2)]
    ptx_b = [Buf() for _ in range(2)]
    pp = [ps(st, "pp%d" % i, [128, 512], F32) for i in range(4)]
    pp_b = [Buf() for _ in range(4)]
    pg = [ps(st, "pg%d" % i, [128, 512], F32) for i in range(2)]
    pg_b = [Buf() for _ in range(2)]

    T.dma("sp", gvec[:], gvec_d[:, :], (), (B_const,), ds_const)
    T.dma("sp", constT[:].rearrange("p c k -> p (c k)"), constT_d[:, :], (), (B_const,), ds_const)
    T.dma("sp", lamv[:].rearrange("p a d -> p (a d)"), lamv_d[:, :], (), (B_const,), ds_const)
    T.dma("sp", subg[:], subg_d[:, :], (), (B_const,), ds_const)
    T.dma("sp", sc[:, 0:1], valid_d[:, :], (), (B_const,), ds_const)
    T.dma("pool", ident[:], ident_d[:, :], (), (B_const,), ds_const_p)
    T.dma("pool", gaw[:], ga_w.rearrange("n c d -> c n d"), (), (B_const,), ds_const_p)
    T.dma("pool", gxw[:], gx_w.rearrange("n c d -> c n d"), (), (B_const,), ds_const_p)
    CB = (B_const,)
    T.op("dve", lambda e: e.tensor_scalar(out=sc[:, 1:2], in0=sc[:, 0:1], scalar1=-1.0, scalar2=30000.0,
                                          op0=ALU.add, op1=ALU.mult), CB, CB)
    T.op("dve", lambda e: e.tensor_tensor(out=lamv[:, 0, :], in0=lamv[:, 0, :], in1=lamv[:, 1, :], op=ALU.mult), CB, CB)
    T.op("dve", lambda e: e.tensor_tensor(out=lamv[:, 2, :], in0=lamv[:, 2, :], in1=lamv[:, 3, :], op=ALU.mult), CB, CB)
    T.op("dve", lambda e: e.tensor_reduce(out=sc[:, 4:5], in_=lamv[:, 0, :], axis=mybir.AxisListType.X, op=ALU.add), CB, CB)
    T.op("dve", lambda e: e.tensor_reduce(out=sc[:, 5:6], in_=lamv[:, 2, :], axis=mybir.AxisListType.X, op=ALU.add), CB, CB)
    T.op("act", lambda e: e.activation(out=sc[:, 4:6], in_=sc[:, 4:6], func=AF.Exp), CB, CB)
    T.op("dve", lambda e: e.tensor_tensor(out=sc[:, 2:3], in0=sc[:, 4:5], in1=sc[:, 5:6], op=ALU.subtract), CB, CB)
    T.op("dve", lambda e: e.tensor_scalar(out=sc[:, 2:3], in0=sc[:, 2:3], scalar1=LAM_INIT, scalar2=None, op0=ALU.add), CB, CB)
    T.op("dve", lambda e: e.tensor_scalar(out=sc[:, 3:4], in0=sc[:, 2:3], scalar1=-1.0, scalar2=None, op0=ALU.mult), CB, CB)
    T.op("dve", lambda e: e.tensor_scalar(out=subg[:], in0=subg[:], scalar1=1.0 - LAM_INIT, scalar2=None, op0=ALU.mult), CB, CB)
    T.op("act", lambda e: e.activation(out=c12[:, 0, :], in_=constT[:, :, 7], func=AF.Exp, scale=-1.0), CB, CB)
    T.op("act", lambda e: e.activation(out=c12[:, 0, :], in_=c12[:, 0, :], func=AF.Ln, bias=1.0), CB, CB)
    T.op("dve", lambda e: e.tensor_scalar(out=c12[:, 1, :], in0=c12[:, 0, :], scalar1=-16.0, scalar2=None, op0=ALU.mult), CB, CB)
    T.op("dve", lambda e: e.tensor_scalar(out=c12[:, 0, :], in0=c12[:, 0, :], scalar1=-8.0, scalar2=None, op0=ALU.mult), CB, CB)
    for n in range(16):
        T.op("dve", lambda e, n=n: e.memset(HL[:, n, 0, :], 0.0), (), (HL_b[n],))
        T.op("dve", lambda e, n=n: e.memset(HST[:, n, 0:1], 0.0), (), (HST_b[n],))

    def tile_info(kind, ti):
        if kind == "pre":
            return dict(x=xpre, r0=ti * 512, NT=512, cs=cs_pre, qc=None, kc=ti * 512, ko=None, vo=None)
        if kind == "main":
            return dict(x=xm, r0=ti * 512, NT=512, cs=cs_main, qc=ti * 512, kc=NPRE + ti * 512, ko=k_main, vo=v_main)
        return dict(x=xs, r0=0, NT=256, cs=cs_smp, qc=NMAIN, kc=NPRE + NMAIN, ko=k_smp, vo=v_smp)

    xloads = []
    for kind, ti in s1_tiles:
        inf = tile_info(kind, ti)
        for g in range(inf["NT"] // 128):
            xloads.append((inf["x"], inf["r0"] + g * 128))
    xl = {"i": 0}

    def x_prefetch(upto):
        while xl["i"] <= min(upto, len(xloads) - 1):
            i = xl["i"]
            src, r = xloads[i]
            T.dma("sp", xbuf[i % 2][:], src[r:r + 128, :], (), (xbuf_b[i % 2],), xbuf_ds[i % 2])
            xl["i"] += 1

    def rstd_from_ss(ss_ap, n, bufs):
        T.op("dve", lambda e: e.tensor_scalar(out=ss_ap, in0=ss_ap, scalar1=1.0 / n, scalar2=EPS,
                                              op0=ALU.mult, op1=ALU.add), bufs, bufs)
        T.op("act", lambda e: e.activation(out=ss_ap, in_=ss_ap, func=AF.Sqrt), bufs, bufs)
        T.op("dve", lambda e: e.reciprocal(out=ss_ap, in_=ss_ap), bufs, bufs)

    def norm_transpose(src_tile, src_b, dstT, dst_b, g, gcol0, ptxs, ptxs_b, junk, junk_b, ssq_ap, ssq_buf):
        T.op("act", lambda e: e.activation(out=junk[:], in_=src_tile, func=AF.Square, accum_out=ssq_ap),
             (src_b,), (junk_b, ssq_buf))
        rstd_from_ss(ssq_ap, float(D), (ssq_buf,))
        T.op("dve", lambda e: e.tensor_scalar_mul(out=junk[:], in0=src_tile, scalar1=ssq_ap),
             (src_b, ssq_buf), (junk_b,))
        for q4 in range(4):
            pt = ptxs[q4 % 2]
            ptb = ptxs_b[q4 % 2]
            for j in range(8):
                kc = q4 * 8 + j
                T.op("pe", lambda e, kc=kc, j=j, pt=pt: e.transpose(out=pt[:, j * 128:(j + 1) * 128],
                                                                    in_=junk[:, kc * 128:(kc + 1) * 128], identity=ident[:]),
                     (junk_b, B_const), (ptb,))
            gsl = gvec[:, gcol0 + q4 * 8:gcol0 + q4 * 8 + 8].unsqueeze(2).to_broadcast([128, 8, 128])
            T.op("dve", lambda e, pt=pt, q4=q4, gsl=gsl: e.tensor_tensor(
                out=dstT[:, q4 * 8:(q4 + 1) * 8, g * 128:(g + 1) * 128],
                in0=pt[:].rearrange("p (k t) -> p k t", k=8), in1=gsl, op=ALU.mult),
                (ptb, B_const), (dst_b,))

    deferred = []

    def flush_deferred():
        while deferred:
            deferred.pop(0)()

    first_main = [True]
    gcount = 0
    qkT_i = 0
    alt = {"qk": 0, "v": 0, "y": 0}
    dbg_tiles = int(os.environ.get("DBG_TILES", "99"))
    XR = int(os.environ.get("DBG_XR", "99"))
    dbg_blocks = int(os.environ.get("DBG_BLOCKS", "99"))
    for kind, ti in s1_tiles[:dbg_tiles]:
        inf = tile_info(kind, ti)
        NT = inf["NT"]
        NG = NT // 128
        nseg = 4 if kind == "smp" else 1
        Lseg = NT // nseg
        blocks = (IN_PRE if kind == "pre" else IN_MAIN)[:dbg_blocks]
        if kind == "main" and first_main[0]:
            first_main[0] = False
            for n in range(16):
                T.op("dve", lambda e, n=n: e.tensor_scalar_mul(out=HL[:, n, 0, :], in0=HL[:, n, 0, :], scalar1=VALID),
                     (HL_b[n], B_const), (HL_b[n],))
                T.op("dve", lambda e, n=n: e.tensor_scalar_mul(out=HST[:, n, 0:1], in0=HST[:, n, 0:1], scalar1=VALID),
                     (HST_b[n], B_const), (HST_b[n],))
        if kind == "smp":
            for n in range(16):
                T.op("dve", lambda e, n=n: e.tensor_copy(out=stout[:, n, 0:3], in_=HL[:, n, 0, :]), (HL_b[n],), (stout_b,))
                T.op("dve", lambda e, n=n: e.tensor_copy(out=stout[:, n, 3:4], in_=HST[:, n, 0:1]), (HST_b[n],), (stout_b,))
            T.dma("sp", st_main[:, :], stout[:].rearrange("p c k -> p (c k)"), (stout_b,), (), stout_ds)
            for n in range(16):
                T.op("dve", lambda e, n=n: e.tensor_copy(
                    out=HL[:, n, :, :], in_=constT[:, n, 8:20].rearrange("p (s j) -> p s j", s=4)), CB, (HL_b[n],))
                T.op("dve", lambda e, n=n: e.tensor_copy(out=HST[:, n, :], in_=constT[:, n, 20:24]), CB, (HST_b[n],))
        T.dma("sp", cst[:, 0:NG, :], inf["cs"][inf["r0"]:inf["r0"] + NT, :].rearrange("(g p) c -> p g c", p=128),
              (), (cst_b,), cst_ds)
        for g in range(NG):
            x_prefetch(gcount + 1)
            xb = xbuf[gcount % 2]
            norm_transpose(xb[:], xbuf_b[gcount % 2], xnT, xnT_b[g], g, 0, ptx, ptx_b, xsb, xsb_b,
                           ssq[:, g:g + 1], ssq_b)
            gcount += 1
        for (typ, bi) in blocks:
            col0 = COL0[typ] + bi * 512
            for kh in range(2):
                slot, slb = w_next(("in", col0, kh))
                sl3 = slot[:].rearrange("p (kc c) -> p kc c", kc=16)
                if typ in ("q", "k", "v"):
                    for g in range(NG):
                        for kc in range(16):
                            T.op("pe", lambda e, g=g, kc=kc, kh=kh, sl3=sl3: e.matmul(
                                pp[g][:, :], lhsT=xnT[:, kh * 16 + kc, g * 128:(g + 1) * 128], rhs=sl3[:, kc, :],
                                start=(kh == 0 and kc == 0), stop=(kh == 1 and kc == 15)),
                                (xnT_b[g], slb), (pp_b[g],))
                else:
                    for cc in range(4):
                        for kc in range(16):
                            T.op("pe", lambda e, cc=cc, kc=kc, kh=kh, sl3=sl3: e.matmul(
                                pp[cc][:, 0:NT], lhsT=sl3[:, kc, cc * 128:(cc + 1) * 128], rhs=xnT[:, kh * 16 + kc, 0:NT],
                                start=(kh == 0 and kc == 0), stop=(kh == 1 and kc == 15)),
                                tuple(xnT_b[:NG]) + (slb,), (pp_b[cc],))
            if typ in ("q", "k"):
                sub0 = (bi * 4) if typ == "q" else (16 + bi * 4)
                for g in range(NG):
                    T.op("act", lambda e, g=g: e.activation(out=qk32[g][:], in_=pp[g][:, :], func=AF.Copy),
                         (pp_b[g],), (qk32_b[g],))
                flush_deferred()
                for g in range(NG):
                    a_ = g
                    q32, q32b = qk32[g], qk32_b[g]
                    q3 = q32[:].rearrange("p (s d) -> p s d", s=4)
                    x1 = q3[:, :, 0:16]
                    x2 = q3[:, :, 16:32]
                    cos = cst[:, g, 0:64].rearrange("p (s d) -> p s d", s=4)
                    sin = cst[:, g, 64:128].rearrange("p (s d) -> p s d", s=4)
                    rb = (q32b, cst_b)
                    T.op("dve", lambda e, x1=x1, cos=cos: e.tensor_tensor(out=rtmp[:, 0], in0=x1, in1=cos, op=ALU.mult), rb, (rtmp_b,))
                    T.op("dve", lambda e, x2=x2, sin=sin: e.tensor_tensor(out=rtmp[:, 1], in0=x2, in1=sin, op=ALU.mult), rb, (rtmp_b,))
                    T.op("dve", lambda e, x2=x2, cos=cos: e.tensor_tensor(out=rtmp[:, 2], in0=x2, in1=cos, op=ALU.mult), rb, (rtmp_b,))
                    T.op("dve", lambda e, x1=x1, sin=sin: e.tensor_tensor(out=rtmp[:, 3], in0=x1, in1=sin, op=ALU.mult), rb, (rtmp_b,))
                    T.op("dve", lambda e, x1=x1: e.tensor_tensor(out=x1, in0=rtmp[:, 0], in1=rtmp[:, 1], op=ALU.subtract),
                         (rtmp_b,), (q32b,))
                    T.op("dve", lambda e, x2=x2: e.tensor_tensor(out=x2, in0=rtmp[:, 2], in1=rtmp[:, 3], op=ALU.add),
                         (rtmp_b,), (q32b,))
                    if typ == "k" and inf["ko"] is not None:
                        r0 = inf["r0"] + g * 128
                        T.dma("sp", inf["ko"][r0:r0 + 128, bi * 512:(bi + 1) * 512], q32[:], (q32b,), (), qk32_ds[a_])
                    qb, qbb = qkbf[g], qkbf_b[g]
                    T.op("act", lambda e, q32=q32, qb=qb: e.activation(out=qb[:], in_=q32[:], func=AF.Copy), (q32b,), (qbb,))

                    def tr(g=g, qb=qb, qbb=qbb, sub0=sub0, typ=typ, bi=bi, last=(typ == "k" and bi == 1)):
                        pt, ptb = ptx[g % 2], ptx_b[g % 2]
                        for s in range(4):
                            T.op("pe", lambda e, s=s: e.transpose(out=pt[:, s * 128:(s + 1) * 128],
                                                                  in_=qb[:, s * 128:(s + 1) * 128], identity=ident[:]),
                                 (qbb, B_const), (ptb,))
                        st_t, st_b = qkT_st[g], qkT_b[g]
                        T.op("act", lambda e: e.activation(out=st_t[:, :, :],
                                                           in_=pt[:, 0:512].rearrange("p (s t) -> p s t", s=4), func=AF.Copy),
                             (ptb,), (st_b,))
                        if typ == "q":
                            c0 = inf["qc"] + g * 128
                            T.dma("sp", qT_s[sub0:sub0 + 4, :, c0:c0 + 128].rearrange("s d t -> d s t"), st_t[:, :, :],
                                  (st_b,), (), qkT_ds[g])
                        else:
                            c0 = inf["kc"] + g * 128
                            T.dma("sp", kT_s[sub0 - 16:sub0 - 12, :, c0:c0 + 128].rearrange("s d t -> d s t"), st_t[:, :, :],
                                  (st_b,), (), qkT_ds[g])
                    deferred.append(tr)
            elif typ == "v":
                for g in range(NG):
                    T.op("act", lambda e, g=g: e.activation(out=v32[g][:], in_=pp[g][:, :], func=AF.Copy),
                         (pp_b[g],), (v32_b[g],))
                flush_deferred()
                for g in range(NG):
                    a_ = g
                    T.op("dve", lambda e, g=g, a_=a_: e.tensor_copy(out=qkbf[g][:], in_=v32[a_][:]),
                         (v32_b[a_],), (qkbf_b[g],))
                    if inf["vo"] is not None:
                        r0 = inf["r0"] + g * 128
                        T.dma("sp", inf["vo"][r0:r0 + 128, bi * 512:(bi + 1) * 512], v32[a_][:], (v32_b[a_],), (), v32_ds[a_])
                    r0 = inf["kc"] + g * 128
                    T.dma("sp", v_s[r0:r0 + 128, bi * 512:(bi + 1) * 512], qkbf[g][:], (qkbf_b[g],), (), vbf_ds[g])
            elif typ == "xr":
                W = 3 + Lseg

                def v3(t):
                    return t[:, 0:NT].rearrange("p (s l) -> p s l", s=nseg)

                for cc in range(4):
                    xh3 = xh4[:, cc, 0:nseg * W].rearrange("p (s w) -> p s w", s=nseg)
                    T.op("act", lambda e, cc=cc, xh3=xh3: e.activation(out=xh3[:, :, 3:W], in_=v3(pp[cc]), func=AF.Copy),
                         (pp_b[cc],), (xh4_b[cc],))
                flush_deferred()
                for cc in range(4):
                    n = bi * 4 + cc
                    xh3 = xh4[:, cc, 0:nseg * W].rearrange("p (s w) -> p s w", s=nseg)
                    xh_b = xh4_b[cc]
                    if XR < 1:
                        continue
                    T.op("dve", lambda e, n=n: e.tensor_copy(out=xh3[:, :, 0:3], in_=HL[:, n, 0:nseg, :]), (HL_b[n],), (xh_b,))
                    T.op("dve", lambda e, n=n: e.tensor_copy(out=HL[:, n, 0:nseg, :], in_=xh3[:, :, W - 3:W]), (xh_b,), (HL_b[n],))
                    if kind == "smp":
                        T.op("dve", lambda e, n=n: e.tensor_copy(out=stout_s[:, n, :, 0:3], in_=xh3[:, :, W - 3:W]),
                             (xh_b,), (stout_b,))
                    if XR < 2:
                        continue
                    cw = constT[:, n, :]
                    xc, xc_b, xcb, xcb_b = xc4[:, cc, :], xc4_b[cc], xcb4[:, cc, :], xcb4_b[cc]
                    T.op("dve", lambda e, cw=cw, xc=xc: e.tensor_scalar(out=v3(xc), in0=xh3[:, :, 3:W], scalar1=cw[:, 3:4],
                                                                 scalar2=cw[:, 4:5], op0=ALU.mult, op1=ALU.add),
                         (xh_b, B_const), (xc_b,))
                    for j in range(3):
                        T.op("dve", lambda e, cw=cw, j=j, xc=xc: e.scalar_tensor_tensor(
                            out=v3(xc), in0=xh3[:, :, j:j + Lseg], scalar=cw[:, j:j + 1], in1=v3(xc),
                            op0=ALU.mult, op1=ALU.add), (xh_b, xc_b, B_const), (xc_b,))
                    T.op("act", lambda e, xc=xc, xcb=xcb: e.activation(out=xcb[:, 0:NT], in_=xc[:, 0:NT], func=AF.Copy), (xc_b,), (xcb_b,))

                    if XR < 3:
                        continue
                    def gates(n=n, cc=cc, cw=cw, xc=xc, xc_b=xc_b, xcb=xcb, xcb_b=xcb_b):
                        T.op("pe", lambda e: e.matmul(pg[0][:, 0:NT], lhsT=gaw[:, n, :], rhs=xcb[:, 0:NT], start=True, stop=True),
                             (xcb_b, B_const), (pg_b[0],))
                        T.op("pe", lambda e: e.matmul(pg[1][:, 0:NT], lhsT=gxw[:, n, :], rhs=xcb[:, 0:NT], start=True, stop=True),
                             (xcb_b, B_const), (pg_b[1],))
                        T.op("act", lambda e: e.activation(out=rr[:, 0:NT], in_=pg[0][:, 0:NT], func=AF.Sigmoid, bias=cw[:, 5:6]),
                             (pg_b[0], B_const), (rr_b,))
                        T.op("act", lambda e: e.activation(out=ii[:, 0:NT], in_=pg[1][:, 0:NT], func=AF.Sigmoid, bias=cw[:, 6:7]),
                             (pg_b[1], B_const), (ii_b,))
                        if XR < 4:
                            return
                        T.op("act", lambda e: e.activation(out=aa[:, 0:NT], in_=rr[:, 0:NT], func=AF.Exp, scale=c12[:, 0, n:n + 1]),
                             (rr_b, B_const), (aa_b,))
                        T.op("act", lambda e: e.activation(out=mm_[:, 0:NT], in_=rr[:, 0:NT], func=AF.Exp, scale=c12[:, 1, n:n + 1]),
                             (rr_b, B_const), (mm_b,))
                        T.op("act", lambda e: e.activation(out=mm_[:, 0:NT], in_=mm_[:, 0:NT], func=AF.Sqrt, bias=1.0, scale=-1.0),
                             (mm_b,), (mm_b,))
                        T.op("dve", lambda e: e.tensor_tensor(out=uu[:, 0:NT], in0=ii[:, 0:NT], in1=xc[:, 0:NT], op=ALU.mult),
                             (ii_b, xc_b), (uu_b,))
                        T.op("dve", lambda e: e.tensor_tensor(out=uu[:, 0:NT], in0=uu[:, 0:NT], in1=mm_[:, 0:NT], op=ALU.mult),
                             (uu_b, mm_b), (uu_b,))
                        if XR < 5:
                            return
                        for s in range(nseg):
                            T.op("dve", lambda e, s=s: e.tensor_tensor_scan(
                                out=hs4[:, cc, s * Lseg:(s + 1) * Lseg], data0=aa[:, s * Lseg:(s + 1) * Lseg],
                                data1=uu[:, s * Lseg:(s + 1) * Lseg], initial=HST[:, n, s:s + 1], op0=ALU.mult, op1=ALU.add),
                                (aa_b, uu_b, HST_b[n]), (hs4_b[cc],))
                        if XR < 6:
                            return
                        T.op("dve", lambda e: e.tensor_copy(
                            out=HST[:, n, 0:nseg],
                            in_=hs4[:, cc, 0:NT].rearrange("p (s l) -> p s l", s=nseg)[:, :, Lseg - 1]),
                            (hs4_b[cc],), (HST_b[n],))
                        if kind == "smp":
                            T.op("dve", lambda e: e.tensor_copy(out=stout_s[:, n, :, 3], in_=HST[:, n, :]), (HST_b[n],), (stout_b,))
                    deferred.append(gates)
            else:
                for cc in range(4):
                    T.op("act", lambda e, cc=cc: e.activation(out=gr4[:, cc, 0:NT], in_=pp[cc][:, 0:NT], func=AF.Copy),
                         (pp_b[cc],), (gr4_b[cc],))
                flush_deferred()
                for cc in range(4):
                    n = bi * 4 + cc
                    gr32, gr32_b = gr4[:, cc, :], gr4_b[cc]
                    T.op("act", lambda e: e.activation(out=g1[:, 0:NT], in_=gr32[:, 0:NT], func=AF.Square), (gr32_b,), (g1_b,))
                    T.op("dve", lambda e: e.tensor_scalar(out=g1[:, 0:NT], in0=g1[:, 0:NT], scalar1=0.044715, scalar2=1.0,
                                                          op0=ALU.mult, op1=ALU.add), (g1_b,), (g1_b,))
                    T.op("dve", lambda e: e.tensor_tensor(out=g1[:, 0:NT], in0=g1[:, 0:NT], in1=gr32[:, 0:NT], op=ALU.mult),
                         (g1_b, gr32_b), (g1_b,))
                    T.op("act", lambda e: e.activation(out=g2[:, 0:NT], in_=g1[:, 0:NT], func=AF.Sigmoid, scale=GELU_C),
                         (g1_b,), (g2_b,))
                    T.op("dve", lambda e: e.tensor_tensor(out=g2[:, 0:NT], in0=g2[:, 0:NT], in1=gr32[:, 0:NT], op=ALU.mult),
                         (g2_b, gr32_b), (g2_b,))

                    def ymul(n=n, cc=cc):
                        a_ = alt["y"] % 2
                        alt["y"] += 1
                        T.op("dve", lambda e: e.tensor_tensor(out=ylb[a_][:, 0:NT], in0=g2[:, 0:NT], in1=hs4[:, cc, 0:NT], op=ALU.mult),
                             (g2_b, hs4_b[cc]), (ylb_b[a_],))
                        T.dma("sp", mixT_s[16 + n, :, inf["qc"]:inf["qc"] + NT], ylb[a_][:, 0:NT], (ylb_b[a_],), (), ylb_ds[a_])
                    flush_deferred()
                    ymul()
        flush_deferred()
    T.dma("sp", st_smp[:, :], stout_s[:].rearrange("p c s k -> p (c s k)"), (stout_b,), (), stout_ds)
    T.barrier()
    with nc.Block() as block:
        T.replay(block)
    st.close()
    if upto == 1:
        top.close()
        return nc

    st = ExitStack()
    kTt = [sb(st, "kTt%d" % i, [128, 2, 4096], BF16) for i in range(2)]
    vat = [sb(st, "vat%d" % i, [128, 32, 258], BF16) for i in range(2)]
    qTt = [sb(st, "qTt%d" % i, [128, 4, 2048], BF16) for i in range(2)]
    kvq_b = [(Buf(), Buf(), Buf()) for _ in range(2)]
    kvq_ds = [DS() for _ in range(2)]
    kvq_dsp = [DS() for _ in range(2)]
    kcb = sb(st, "kcb", [128, 16, 256], BF16)
    kcb_b = Buf()
    kcb_ds = DS()
    pT = [sb(st, "pT%d" % i, [128, 2, 256], BF16) for i in range(3)]
    pT_b = [Buf() for _ in range(3)]
    o32 = sb(st, "o32", [128, 4, 260], F32)
    o32_b = Buf()
    d32 = sb(st, "d32", [128, 2, 256], F32)
    d32_b = Buf()
    rs = sb(st, "rs", [128, 8], F32)
    rs_b = Buf()
    onb = sb(st, "onb", [128, 2, 256], BF16)
    onb_b = Buf()
    junk2 = sb(st, "junk2", [128, 256], F32)
    junk2_b = Buf()
    oT_st = [sb(st, "oTst%d" % i, [128, 2, 256], BF16) for i in range(2)]
    oT_b = [Buf() for _ in range(2)]
    oT_ds = [DS() for _ in range(2)]
    psc = [ps(st, "psc%d" % i, [128, 512], F32) for i in range(3)]
    psc_b = [Buf() for _ in range(3)]
    po = [ps(st, "po%d" % i, [128, 512], F32) for i in range(4)]
    po_b = [Buf() for _ in range(4)]
    pt2_0 = ps(st, "pt2_0", [128, 1024], BF16)
    pt2 = [pt2_0, pt2_0]
    pt2_0b = Buf()
    pt2_b = [pt2_0b, pt2_0b]
    for i in range(2):
        T.op("dve", lambda e, i=i: e.memset(vat[i][:, :, 256:258], 1.0), (), (kvq_b[i][1],))
    acount = {"o": 0, "blk": 0}

    def attn_block(kT_of, v_of, q_of, bias_of, nblocks, band0, NS, kbuf, qbuf, out_store):
        QW = NS * 128
        items = []
        for j in range(nblocks):
            spj = (j - band0) if band0 is not None else -1
            col0 = 128 * spj if spj > 0 else 0
            items.append((j, spj, col0))

        def qk(it):
            j, spj, col0 = it
            pb = acount["blk"] % 3
            for c in range(2):
                kt = kT_of(c, j)
                nk = kt.shape[1]
                T.op("pe", lambda e, c=c, kt=kt, nk=nk, col0=col0, pb=pb: e.matmul(
                    psc[pb][0:nk, c * 256 + col0:c * 256 + QW], lhsT=kt, rhs=q_of(c, col0), start=True, stop=True),
                    tuple(kbuf) + (qbuf,), (psc_b[pb],))
            nk = kT_of(0, j).shape[1]
            bj = bias_of(j)
            T.op("act", lambda e, pb=pb, nk=nk, col0=col0, bj=bj: e.activation(
                out=pT[pb][0:nk, :, col0:QW],
                in_=psc[pb][0:nk, :].rearrange("p (c t) -> p c t", c=2)[:, :, col0:QW],
                func=AF.Exp, scale=SCALE, bias=bj), (psc_b[pb], B_const), (pT_b[pb],))
            if spj >= 0:
                T.op("dve", lambda e, pb=pb, spj=spj: e.memset(pT[pb][64:128, :, spj * 128:spj * 128 + 64], 0.0),
                     (), (pT_b[pb],))
            acount["blk"] += 1
            return pb

        def pv(it, pb):
            j, spj, col0 = it
            vv, nk = v_of(j)
            for c in range(2):
                for s in range(max(spj, 0), NS):
                    last = (band0 + s) if band0 is not None else nblocks - 1
                    T.op("pe", lambda e, c=c, s=s, vv=vv, nk=nk, pb=pb, j=j, last=last: e.matmul(
                        po[c * 2 + s][:, 0:257], lhsT=pT[pb][0:nk, c, s * 128:(s + 1) * 128], rhs=vv,
                        start=(j == 0), stop=(j == last)), (pT_b[pb],) + tuple(kbuf), (po_b[c * 2 + s],))

        pend_pv = []
        for it in items:
            pb = qk(it)
            pend_pv.append((it, pb))
            if len(pend_pv) > 2:
                pv(*pend_pv.pop(0))
        while pend_pv:
            pv(*pend_pv.pop(0))
        for c in range(2):
            for s in range(NS):
                i4 = c * 2 + s
                if i4 % 2 == 0:
                    T.op("act", lambda e, i4=i4: e.activation(out=o32[:, i4, 0:257], in_=po[i4][:, 0:257], func=AF.Copy),
                         (po_b[i4],), (o32_b,))
                else:
                    T.op("dve", lambda e, i4=i4: e.tensor_copy(out=o32[:, i4, 0:257], in_=po[i4][:, 0:257]),
                         (po_b[i4],), (o32_b,))
        ob = (o32_b, rs_b)
        for s in range(NS):
            T.op("dve", lambda e, s=s: e.reciprocal(out=rs[:, 0:1], in_=o32[:, s, 256:257]), (o32_b,), (rs_b,))
            T.op("dve", lambda e, s=s: e.reciprocal(out=rs[:, 1:2], in_=o32[:, 2 + s, 256:257]), (o32_b,), (rs_b,))
            T.op("dve", lambda e: e.tensor_tensor(out=rs[:, 1:2], in0=rs[:, 1:2], in1=NLAM, op=ALU.mult), (rs_b, B_const), (rs_b,))
            T.op("dve", lambda e, s=s: e.tensor_scalar_mul(out=d32[:, s, :], in0=o32[:, s, 0:256], scalar1=rs[:, 0:1]), ob, (d32_b,))
            T.op("dve", lambda e, s=s: e.scalar_tensor_tensor(out=d32[:, s, :], in0=o32[:, 2 + s, 0:256], scalar=rs[:, 1:2],
                                                              in1=d32[:, s, :], op0=ALU.mult, op1=ALU.add),
                 (o32_b, rs_b, d32_b), (d32_b,))
            T.op("act", lambda e, s=s: e.activation(out=junk2[:], in_=d32[:, s, :], func=AF.Square, accum_out=rs[:, 2:3]),
                 (d32_b,), (junk2_b, rs_b))
            rstd_from_ss(rs[:, 2:3], 256.0, (rs_b,))
            T.op("dve", lambda e, s=s: e.scalar_tensor_tensor(out=onb[:, s, :], in0=d32[:, s, :], scalar=rs[:, 2:3], in1=subg[:],
                                                              op0=ALU.mult, op1=ALU.mult), (d32_b, rs_b, B_const), (onb_b,))
        oi = acount["o"] % 2
        acount["o"] += 1
        for s in range(NS):
            for eh in range(2):
                T.op("pe", lambda e, s=s, eh=eh: e.transpose(out=pt2[oi][:, (eh * 2 + s) * 128:(eh * 2 + s + 1) * 128],
                                                             in_=onb[:, s, eh * 128:(eh + 1) * 128], identity=ident[:]),
                     (onb_b, B_const), (pt2_b[oi],))
        T.op("act", lambda e: e.activation(out=oT_st[oi][:, :, 0:QW],
                                           in_=pt2[oi][:, 0:512].rearrange("p (a t) -> p a t", a=2)[:, :, 0:QW], func=AF.Copy),
             (pt2_b[oi],), (oT_b[oi],))
        out_store(oT_st[oi], oT_b[oi], oT_ds[oi])

    for h in range(4):
        ab = h % 2
        kb, vb, qb = kvq_b[ab]
        T.dma("sp", kTt[ab][:], kT_s[h * 2:h * 2 + 2, :, 0:4096].rearrange("c d t -> d c t"), (), (kb,), kvq_ds[ab])
        T.dma("sp", vat[ab][:, :, 0:256], v_s[0:4096, h * 256:(h + 1) * 256].rearrange("(g p) e -> p g e", p=128),
              (), (vb,), kvq_ds[ab])
        T.dma("sp", qTt[ab][:], qT_s[h * 4:h * 4 + 4, :, 0:2048].rearrange("s d t -> d s t"), (), (qb,), kvq_ds[ab])
        kvb = Buf()
        for g in range(2):
            for Q in range(8):
                A = NPRE + Q * 256
                nfull = A // 128

                def kT_of(c, j, ab=ab):
                    return kTt[ab][:, c, j * 128:(j + 1) * 128]

                def v_of(j, ab=ab):
                    return vat[ab][:, j, 0:257], 128

                def q_of(c, col0, ab=ab, g=g, Q=Q):
                    return qTt[ab][:, g * 2 + c, Q * 256 + col0:(Q + 1) * 256]

                def bias_of(j):
                    return PREB if j < 16 else 0.0

                def out_store(ot, otb, ods, h=h, g=g, Q=Q):
                    ch = (h * 2 + g) * 2
                    T.dma("sp", mixT_s[ch:ch + 2, :, Q * 256:(Q + 1) * 256].rearrange("a d t -> d a t"), ot[:, :, :],
                          (otb,), (), ods)

                attn_block(kT_of, v_of, q_of, bias_of, nfull + 2, nfull, 2, (kb, vb), qb, out_store)
    T.barrier()
    with nc.Block() as block:
        T.replay(block)

    kTs = kTt
    for sq in range(4):
        for h in range(4):
            ab = (sq * 4 + h) % 2
            kb, vb, qb = kvq_b[ab]
            T.dma("pool", kcb[:], ck[sq, :, h * 256:(h + 1) * 256].rearrange("(g p) e -> p g e", p=128), (), (kcb_b,), kcb_ds)
            for gg in range(16):
                for c in range(2):
                    idx = gg * 2 + c
                    pti = (idx // 8) % 2
                    T.op("pe", lambda e, gg=gg, c=c, idx=idx, pti=pti: e.transpose(
                        out=pt2[pti][:, (idx % 8) * 128:(idx % 8 + 1) * 128], in_=kcb[:, gg, c * 128:(c + 1) * 128],
                        identity=ident[:]), (kcb_b, B_const), (pt2_b[pti],))
                if gg % 4 == 3:
                    pti = ((gg * 2) // 8) % 2
                    g0 = gg - 3
                    T.op("dve", lambda e, pti=pti, g0=g0, ab=ab: e.tensor_copy(
                        out=kTs[ab][:, :, g0 * 128:(g0 + 4) * 128].rearrange("p c (g t) -> p g c t", g=4),
                        in_=pt2[pti][:].rearrange("p (g c t) -> p g c t", g=4, c=2)), (pt2_b[pti],), (kb,))
            T.dma("sp", kTs[ab][:, :, 2048:2112],
                  kT_s[h * 2:h * 2 + 2, :, NPRE + NMAIN + sq * 64:NPRE + NMAIN + sq * 64 + 64].rearrange("c d t -> d c t"),
                  (), (kb,), kvq_ds[ab])
            T.dma("pool", vat[ab][:, 0:16, 0:256], cv[sq, :, h * 256:(h + 1) * 256].rearrange("(g p) e -> p g e", p=128),
                  (), (vb,), kvq_dsp[ab])
            r0 = NPRE + NMAIN + sq * 64
            T.dma("sp", vat[ab][0:64, 16, 0:256], v_s[r0:r0 + 64, h * 256:(h + 1) * 256], (), (vb,), kvq_ds[ab])
            c0 = NMAIN + sq * 64
            for g_ in range(2):
                for c_ in range(2):
                    T.dma("sp", qTt[ab][:, c_, g_ * 64:(g_ + 1) * 64], qT_s[h * 4 + g_ * 2 + c_, :, c0:c0 + 64],
                          (), (qb,), kvq_ds[ab])

            def kT_of(c, j, ab=ab):
                return kTs[ab][:, c, j * 128:min((j + 1) * 128, 2112)]

            def v_of(j, ab=ab):
                nk = 128 if j < 16 else 64
                return vat[ab][0:nk, j, 0:257], nk

            def q_of(c, col0, ab=ab):
                return qTt[ab][:, c, 0:128]

            def bias_of(j):
                return 0.0

            def out_store(ot, otb, ods, h=h, sq=sq):
                for g in range(2):
                    ch = (h * 2 + g) * 2
                    c0 = NMAIN + sq * 64
                    T.dma("sp", mixT_s[ch:ch + 2, :, c0:c0 + 64].rearrange("a d t -> d a t"), ot[:, :, g * 64:(g + 1) * 64],
                          (otb,), (), ods)

            attn_block(kT_of, v_of, q_of, bias_of, 17, None, 1, (kb, vb), qb, out_store)
    T.barrier()
    with nc.Block() as block:
        T.replay(block)
    st.close()
    if upto == 2:
        top.close()
        return nc

    st = ExitStack()
    ws_begin(st, 4, plan3)
    hh = sb(st, "hh", [128, 4, D], F32)
    hh_b = [Buf() for _ in range(4)]
    hh_ds = [DS() for _ in range(4)]
    actT = sb(st, "actT", [128, 32, 512], BF16)
    actT_b = [Buf() for _ in range(4)]
    actT_ds = DS()
    hsb = sb(st, "hsb", [128, D], BF16)
    hsb_b = Buf()
    gfin = sb(st, "gfin", [128, D], F32)
    ssq3 = sb(st, "ssq3", [128, 8], F32)
    ssq3_b = Buf()
    z32 = [sb(st, "z32_%d" % i, [128, 512], F32) for i in range(2)]
    z32_b = [Buf() for _ in range(2)]
    zT = [sb(st, "zT%d" % i, [128, 4, 512], BF16) for i in range(2)]
    zT_b = [Buf() for _ in range(2)]
    ptx1 = ps(st, "ptx3", [128, 1024], BF16)
    ptx = [ptx1, ptx1]
    ptx1_b = Buf()
    ptx_b = [ptx1_b, ptx1_b]
    pp = [ps(st, "pp3_%d" % i, [128, 512], F32) for i in range(4)]
    pp_b = [Buf() for _ in range(4)]
    pd = [ps(st, "pd%d" % i, [128, 512], F32) for i in range(3)]
    pd_b = [Buf() for _ in range(3)]
    T.dma("sp", gfin[:], gfin_d[:, :], (), (B_const,), ds_const)

    pcount = 0
    for kind, ti in s3_tiles:
        if kind == "main":
            xsrc, r0, NT, c0, ydst = xm, ti * 512, 512, ti * 512, y_main
        else:
            xsrc, r0, NT, c0, ydst = xs, 0, 256, NMAIN, y_smp
        NG = NT // 128
        for g in range(NG):
            T.dma("sp", actT[:, :, g * 128:(g + 1) * 128],
                  mixT_s[:, :, c0 + g * 128:c0 + (g + 1) * 128].rearrange("k d t -> d k t"), (), (actT_b[g],), actT_ds)
            T.dma("sp", hh[:, g, :], xsrc[r0 + g * 128:r0 + (g + 1) * 128, :], (), (hh_b[g],), hh_ds[g])
        for cb in range(8):
            for kh in range(2):
                slot, slb = w_next(("out", cb * 512, kh))
                sl3 = slot[:].rearrange("p (kc c) -> p kc c", kc=16)
                for g in range(NG):
                    for kc in range(16):
                        T.op("pe", lambda e, g=g, kc=kc, kh=kh, sl3=sl3: e.matmul(
                            pp[g][:, :], lhsT=actT[:, kh * 16 + kc, g * 128:(g + 1) * 128], rhs=sl3[:, kc, :],
                            start=(kh == 0 and kc == 0), stop=(kh == 1 and kc == 15)), (actT_b[g], slb), (pp_b[g],))
            for g in range(NG):
                T.op("dve", lambda e, g=g, cb=cb: e.tensor_tensor(out=hh[:, g, cb * 512:(cb + 1) * 512],
                                                                  in0=pp[g][:, :], in1=hh[:, g, cb * 512:(cb + 1) * 512], op=ALU.add),
                     (pp_b[g], hh_b[g]), (hh_b[g],))
        for g in range(NG):
            norm_transpose(hh[:, g, :], hh_b[g], actT, actT_b[g], g, 32, ptx, ptx_b, hsb, hsb_b, ssq3[:, g:g + 1], ssq3_b)
        pend = []

        def down(sbk, zi):
            nonlocal pcount
            slot_a, sla_b = w_next(("dA", sbk))
            slot_bb, slbb_b = w_next(("dB", sbk))
            sd = [slot_a[:].rearrange("p (f c) -> p f c", f=2), slot_bb[:].rearrange("p (f c) -> p f c", f=2)]
            sdb = [sla_b, slbb_b]
            for g in range(NG):
                for cb in range(8):
                    bk = pcount % 3
                    pcount += 1
                    for f in range(4):
                        T.op("pe", lambda e, g=g, cb=cb, f=f, bk=bk: e.matmul(
                            pd[bk][:, :], lhsT=zT[zi][:, f, g * 128:(g + 1) * 128], rhs=sd[f // 2][:, f % 2, cb * 512:(cb + 1) * 512],
                            start=(f == 0), stop=(f == 3)), (zT_b[zi], sdb[f // 2]), (pd_b[bk],))
                    T.op("dve", lambda e, g=g, cb=cb, bk=bk: e.tensor_tensor(
                        out=hh[:, g, cb * 512:(cb + 1) * 512], in0=pd[bk][:, :], in1=hh[:, g, cb * 512:(cb + 1) * 512], op=ALU.add),
                        (pd_b[bk], hh_b[g]), (hh_b[g],))

        for sbk in range(NSB):
            zi = sbk % 2
            for hf, nm in ((0, "upA"), (1, "upB")):
                slot_u, slu_b = w_next((nm, sbk))
                su3 = slot_u[:].rearrange("p (kc c) -> p kc c", kc=16)
                for f in range(4):
                    for kc in range(16):
                        T.op("pe", lambda e, f=f, kc=kc, su3=su3, hf=hf: e.matmul(
                            pp[f][:, 0:NT], lhsT=su3[:, kc, f * 128:(f + 1) * 128], rhs=actT[:, hf * 16 + kc, 0:NT],
                            start=(hf == 0 and kc == 0), stop=(hf == 1 and kc == 15)), tuple(actT_b[:NG]) + (slu_b,), (pp_b[f],))
            for f in range(4):
                T.op("act", lambda e, f=f: e.activation(out=z32[f % 2][:, 0:NT], in_=pp[f][:, 0:NT], func=AF.Relu),
                     (pp_b[f],), (z32_b[f % 2],))
                T.op("act", lambda e, f=f, zi=zi: e.activation(out=zT[zi][:, f, 0:NT], in_=z32[f % 2][:, 0:NT], func=AF.Square),
                     (z32_b[f % 2],), (zT_b[zi],))
            if pend:
                down(*pend.pop(0))
            pend.append((sbk, zi))
        down(*pend.pop(0))
        for g in range(NG):
            T.op("act", lambda e, g=g: e.activation(out=hsb[:], in_=hh[:, g, :], func=AF.Square, accum_out=ssq3[:, 4 + g:5 + g]),
                 (hh_b[g],), (hsb_b, ssq3_b))
            rstd_from_ss(ssq3[:, 4 + g:5 + g], float(D), (ssq3_b,))
            for hf in range(2):
                T.op("dve", lambda e, g=g, hf=hf: e.scalar_tensor_tensor(
                    out=hh[:, g, hf * 2048:(hf + 1) * 2048], in0=hh[:, g, hf * 2048:(hf + 1) * 2048], scalar=ssq3[:, 4 + g:5 + g],
                    in1=gfin[:, hf * 2048:(hf + 1) * 2048], op0=ALU.mult, op1=ALU.mult), (hh_b[g], ssq3_b, B_const), (hh_b[g],))
            T.dma("sp", ydst[r0 + g * 128:r0 + (g + 1) * 128, :], hh[:, g, :], (hh_b[g],), (), hh_ds[g])
    T.barrier()
    with nc.Block() as block:
        T.replay(block)
    st.close()
    top.close()
    return nc


_NC_CACHE = {}


def _rope_table(pos):
    half = 16
    inv = (np.float32(500000.0) ** (-np.arange(half, dtype=np.float32) / np.float32(half))).astype(np.float32)
    ang = pos.astype(np.float32)[:, None] * inv[None, :]
    cos = np.cos(ang).astype(np.float32)
    sin = np.sin(ang).astype(np.float32)
    return np.ascontiguousarray(np.concatenate([np.tile(cos, (1, 4)), np.tile(sin, (1, 4))], axis=1))


def kernel(x_prompt, x_sample, cache_k, cache_v, state_conv, state_lru,
           norm_mix, w_in, conv_w, conv_b, gate_a_w, gate_a_b, gate_x_w, gate_x_b,
           lru_lambda, lambda_q1, lambda_k1, lambda_q2, lambda_k2, subln_g,
           w_out, norm_mlp, w_up, w_down, norm_final):
    f32 = np.float32
    A = lambda a: np.ascontiguousarray(np.asarray(a, dtype=f32))
    x_prompt = A(x_prompt); x_sample = A(x_sample)
    cache_k = A(cache_k); cache_v = A(cache_v)
    state_conv = A(state_conv); state_lru = A(state_lru)
    upto = _NC_CACHE.get("upto", 3)
    if "nc" not in _NC_CACHE:
        _NC_CACHE["nc"] = build_program(upto)
    nc = _NC_CACHE["nc"]

    gvec = np.concatenate([A(norm_mix)[0].reshape(32, 128).T, A(norm_mlp)[0].reshape(32, 128).T], axis=1)
    gfin = np.tile(A(norm_final).reshape(1, D), (128, 1))
    lamv = np.tile(np.concatenate([A(lambda_q1)[0], A(lambda_k1)[0], A(lambda_q2)[0], A(lambda_k2)[0]]).reshape(1, 512), (128, 1))
    subg = np.tile(A(subln_g)[0].reshape(1, 256), (128, 1))
    ident = np.eye(128, dtype=f32)
    shared = dict(
        gvec=A(gvec), gfin=A(gfin), lamv=A(lamv), subg=A(subg), ident=ident,
        w_in=A(w_in)[0],
        w_out=A(w_out)[0] if upto >= 3 else np.zeros((128, 128), f32),
        w_up=A(w_up)[0] if upto >= 3 else np.zeros((128, 128), f32),
        w_down=A(w_down)[0] if upto >= 3 else np.zeros((128, 128), f32),
        ga_w=A(gate_a_w)[0], gx_w=A(gate_x_w)[0],
        cs_pre=_rope_table(np.arange(NPRE)),
        cs_smp=_rope_table(np.tile(PAST + np.arange(64), 4)),
    )
    in_maps = []
    for c in range(8):
        b, half = c // 2, c % 2
        rows = np.concatenate([
            A(conv_w)[0], A(conv_b), A(gate_a_b)[0].reshape(1, 2048), A(gate_x_b)[0].reshape(1, 2048), A(lru_lambda),
            state_conv[0, 4 * c:4 * c + 4].reshape(12, 2048), state_lru[0, 4 * c:4 * c + 4]], axis=0)
        constT = rows.reshape(24, 16, 128).transpose(2, 1, 0).reshape(128, 16 * 24)
        m = dict(shared)
        m.update(
            xm=x_prompt[b, half * 2048:(half + 1) * 2048],
            xpre=x_prompt[b, 0:2048],
            xs=x_sample[4 * c:4 * c + 4].reshape(256, D),
            ck=cache_k[0, 4 * c:4 * c + 4].reshape(4, PAST, 1024),
            cv=cache_v[0, 4 * c:4 * c + 4].reshape(4, PAST, 1024),
            constT=A(constT),
            valid=np.full((128, 1), float(half), dtype=f32),
            cs_main=_rope_table(half * 2048 + np.arange(NMAIN)),
        )
        in_maps.append({k: np.ascontiguousarray(v) for k, v in m.items()})
    if os.environ.get("DBG_CORES"):
        res = run_bass_kernel_spmd(nc, [in_maps[1]], core_ids=[0])
        R = [res.results[0]] * 8
    else:
        res = run_bass_kernel_spmd(nc, in_maps, core_ids=list(range(8)))
        R = res.results

    y_prompt = np.empty((4, 4096, D), f32)
    k_prompt = np.empty((1, 4, 4096, 4, 256), f32)
    v_prompt = np.empty((1, 4, 4096, 4, 256), f32)
    conv_prompt = np.empty((1, 4, 3, 2048), f32)
    lru_prompt = np.empty((1, 4, 2048), f32)
    y_sample = np.empty((32, 64, D), f32)
    k_sample = np.empty((1, 32, 64, 4, 256), f32)
    v_sample = np.empty((1, 32, 64, 4, 256), f32)
    conv_sample = np.empty((1, 32, 3, 2048), f32)
    lru_sample = np.empty((1, 32, 2048), f32)
    for c in range(8):
        b, half = c // 2, c % 2
        r = R[c]
        sl = slice(half * 2048, (half + 1) * 2048)
        y_prompt[b, sl] = r["y_main"]
        k_prompt[0, b, sl] = r["k_main"].reshape(2048, 4, 256)
        v_prompt[0, b, sl] = r["v_main"].reshape(2048, 4, 256)
        if half == 1:
            stm = r["st_main"].reshape(128, 16, 4)
            full = stm.transpose(2, 1, 0).reshape(4, 2048)
            conv_prompt[0, b] = full[0:3]
            lru_prompt[0, b] = full[3]
        y_sample[4 * c:4 * c + 4] = r["y_smp"].reshape(4, 64, D)
        k_sample[0, 4 * c:4 * c + 4] = r["k_smp"].reshape(4, 64, 4, 256)
        v_sample[0, 4 * c:4 * c + 4] = r["v_smp"].reshape(4, 64, 4, 256)
        sts = r["st_smp"].reshape(128, 16, 4, 4)
        fulls = sts.transpose(2, 3, 1, 0).reshape(4, 4, 2048)
        conv_sample[0, 4 * c:4 * c + 4] = fulls[:, 0:3]
        lru_sample[0, 4 * c:4 * c + 4] = fulls[:, 3]
    return (y_prompt, y_sample, k_prompt, v_prompt, conv_prompt, lru_prompt,
            k_sample, v_sample, conv_sample, lru_sample)
```

```python
import math
import os
from contextlib import ExitStack

import numpy as np
import concourse.bass as bass
import concourse.mybir as mybir
from concourse.bass_utils import run_bass_kernel_spmd

F32 = mybir.dt.float32
BF16 = mybir.dt.bfloat16
AF = mybir.ActivationFunctionType
ALU = mybir.AluOpType

D = 4096
NPRE = 2048
NMAIN = 2048
NSMP = 256
NQ = NMAIN + NSMP
NK = NPRE + NMAIN + NSMP
PAST = 2048
EPS = 1e-6
LAM_INIT = 0.2
SCALE = 128 ** -0.5
NSLOT = 3
SAME_ENGINE_SYNC = True
GELU_C = 2.0 * math.sqrt(2.0 / math.pi)


class Buf:
    __slots__ = ("name", "w", "r")

    def __init__(self, name=""):
        self.name = name
        self.w = None
        self.r = {}


class DSem:
    def __init__(self, sem):
        self.sem = sem
        self.count = 0


class Ent:
    __slots__ = ("fn", "waits", "inc", "val", "dsem")

    def __init__(self, fn):
        self.fn = fn
        self.waits = []
        self.inc = False
        self.val = None
        self.dsem = None


class _Rec:
    def __getattr__(self, name):
        def f(*a, **kw):
            self.__dict__["call"] = (name, a, kw)
        return f


COMPUTE = ("pe", "dve", "act", "pool")
ENGS = ("pe", "dve", "act", "pool", "sp")


class Trk:
    def __init__(self, nc, csem):
        self.nc = nc
        self.csem = csem
        self.base = {e: 0 for e in COMPUTE}
        self.dsems = []
        self.epoch = 0
        self._reset()

    def _reset(self):
        self.streams = {e: [] for e in ENGS}
        self.waited = {e: {p: -1 for p in COMPUTE} for e in ENGS}
        self.waitedD = {e: {} for e in ENGS}

    def new_dsem(self, sem):
        d = DSem(sem)
        self.dsems.append(d)
        return d

    def _add_wait(self, eng, ent, ref):
        if ref is None or ref[0] != self.epoch:
            return
        if ref[1] == "c":
            _, _, peng, idx = ref
            if peng == eng and (eng in ("pe", "sp") or not SAME_ENGINE_SYNC):
                return
            if idx <= self.waited[eng][peng]:
                return
            self.waited[eng][peng] = idx
            self.streams[peng][idx].inc = True
            ent.waits.append(("c", peng, idx))
        else:
            ds = ref[2]
            cnt = ds.count
            if self.waitedD[eng].get(ds, 0) >= cnt:
                return
            self.waitedD[eng][ds] = cnt
            ent.waits.append(("d", ds, cnt))

    def _record(self, eng, ent, reads, writes, ref_maker):
        deps = []
        for b in reads:
            if b.w is not None:
                deps.append(b.w)
        for b in writes:
            if b.w is not None:
                deps.append(b.w)
            deps.extend(b.r.values())
        for d in deps:
            self._add_wait(eng, ent, d)
        idx = len(self.streams[eng])
        self.streams[eng].append(ent)
        ref = ref_maker(idx)
        key = ref[2] if ref[1] == "d" else eng
        for b in reads:
            b.r[key] = ref
        for b in writes:
            b.w = ref
            b.r = {}
        return ref

    def op(self, eng, fn, reads=(), writes=()):
        rec = _Rec()
        fn(rec)
        name, args, kw = rec.call
        ent = Ent(lambda e: getattr(e, name)(*args, **kw))
        return self._record(eng, ent, reads, writes,
                            lambda idx: (self.epoch, "c", eng, idx))

    def dma(self, eng, out, in_, reads, writes, dsem):
        ent = Ent(lambda e: e.dma_start(out=out, in_=in_))
        ent.dsem = dsem
        ref = self._record(eng, ent, reads, writes,
                           lambda idx: (self.epoch, "d", dsem))
        dsem.count += 16
        return ref

    def barrier(self):
        last = {}
        for e in COMPUTE:
            st = self.streams[e]
            for i in range(len(st) - 1, -1, -1):
                if st[i].dsem is None:
                    last[e] = i
                    break
        for eng in ENGS:
            ent = Ent(lambda e: e.nop())
            for p, i in last.items():
                if p != eng and i > self.waited[eng][p]:
                    self.streams[p][i].inc = True
                    ent.waits.append(("c", p, i))
            for ds in self.dsems:
                if ds.count > 0 and self.waitedD[eng].get(ds, 0) < ds.count:
                    ent.waits.append(("d", ds, ds.count))
            self.streams[eng].append(ent)
        self.epoch += 1

    def replay(self, block):
        nc = self.nc
        for e in COMPUTE:
            c = self.base[e]
            for ent in self.streams[e]:
                if ent.inc:
                    c += 1
                    ent.val = c
            self.base[e] = c
        streams = self.streams
        csem = self.csem

        def run(engname):
            def f(engobj):
                for ent in streams[engname]:
                    for w in ent.waits:
                        if w[0] == "c":
                            engobj.wait_ge(csem[w[1]], streams[w[1]][w[2]].val)
                        else:
                            engobj.wait_ge(w[1].sem, w[2])
                    ins = ent.fn(engobj)
                    if ent.dsem is not None:
                        ins.then_inc(ent.dsem.sem, 16)
                    elif ent.inc:
                        ins.then_inc(csem[engname], 1)
            return f

        block.tensor(run("pe"))
        block.vector(run("dve"))
        block.scalar(run("act"))
        block.gpsimd(run("pool"))
        block.sync(run("sp"))
        self._reset()


def build_program(upto=3):
    nc = bass.Bass("TRN2", target_bir_lowering=False)

    def din(name, shape, dt=F32):
        return nc.dram_tensor(name, list(shape), dt, kind="ExternalInput").ap()

    def dout(name, shape, dt=F32):
        return nc.dram_tensor(name, list(shape), dt, kind="ExternalOutput").ap()

    def dscr(name, shape, dt):
        return nc.dram_tensor(name, list(shape), dt).ap()

    xm = din("xm", [NMAIN, D])
    xpre = din("xpre", [NPRE, D])
    xs = din("xs", [NSMP, D])
    ck = din("ck", [4, PAST, 1024])
    cv = din("cv", [4, PAST, 1024])
    constT_d = din("constT", [128, 16 * 24])
    gvec_d = din("gvec", [128, 64])
    gfin_d = din("gfin", [128, D])
    lamv_d = din("lamv", [128, 512])
    subg_d = din("subg", [128, 256])
    valid_d = din("valid", [128, 1])
    ident_d = din("ident", [128, 128])
    cs_pre = din("cs_pre", [NPRE, 128])
    cs_main = din("cs_main", [NMAIN, 128])
    cs_smp = din("cs_smp", [NSMP, 128])
    w_in = din("w_in", [D, 8192])
    w_out = din("w_out", [D, D] if upto >= 3 else [128, 128])
    w_up = din("w_up", [D, 16384] if upto >= 3 else [128, 128])
    w_down = din("w_down", [16384, D] if upto >= 3 else [128, 128])
    ga_w = din("ga_w", [16, 128, 128])
    gx_w = din("gx_w", [16, 128, 128])

    y_main = dout("y_main", [NMAIN, D])
    y_smp = dout("y_smp", [NSMP, D])
    k_main = dout("k_main", [NMAIN, 1024])
    v_main = dout("v_main", [NMAIN, 1024])
    k_smp = dout("k_smp", [NSMP, 1024])
    v_smp = dout("v_smp", [NSMP, 1024])
    st_main = dout("st_main", [128, 64])
    st_smp = dout("st_smp", [128, 256])

    qT_s = dscr("qT_s", [16, 128, NQ], BF16)
    kT_s = dscr("kT_s", [8, 128, NK], BF16)
    v_s = dscr("v_s", [NK, 1024], BF16)
    mixT_s = dscr("mixT_s", [32, 128, NQ], BF16)

    top = ExitStack()
    sems = {e: top.enter_context(nc.semaphore("c_" + e)) for e in COMPUTE}
    T = Trk(nc, sems)
    nds = [0]

    def DS():
        nds[0] += 1
        return T.new_dsem(top.enter_context(nc.semaphore("d%d" % nds[0])))

    uniq = [0]

    def sb(stack, name, shape, dt):
        uniq[0] += 1
        return stack.enter_context(nc.sbuf_tensor("sb%d_%s" % (uniq[0], name), list(shape), dt))

    def ps(stack, name, shape, dt):
        uniq[0] += 1
        return stack.enter_context(nc.psum_tensor("ps%d_%s" % (uniq[0], name), list(shape), dt))

    MAXSLOT = 4
    slot_b = [Buf("slot%d" % i) for i in range(MAXSLOT)]
    slot_ds = [DS() for _ in range(MAXSLOT)]
    WS = {"slots": None, "n": 0, "plan": None, "issued": 0, "used": 0}

    def ws_begin(stack, n, plan_):
        WS["slots"] = [sb(stack, "wslot%d" % i, [128, 8192], BF16) for i in range(n)]
        WS["n"] = n
        WS["plan"] = plan_
        WS["issued"] = 0
        WS["used"] = 0
    ident = sb(top, "ident", [128, 128], BF16)
    gvec = sb(top, "gvec", [128, 64], F32)
    constT = sb(top, "constT", [128, 16, 24], F32)
    lamv = sb(top, "lamv", [128, 4, 128], F32)
    subg = sb(top, "subg", [128, 256], F32)
    sc = sb(top, "scal", [128, 16], F32)
    c12 = sb(top, "c12", [128, 2, 16], F32)
    B_const = Buf("consts")
    ds_const = DS()
    ds_const_p = DS()
    VALID = sc[:, 0:1]
    PREB = sc[:, 1:2]
    NLAM = sc[:, 3:4]

    IN_MAIN = ([("q", i) for i in range(4)] + [("k", i) for i in range(2)] + [("v", i) for i in range(2)]
               + [x for i in range(4) for x in (("xr", i), ("gr", i))])
    IN_PRE = [("k", 0), ("k", 1), ("v", 0), ("v", 1)] + [("xr", i) for i in range(4)]
    COL0 = {"q": 0, "k": 2048, "v": 3072, "xr": 4096, "gr": 6144}
    s1_tiles = ([("pre", i) for i in range(4)] + [("main", i) for i in range(4)] + [("smp", 0)])
    plan1 = []
    for kind, _ in s1_tiles:
        for (typ, bi) in (IN_PRE if kind == "pre" else IN_MAIN):
            for kh in range(2):
                plan1.append(("in", COL0[typ] + bi * 512, kh))
    s3_tiles = [("main", i) for i in range(4)] + [("smp", 0)]
    NSB = 32
    plan3 = []
    for _ in s3_tiles:
        for cb in range(8):
            for kh in range(2):
                plan3.append(("out", cb * 512, kh))
        for sbk in range(NSB):
            plan3.append(("upA", sbk))
            plan3.append(("upB", sbk))
            if sbk > 0:
                plan3.append(("dA", sbk - 1))
                plan3.append(("dB", sbk - 1))
        plan3.append(("dA", NSB - 1))
        plan3.append(("dB", NSB - 1))

    def w_issue(i):
        d = WS["plan"][i]
        s_ = i % WS["n"]
        slot = WS["slots"][s_]
        if d[0] == "in" or d[0] == "out":
            w = w_in if d[0] == "in" else w_out
            src = w[d[2] * 2048:(d[2] + 1) * 2048, d[1]:d[1] + 512].rearrange("(kc p) c -> p kc c", p=128)
            dst = slot[:].rearrange("p (kc c) -> p kc c", kc=16)
        elif d[0] in ("upA", "upB"):
            r0_ = 0 if d[0] == "upA" else 2048
            src = w_up[r0_:r0_ + 2048, d[1] * 512:(d[1] + 1) * 512].rearrange("(kc p) c -> p kc c", p=128)
            dst = slot[:].rearrange("p (kc c) -> p kc c", kc=16)
        else:
            r0_ = d[1] * 512 + (0 if d[0] == "dA" else 256)
            src = w_down[r0_:r0_ + 256, :].rearrange("(f p) c -> p f c", p=128)
            dst = slot[:].rearrange("p (f c) -> p f c", f=2)
        T.dma("pool", dst, src, (), (slot_b[s_],), slot_ds[s_])

    def w_next(desc):
        i = WS["used"]
        plan_ = WS["plan"]
        if os.environ.get("DBG_TILES") is None:
            assert plan_[i] == desc, (plan_[i], desc)
        else:
            plan_[i] = desc
        while WS["issued"] < min(len(plan_), i + 3):
            w_issue(WS["issued"])
            WS["issued"] += 1
        WS["used"] += 1
        s_ = i % WS["n"]
        return WS["slots"][s_], slot_b[s_]

    st = ExitStack()
    ws_begin(st, 3, plan1)
    xbuf = [sb(st, "xbuf%d" % i, [128, D], F32) for i in range(2)]
    xbuf_b = [Buf("xbuf%d" % i) for i in range(2)]
    xbuf_ds = [DS() for _ in range(2)]
    xsb = sb(st, "xsb", [128, D], BF16)
    xsb_b = Buf("xsb")
    xnT = sb(st, "xnT", [128, 32, 512], BF16)
    xnT_b = [Buf("xnT%d" % g) for g in range(4)]
    ssq = sb(st, "ssq", [128, 4], F32)
    ssq_b = Buf("ssq")
    cst = sb(st, "cst", [128, 4, 128], F32)
    cst_b = Buf("cst")
    cst_ds = DS()
    qk32 = [sb(st, "qk32_%d" % i, [128, 512], F32) for i in range(4)]
    qk32_b = [Buf() for _ in range(4)]
    qk32_ds = [DS() for _ in range(4)]
    rtmp = sb(st, "rtmp", [128, 4, 4, 16], F32)
    rtmp_b = Buf()
    qkbf = [sb(st, "qkbf%d" % i, [128, 512], BF16) for i in range(4)]
    qkbf_b = [Buf() for _ in range(4)]
    qkT_st = [sb(st, "qkTst%d" % i, [128, 4, 128], BF16) for i in range(4)]
    qkT_b = [Buf() for _ in range(4)]
    qkT_ds = [DS() for _ in range(4)]
    v32, v32_b, v32_ds = qk32, qk32_b, qk32_ds
    vbf_ds = [DS() for _ in range(4)]
    gaw = sb(st, "gaw", [128, 16, 128], BF16)
    gxw = sb(st, "gxw", [128, 16, 128], BF16)
    HL = sb(st, "HL", [128, 16, 4, 3], F32)
    HST = sb(st, "HST", [128, 16, 4], F32)
    HL_b = [Buf() for _ in range(16)]
    HST_b = [Buf() for _ in range(16)]
    stout = sb(st, "stout", [128, 16, 4], F32)
    stout_s = sb(st, "stout_s", [128, 16, 4, 4], F32)
    stout_b = Buf()
    stout_ds = DS()

    def L(name, shape, dt=F32):
        return sb(st, name, shape, dt), Buf(name)

    xh4 = sb(st, "xh4", [128, 4, 528], F32)
    xh4_b = [Buf() for _ in range(4)]
    xc4 = sb(st, "xc4", [128, 4, 512], F32)
    xc4_b = [Buf() for _ in range(4)]
    xcb4 = sb(st, "xcb4", [128, 4, 512], BF16)
    xcb4_b = [Buf() for _ in range(4)]
    rr, rr_b = L("rr", [128, 512])
    ii, ii_b = L("ii", [128, 512])
    aa, aa_b = L("aa", [128, 512])
    mm_, mm_b = L("mm", [128, 512])
    uu, uu_b = L("uu", [128, 512])
    hs4 = sb(st, "hs4", [128, 4, 512], F32)
    hs4_b = [Buf() for _ in range(4)]
    gr4 = sb(st, "gr4", [128, 4, 512], F32)
    gr4_b = [Buf() for _ in range(4)]
    g1, g1_b = ii, ii_b
    g2, g2_b = aa, aa_b
    ylb = [sb(st, "ylb%d" % i, [128, 512], BF16) for i in range(2)]
    ylb_b = [Buf() for _ in range(2)]
    ylb_ds = [DS() for _ in range(2)]

    ptx = [ps(st, "ptx%d" % i, [128, 1024], BF16) for i in range(2)]
    ptx_b = [Buf() for _ in range(2)]
    pp = [ps(st, "pp%d" % i, [128, 512], F32) for i in range(4)]
    pp_b = [Buf() for _ in range(4)]
    pg = [ps(st, "pg%d" % i, [128, 512], F32) for i in range(2)]
    pg_b = [Buf() for _ in range(2)]

    T.dma("sp", gvec[:], gvec_d[:, :], (), (B_const,), ds_const)
    T.dma("sp", constT[:].rearrange("p c k -> p (c k)"), constT_d[:, :], (), (B_const,), ds_const)
    T.dma("sp", lamv[:].rearrange("p a d -> p (a d)"), lamv_d[:, :], (), (B_const,), ds_const)
    T.dma("sp", subg[:], subg_d[:, :], (), (B_const,), ds_const)
    T.dma("sp", sc[:, 0:1], valid_d[:, :], (), (B_const,), ds_const)
    T.dma("pool", ident[:], ident_d[:, :], (), (B_const,), ds_const_p)
    T.dma("pool", gaw[:], ga_w.rearrange("n c d -> c n d"), (), (B_const,), ds_const_p)
    T.dma("pool", gxw[:], gx_w.rearrange("n c d -> c n d"), (), (B_const,), ds_const_p)
    CB = (B_const,)
    T.op("dve", lambda e: e.tensor_scalar(out=sc[:, 1:2], in0=sc[:, 0:1], scalar1=-1.0, scalar2=30000.0,
                                          op0=ALU.add, op1=ALU.mult), CB, CB)
    T.op("dve", lambda e: e.tensor_tensor(out=lamv[:, 0, :], in0=lamv[:, 0, :], in1=lamv[:, 1, :], op=ALU.mult), CB, CB)
    T.op("dve", lambda e: e.tensor_tensor(out=lamv[:, 2, :], in0=lamv[:, 2, :], in1=lamv[:, 3, :], op=ALU.mult), CB, CB)
    T.op("dve", lambda e: e.tensor_reduce(out=sc[:, 4:5], in_=lamv[:, 0, :], axis=mybir.AxisListType.X, op=ALU.add), CB, CB)
    T.op("dve", lambda e: e.tensor_reduce(out=sc[:, 5:6], in_=lamv[:, 2, :], axis=mybir.AxisListType.X, op=ALU.add), CB, CB)
    T.op("act", lambda e: e.activation(out=sc[:, 4:6], in_=sc[:, 4:6], func=AF.Exp), CB, CB)
    T.op("dve", lambda e: e.tensor_tensor(out=sc[:, 2:3], in0=sc[:, 4:5], in1=sc[:, 5:6], op=ALU.subtract), CB, CB)
    T.op("dve", lambda e: e.tensor_scalar(out=sc[:, 2:3], in0=sc[:, 2:3], scalar1=LAM_INIT, scalar2=None, op0=ALU.add), CB, CB)
    T.op("dve", lambda e: e.tensor_scalar(out=sc[:, 3:4], in0=sc[:, 2:3], scalar1=-1.0, scalar2=None, op0=ALU.mult), CB, CB)
    T.op("dve", lambda e: e.tensor_scalar(out=subg[:], in0=subg[:], scalar1=1.0 - LAM_INIT, scalar2=None, op0=ALU.mult), CB, CB)
    T.op("act", lambda e: e.activation(out=c12[:, 0, :], in_=constT[:, :, 7], func=AF.Exp, scale=-1.0), CB, CB)
    T.op("act", lambda e: e.activation(out=c12[:, 0, :], in_=c12[:, 0, :], func=AF.Ln, bias=1.0), CB, CB)
    T.op("dve", lambda e: e.tensor_scalar(out=c12[:, 1, :], in0=c12[:, 0, :], scalar1=-16.0, scalar2=None, op0=ALU.mult), CB, CB)
    T.op("dve", lambda e: e.tensor_scalar(out=c12[:, 0, :], in0=c12[:, 0, :], scalar1=-8.0, scalar2=None, op0=ALU.mult), CB, CB)
    for n in range(16):
        T.op("dve", lambda e, n=n: e.memset(HL[:, n, 0, :], 0.0), (), (HL_b[n],))
        T.op("dve", lambda e, n=n: e.memset(HST[:, n, 0:1], 0.0), (), (HST_b[n],))

    def tile_info(kind, ti):
        if kind == "pre":
            return dict(x=xpre, r0=ti * 512, NT=512, cs=cs_pre, qc=None, kc=ti * 512, ko=None, vo=None)
        if kind == "main":
            return dict(x=xm, r0=ti * 512, NT=512, cs=cs_main, qc=ti * 512, kc=NPRE + ti * 512, ko=k_main, vo=v_main)
        return dict(x=xs, r0=0, NT=256, cs=cs_smp, qc=NMAIN, kc=NPRE + NMAIN, ko=k_smp, vo=v_smp)

    xloads = []
    for kind, ti in s1_tiles:
        inf = tile_info(kind, ti)
        for g in range(inf["NT"] // 128):
            xloads.append((inf["x"], inf["r0"] + g * 128))
    xl = {"i": 0}

    def x_prefetch(upto):
        while xl["i"] <= min(upto, len(xloads) - 1):
            i = xl["i"]
            src, r = xloads[i]
            T.dma("sp", xbuf[i % 2][:], src[r:r + 128, :], (), (xbuf_b[i % 2],), xbuf_ds[i % 2])
            xl["i"] += 1

    def rstd_from_ss(ss_ap, n, bufs):
        T.op("dve", lambda e: e.tensor_scalar(out=ss_ap, in0=ss_ap, scalar1=1.0 / n, scalar2=EPS,
                                              op0=ALU.mult, op1=ALU.add), bufs, bufs)
        T.op("act", lambda e: e.activation(out=ss_ap, in_=ss_ap, func=AF.Sqrt), bufs, bufs)
        T.op("dve", lambda e: e.reciprocal(out=ss_ap, in_=ss_ap), bufs, bufs)

    def norm_transpose(src_tile, src_b, dstT, dst_b, g, gcol0, ptxs, ptxs_b, junk, junk_b, ssq_ap, ssq_buf):
        T.op("act", lambda e: e.activation(out=junk[:], in_=src_tile, func=AF.Square, accum_out=ssq_ap),
             (src_b,), (junk_b, ssq_buf))
        rstd_from_ss(ssq_ap, float(D), (ssq_buf,))
        T.op("dve", lambda e: e.tensor_scalar_mul(out=junk[:], in0=src_tile, scalar1=ssq_ap),
             (src_b, ssq_buf), (junk_b,))
        for q4 in range(4):
            pt = ptxs[q4 % 2]
            ptb = ptxs_b[q4 % 2]
            for j in range(8):
                kc = q4 * 8 + j
                T.op("pe", lambda e, kc=kc, j=j, pt=pt: e.transpose(out=pt[:, j * 128:(j + 1) * 128],
                                                                    in_=junk[:, kc * 128:(kc + 1) * 128], identity=ident[:]),
                     (junk_b, B_const), (ptb,))
            gsl = gvec[:, gcol0 + q4 * 8:gcol0 + q4 * 8 + 8].unsqueeze(2).to_broadcast([128, 8, 128])
            T.op("dve", lambda e, pt=pt, q4=q4, gsl=gsl: e.tensor_tensor(
                out=dstT[:, q4 * 8:(q4 + 1) * 8, g * 128:(g + 1) * 128],
                in0=pt[:].rearrange("p (k t) -> p k t", k=8), in1=gsl, op=ALU.mult),
                (ptb, B_const), (dst_b,))

    deferred = []

    def flush_deferred():
        while deferred:
            deferred.pop(0)()

    first_main = [True]
    gcount = 0
    qkT_i = 0
    alt = {"qk": 0, "v": 0, "y": 0}
    dbg_tiles = int(os.environ.get("DBG_TILES", "99"))
    XR = int(os.environ.get("DBG_XR", "99"))
    dbg_blocks = int(os.environ.get("DBG_BLOCKS", "99"))
    for kind, ti in s1_tiles[:dbg_tiles]:
        inf = tile_info(kind, ti)
        NT = inf["NT"]
        NG = NT // 128
        nseg = 4 if kind == "smp" else 1
        Lseg = NT // nseg
        blocks = (IN_PRE if kind == "pre" else IN_MAIN)[:dbg_blocks]
        if kind == "main" and first_main[0]:
            first_main[0] = False
            for n in range(16):
                T.op("dve", lambda e, n=n: e.tensor_scalar_mul(out=HL[:, n, 0, :], in0=HL[:, n, 0, :], scalar1=VALID),
                     (HL_b[n], B_const), (HL_b[n],))
                T.op("dve", lambda e, n=n: e.tensor_scalar_mul(out=HST[:, n, 0:1], in0=HST[:, n, 0:1], scalar1=VALID),
                     (HST_b[n], B_const), (HST_b[n],))
        if kind == "smp":
            for n in range(16):
                T.op("dve", lambda e, n=n: e.tensor_copy(out=stout[:, n, 0:3], in_=HL[:, n, 0, :]), (HL_b[n],), (stout_b,))
                T.op("dve", lambda e, n=n: e.tensor_copy(out=stout[:, n, 3:4], in_=HST[:, n, 0:1]), (HST_b[n],), (stout_b,))
            T.dma("sp", st_main[:, :], stout[:].rearrange("p c k -> p (c k)"), (stout_b,), (), stout_ds)
            for n in range(16):
                T.op("dve", lambda e, n=n: e.tensor_copy(
                    out=HL[:, n, :, :], in_=constT[:, n, 8:20].rearrange("p (s j) -> p s j", s=4)), CB, (HL_b[n],))
                T.op("dve", lambda e, n=n: e.tensor_copy(out=HST[:, n, :], in_=constT[:, n, 20:24]), CB, (HST_b[n],))
        T.dma("sp", cst[:, 0:NG, :], inf["cs"][inf["r0"]:inf["r0"] + NT, :].rearrange("(g p) c -> p g c", p=128),
              (), (cst_b,), cst_ds)
        for g in range(NG):
            x_prefetch(gcount + 1)
            xb = xbuf[gcount % 2]
            norm_transpose(xb[:], xbuf_b[gcount % 2], xnT, xnT_b[g], g, 0, ptx, ptx_b, xsb, xsb_b,
                           ssq[:, g:g + 1], ssq_b)
            gcount += 1
        for (typ, bi) in blocks:
            col0 = COL0[typ] + bi * 512
            for kh in range(2):
                slot, slb = w_next(("in", col0, kh))
                sl3 = slot[:].rearrange("p (kc c) -> p kc c", kc=16)
                if typ in ("q", "k", "v"):
                    for g in range(NG):
                        for kc in range(16):
                            T.op("pe", lambda e, g=g, kc=kc, kh=kh, sl3=sl3: e.matmul(
                                pp[g][:, :], lhsT=xnT[:, kh * 16 + kc, g * 128:(g + 1) * 128], rhs=sl3[:, kc, :],
                                start=(kh == 0 and kc == 0), stop=(kh == 1 and kc == 15)),
                                (xnT_b[g], slb), (pp_b[g],))
                else:
                    for cc in range(4):
                        for kc in range(16):
                            T.op("pe", lambda e, cc=cc, kc=kc, kh=kh, sl3=sl3: e.matmul(
                                pp[cc][:, 0:NT], lhsT=sl3[:, kc, cc * 128:(cc + 1) * 128], rhs=xnT[:, kh * 16 + kc, 0:NT],
                                start=(kh == 0 and kc == 0), stop=(kh == 1 and kc == 15)),
                                tuple(xnT_b[:NG]) + (slb,), (pp_b[cc],))
            if typ in ("q", "k"):
                sub0 = (bi * 4) if typ == "q" else (16 + bi * 4)
                for g in range(NG):
                    T.op("act", lambda e, g=g: e.activation(out=qk32[g][:], in_=pp[g][:, :], func=AF.Copy),
                         (pp_b[g],), (qk32_b[g],))
                flush_deferred()
                for g in range(NG):
                    a_ = g
                    q32, q32b = qk32[g], qk32_b[g]
                    q3 = q32[:].rearrange("p (s d) -> p s d", s=4)
                    x1 = q3[:, :, 0:16]
                    x2 = q3[:, :, 16:32]
                    cos = cst[:, g, 0:64].rearrange("p (s d) -> p s d", s=4)
                    sin = cst[:, g, 64:128].rearrange("p (s d) -> p s d", s=4)
                    rb = (q32b, cst_b)
                    T.op("dve", lambda e, x1=x1, cos=cos: e.tensor_tensor(out=rtmp[:, 0], in0=x1, in1=cos, op=ALU.mult), rb, (rtmp_b,))
                    T.op("dve", lambda e, x2=x2, sin=sin: e.tensor_tensor(out=rtmp[:, 1], in0=x2, in1=sin, op=ALU.mult), rb, (rtmp_b,))
                    T.op("dve", lambda e, x2=x2, cos=cos: e.tensor_tensor(out=rtmp[:, 2], in0=x2, in1=cos, op=ALU.mult), rb, (rtmp_b,))
                    T.op("dve", lambda e, x1=x1, sin=sin: e.tensor_tensor(out=rtmp[:, 3], in0=x1, in1=sin, op=ALU.mult), rb, (rtmp_b,))
                    T.op("dve", lambda e, x1=x1: e.tensor_tensor(out=x1, in0=rtmp[:, 0], in1=rtmp[:, 1], op=ALU.subtract),
                         (rtmp_b,), (q32b,))
                    T.op("dve", lambda e, x2=x2: e.tensor_tensor(out=x2, in0=rtmp[:, 2], in1=rtmp[:, 3], op=ALU.add),
                         (rtmp_b,), (q32b,))
                    if typ == "k" and inf["ko"] is not None:
                        r0 = inf["r0"] + g * 128
                        T.dma("sp", inf["ko"][r0:r0 + 128, bi * 512:(bi + 1) * 512], q32[:], (q32b,), (), qk32_ds[a_])
                    qb, qbb = qkbf[g], qkbf_b[g]
                    T.op("act", lambda e, q32=q32, qb=qb: e.activation(out=qb[:], in_=q32[:], func=AF.Copy), (q32b,), (qbb,))

                    def tr(g=g, qb=qb, qbb=qbb, sub0=sub0, typ=typ, bi=bi, last=(typ == "k" and bi == 1)):
                        pt, ptb = ptx[g % 2], ptx_b[g % 2]
                        for s in range(4):
                            T.op("pe", lambda e, s=s: e.transpose(out=pt[:, s * 128:(s + 1) * 128],
                                                                  in_=qb[:, s * 128:(s + 1) * 128], identity=ident[:]),
                                 (qbb, B_const), (ptb,))
                        st_t, st_b = qkT_st[g], qkT_b[g]
                        T.op("act", lambda e: e.activation(out=st_t[:, :, :],
                                                           in_=pt[:, 0:512].rearrange("p (s t) -> p s t", s=4), func=AF.Copy),
                             (ptb,), (st_b,))
                        if typ == "q":
                            c0 = inf["qc"] + g * 128
                            T.dma("sp", qT_s[sub0:sub0 + 4, :, c0:c0 + 128].rearrange("s d t -> d s t"), st_t[:, :, :],
                                  (st_b,), (), qkT_ds[g])
                        else:
                            c0 = inf["kc"] + g * 128
                            T.dma("sp", kT_s[sub0 - 16:sub0 - 12, :, c0:c0 + 128].rearrange("s d t -> d s t"), st_t[:, :, :],
                                  (st_b,), (), qkT_ds[g])
                    deferred.append(tr)
            elif typ == "v":
                for g in range(NG):
                    T.op("act", lambda e, g=g: e.activation(out=v32[g][:], in_=pp[g][:, :], func=AF.Copy),
                         (pp_b[g],), (v32_b[g],))
                flush_deferred()
                for g in range(NG):
                    a_ = g
                    T.op("dve", lambda e, g=g, a_=a_: e.tensor_copy(out=qkbf[g][:], in_=v32[a_][:]),
                         (v32_b[a_],), (qkbf_b[g],))
                    if inf["vo"] is not None:
                        r0 = inf["r0"] + g * 128
                        T.dma("sp", inf["vo"][r0:r0 + 128, bi * 512:(bi + 1) * 512], v32[a_][:], (v32_b[a_],), (), v32_ds[a_])
                    r0 = inf["kc"] + g * 128
                    T.dma("sp", v_s[r0:r0 + 128, bi * 512:(bi + 1) * 512], qkbf[g][:], (qkbf_b[g],), (), vbf_ds[g])
            elif typ == "xr":
                W = 3 + Lseg

                def v3(t):
                    return t[:, 0:NT].rearrange("p (s l) -> p s l", s=nseg)

                for cc in range(4):
                    xh3 = xh4[:, cc, 0:nseg * W].rearrange("p (s w) -> p s w", s=nseg)
                    T.op("act", lambda e, cc=cc, xh3=xh3: e.activation(out=xh3[:, :, 3:W], in_=v3(pp[cc]), func=AF.Copy),
                         (pp_b[cc],), (xh4_b[cc],))
                flush_deferred()
                for cc in range(4):
                    n = bi * 4 + cc
                    xh3 = xh4[:, cc, 0:nseg * W].rearrange("p (s w) -> p s w", s=nseg)
                    xh_b = xh4_b[cc]
                    if XR < 1:
                        continue
                    T.op("dve", lambda e, n=n: e.tensor_copy(out=xh3[:, :, 0:3], in_=HL[:, n, 0:nseg, :]), (HL_b[n],), (xh_b,))
                    T.op("dve", lambda e, n=n: e.tensor_copy(out=HL[:, n, 0:nseg, :], in_=xh3[:, :, W - 3:W]), (xh_b,), (HL_b[n],))
                    if kind == "smp":
                        T.op("dve", lambda e, n=n: e.tensor_copy(out=stout_s[:, n, :, 0:3], in_=xh3[:, :, W - 3:W]),
                             (xh_b,), (stout_b,))
                    if XR < 2:
                        continue
                    cw = constT[:, n, :]
                    xc, xc_b, xcb, xcb_b = xc4[:, cc, :], xc4_b[cc], xcb4[:, cc, :], xcb4_b[cc]
                    T.op("dve", lambda e, cw=cw, xc=xc: e.tensor_scalar(out=v3(xc), in0=xh3[:, :, 3:W], scalar1=cw[:, 3:4],
                                                                 scalar2=cw[:, 4:5], op0=ALU.mult, op1=ALU.add),
                         (xh_b, B_const), (xc_b,))
                    for j in range(3):
                        T.op("dve", lambda e, cw=cw, j=j, xc=xc: e.scalar_tensor_tensor(
                            out=v3(xc), in0=xh3[:, :, j:j + Lseg], scalar=cw[:, j:j + 1], in1=v3(xc),
                            op0=ALU.mult, op1=ALU.add), (xh_b, xc_b, B_const), (xc_b,))
                    T.op("act", lambda e, xc=xc, xcb=xcb: e.activation(out=xcb[:, 0:NT], in_=xc[:, 0:NT], func=AF.Copy), (xc_b,), (xcb_b,))

                    if XR < 3:
                        continue
                    def gates(n=n, cc=cc, cw=cw, xc=xc, xc_b=xc_b, xcb=xcb, xcb_b=xcb_b):
                        T.op("pe", lambda e: e.matmul(pg[0][:, 0:NT], lhsT=gaw[:, n, :], rhs=xcb[:, 0:NT], start=True, stop=True),
                             (xcb_b, B_const), (pg_b[0],))
                        T.op("pe", lambda e: e.matmul(pg[1][:, 0:NT], lhsT=gxw[:, n, :], rhs=xcb[:, 0:NT], start=True, stop=True),
                             (xcb_b, B_const), (pg_b[1],))
                        T.op("act", lambda e: e.activation(out=rr[:, 0:NT], in_=pg[0][:, 0:NT], func=AF.Sigmoid, bias=cw[:, 5:6]),
                             (pg_b[0], B_const), (rr_b,))
                        T.op("act", lambda e: e.activation(out=ii[:, 0:NT], in_=pg[1][:, 0:NT], func=AF.Sigmoid, bias=cw[:, 6:7]),
                             (pg_b[1], B_const), (ii_b,))
                        if XR < 4:
                            return
                        T.op("act", lambda e: e.activation(out=aa[:, 0:NT], in_=rr[:, 0:NT], func=AF.Exp, scale=c12[:, 0, n:n + 1]),
                             (rr_b, B_const), (aa_b,))
                        T.op("act", lambda e: e.activation(out=mm_[:, 0:NT], in_=rr[:, 0:NT], func=AF.Exp, scale=c12[:, 1, n:n + 1]),
                             (rr_b, B_const), (mm_b,))
                        T.op("act", lambda e: e.activation(out=mm_[:, 0:NT], in_=mm_[:, 0:NT], func=AF.Sqrt, bias=1.0, scale=-1.0),
                             (mm_b,), (mm_b,))
                        T.op("dve", lambda e: e.tensor_tensor(out=uu[:, 0:NT], in0=ii[:, 0:NT], in1=xc[:, 0:NT], op=ALU.mult),
                             (ii_b, xc_b), (uu_b,))
                        T.op("dve", lambda e: e.tensor_tensor(out=uu[:, 0:NT], in0=uu[:, 0:NT], in1=mm_[:, 0:NT], op=ALU.mult),
                             (uu_b, mm_b), (uu_b,))
                        if XR < 5:
                            return
                        for s in range(nseg):
                            T.op("dve", lambda e, s=s: e.tensor_tensor_scan(
                                out=hs4[:, cc, s * Lseg:(s + 1) * Lseg], data0=aa[:, s * Lseg:(s + 1) * Lseg],
                                data1=uu[:, s * Lseg:(s + 1) * Lseg], initial=HST[:, n, s:s + 1], op0=ALU.mult, op1=ALU.add),
                                (aa_b, uu_b, HST_b[n]), (hs4_b[cc],))
                        if XR < 6:
                            return
                        T.op("dve", lambda e: e.tensor_copy(
                            out=HST[:, n, 0:nseg],
                            in_=hs4[:, cc, 0:NT].rearrange("p (s l) -> p s l", s=nseg)[:, :, Lseg - 1]),
                            (hs4_b[cc],), (HST_b[n],))
                        if kind == "smp":
                            T.op("dve", lambda e: e.tensor_copy(out=stout_s[:, n, :, 3], in_=HST[:, n, :]), (HST_b[n],), (stout_b,))
                    deferred.append(gates)
            else:
                for cc in range(4):
                    T.op("act", lambda e, cc=cc: e.activation(out=gr4[:, cc, 0:NT], in_=pp[cc][:, 0:NT], func=AF.Copy),
                         (pp_b[cc],), (gr4_b[cc],))
                flush_deferred()
                for cc in range(4):
                    n = bi * 4 + cc
                    gr32, gr32_b = gr4[:, cc, :], gr4_b[cc]
                    T.op("act", lambda e: e.activation(out=g1[:, 0:NT], in_=gr32[:, 0:NT], func=AF.Square), (gr32_b,), (g1_b,))
                    T.op("dve", lambda e: e.tensor_scalar(out=g1[:, 0:NT], in0=g1[:, 0:NT], scalar1=0.044715, scalar2=1.0,
                                                          op0=ALU.mult, op1=ALU.add), (g1_b,), (g1_b,))
                    T.op("dve", lambda e: e.tensor_tensor(out=g1[:, 0:NT], in0=g1[:, 0:NT], in1=gr32[:, 0:NT], op=ALU.mult),
                         (g1_b, gr32_b), (g1_b,))
                    T.op("act", lambda e: e.activation(out=g2[:, 0:NT], in_=g1[:, 0:NT], func=AF.Sigmoid, scale=GELU_C),
                         (g1_b,), (g2_b,))
                    T.op("dve", lambda e: e.tensor_tensor(out=g2[:, 0:NT], in0=g2[:, 0:NT], in1=gr32[:, 0:NT], op=ALU.mult),
                         (g2_b, gr32_b), (g2_b,))

                    def ymul(n=n, cc=cc):
                        a_ = alt["y"] % 2
                        alt["y"] += 1
                        T.op("dve", lambda e: e.tensor_tensor(out=ylb[a_][:, 0:NT], in0=g2[:, 0:NT], in1=hs4[:, cc, 0:NT], op=ALU.mult),
                             (g2_b, hs4_b[cc]), (ylb_b[a_],))
                        T.dma("sp", mixT_s[16 + n, :, inf["qc"]:inf["qc"] + NT], ylb[a_][:, 0:NT], (ylb_b[a_],), (), ylb_ds[a_])
                    flush_deferred()
                    ymul()
        flush_deferred()
    T.dma("sp", st_smp[:, :], stout_s[:].rearrange("p c s k -> p (c s k)"), (stout_b,), (), stout_ds)
    T.barrier()
    with nc.Block() as block:
        T.replay(block)
    st.close()
    if upto == 1:
        top.close()
        return nc

    st = ExitStack()
    kTt = [sb(st, "kTt%d" % i, [128, 2, 4096], BF16) for i in range(2)]
    vat = [sb(st, "vat%d" % i, [128, 32, 258], BF16) for i in range(2)]
    qTt = [sb(st, "qTt%d" % i, [128, 4, 2048], BF16) for i in range(2)]
    kvq_b = [(Buf(), Buf(), Buf()) for _ in range(2)]
    kvq_ds = [DS() for _ in range(2)]
    kvq_dsp = [DS() for _ in range(2)]
    kcb = sb(st, "kcb", [128, 16, 256], BF16)
    kcb_b = Buf()
    kcb_ds = DS()
    pT = [sb(st, "pT%d" % i, [128, 2, 256], BF16) for i in range(3)]
    pT_b = [Buf() for _ in range(3)]
    o32 = sb(st, "o32", [128, 4, 260], F32)
    o32_b = Buf()
    d32s = [sb(st, "d32_%d" % i, [128, 2, 256], F32) for i in range(2)]
    d32s_b = [Buf() for _ in range(2)]
    rss = [sb(st, "rs%d" % i, [128, 8], F32) for i in range(2)]
    rss_b = [Buf() for _ in range(2)]
    fin_pending = []
    onb = sb(st, "onb", [128, 2, 256], BF16)
    onb_b = Buf()
    junk2 = sb(st, "junk2", [128, 256], F32)
    junk2_b = Buf()
    oT_st = [sb(st, "oTst%d" % i, [128, 2, 256], BF16) for i in range(2)]
    oT_b = [Buf() for _ in range(2)]
    oT_ds = [DS() for _ in range(2)]
    psc = [ps(st, "psc%d" % i, [128, 512], F32) for i in range(3)]
    psc_b = [Buf() for _ in range(3)]
    po = [ps(st, "po%d" % i, [128, 512], F32) for i in range(4)]
    po_b = [Buf() for _ in range(4)]
    pt2_0 = ps(st, "pt2_0", [128, 1024], BF16)
    pt2 = [pt2_0, pt2_0]
    pt2_0b = Buf()
    pt2_b = [pt2_0b, pt2_0b]
    for i in range(2):
        T.op("dve", lambda e, i=i: e.memset(vat[i][:, :, 256:258], 1.0), (), (kvq_b[i][1],))
    acount = {"o": 0, "blk": 0}

    def attn_block(kT_of, v_of, q_of, bias_of, nblocks, band0, NS, kbuf, qbuf, out_store):
        QW = NS * 128
        items = []
        for j in range(nblocks):
            spj = (j - band0) if band0 is not None else -1
            col0 = 128 * spj if spj > 0 else 0
            items.append((j, spj, col0))

        def qk(it):
            j, spj, col0 = it
            pb = acount["blk"] % 3
            for c in range(2):
                kt = kT_of(c, j)
                nk = kt.shape[1]
                T.op("pe", lambda e, c=c, kt=kt, nk=nk, col0=col0, pb=pb: e.matmul(
                    psc[pb][0:nk, c * 256 + col0:c * 256 + QW], lhsT=kt, rhs=q_of(c, col0), start=True, stop=True),
                    tuple(kbuf) + (qbuf,), (psc_b[pb],))
            nk = kT_of(0, j).shape[1]
            bj = bias_of(j)
            T.op("act", lambda e, pb=pb, nk=nk, col0=col0, bj=bj: e.activation(
                out=pT[pb][0:nk, :, col0:QW],
                in_=psc[pb][0:nk, :].rearrange("p (c t) -> p c t", c=2)[:, :, col0:QW],
                func=AF.Exp, scale=SCALE, bias=bj), (psc_b[pb], B_const), (pT_b[pb],))
            if spj >= 0:
                T.op("dve", lambda e, pb=pb, spj=spj: e.memset(pT[pb][64:128, :, spj * 128:spj * 128 + 64], 0.0),
                     (), (pT_b[pb],))
            acount["blk"] += 1
            return pb

        def pv(it, pb):
            j, spj, col0 = it
            vv, nk = v_of(j)
            for c in range(2):
                for s in range(max(spj, 0), NS):
                    last = (band0 + s) if band0 is not None else nblocks - 1
                    T.op("pe", lambda e, c=c, s=s, vv=vv, nk=nk, pb=pb, j=j, last=last: e.matmul(
                        po[c * 2 + s][:, 0:257], lhsT=pT[pb][0:nk, c, s * 128:(s + 1) * 128], rhs=vv,
                        start=(j == 0), stop=(j == last)), (pT_b[pb],) + tuple(kbuf), (po_b[c * 2 + s],))

        pend_pv = []
        for it in items:
            pb = qk(it)
            pend_pv.append((it, pb))
            if len(pend_pv) > 2:
                pv(*pend_pv.pop(0))
        while pend_pv:
            pv(*pend_pv.pop(0))
        fin_flush(0)
        for c in range(2):
            for s in range(NS):
                i4 = c * 2 + s
                if i4 % 2 == 0:
                    T.op("act", lambda e, i4=i4: e.activation(out=o32[:, i4, 0:257], in_=po[i4][:, 0:257], func=AF.Copy),
                         (po_b[i4],), (o32_b,))
                else:
                    T.op("dve", lambda e, i4=i4: e.tensor_copy(out=o32[:, i4, 0:257], in_=po[i4][:, 0:257]),
                         (po_b[i4],), (o32_b,))
        fi = acount["o"] % 2
        d32, d32_b, rs, rs_b = d32s[fi], d32s_b[fi], rss[fi], rss_b[fi]
        ob = (o32_b, rs_b)
        for s in range(NS):
            T.op("dve", lambda e, s=s: e.reciprocal(out=rs[:, 0:1], in_=o32[:, s, 256:257]), (o32_b,), (rs_b,))
            T.op("dve", lambda e, s=s: e.reciprocal(out=rs[:, 1:2], in_=o32[:, 2 + s, 256:257]), (o32_b,), (rs_b,))
            T.op("dve", lambda e: e.tensor_tensor(out=rs[:, 1:2], in0=rs[:, 1:2], in1=NLAM, op=ALU.mult), (rs_b, B_const), (rs_b,))
            T.op("dve", lambda e, s=s: e.tensor_scalar_mul(out=d32[:, s, :], in0=o32[:, s, 0:256], scalar1=rs[:, 0:1]), ob, (d32_b,))
            T.op("dve", lambda e, s=s: e.scalar_tensor_tensor(out=d32[:, s, :], in0=o32[:, 2 + s, 0:256], scalar=rs[:, 1:2],
                                                              in1=d32[:, s, :], op0=ALU.mult, op1=ALU.add),
                 (o32_b, rs_b, d32_b), (d32_b,))
            T.op("dve", lambda e, s=s: e.scalar_tensor_tensor(out=junk2[:], in0=d32[:, s, :], scalar=1.0, in1=d32[:, s, :],
                                                              op0=ALU.mult, op1=ALU.mult, accum_out=rs[:, 4 + s:5 + s]),
                 (d32_b,), (junk2_b, rs_b))
        T.op("dve", lambda e: e.tensor_scalar(out=rs[:, 4:4 + NS], in0=rs[:, 4:4 + NS], scalar1=1.0 / 256.0, scalar2=EPS,
                                              op0=ALU.mult, op1=ALU.add), (rs_b,), (rs_b,))
        oi = acount["o"] % 2
        acount["o"] += 1

        def finB(d32=d32, d32_b=d32_b, rs=rs, rs_b=rs_b, oi=oi, NS=NS, QW=QW, out_store=out_store):
            T.op("act", lambda e: e.activation(out=rs[:, 4:4 + NS], in_=rs[:, 4:4 + NS], func=AF.Ln), (rs_b,), (rs_b,))
            T.op("act", lambda e: e.activation(out=rs[:, 4:4 + NS], in_=rs[:, 4:4 + NS], func=AF.Exp, scale=-0.5), (rs_b,), (rs_b,))
            for s in range(NS):
                T.op("dve", lambda e, s=s: e.scalar_tensor_tensor(out=onb[:, s, :], in0=d32[:, s, :], scalar=rs[:, 4 + s:5 + s], in1=subg[:],
                                                                  op0=ALU.mult, op1=ALU.mult), (d32_b, rs_b, B_const), (onb_b,))
            for s in range(NS):
                for eh in range(2):
                    T.op("pe", lambda e, s=s, eh=eh: e.transpose(out=pt2[oi][:, (eh * 2 + s) * 128:(eh * 2 + s + 1) * 128],
                                                                 in_=onb[:, s, eh * 128:(eh + 1) * 128], identity=ident[:]),
                         (onb_b, B_const), (pt2_b[oi],))
            T.op("act", lambda e: e.activation(out=oT_st[oi][:, :, 0:QW],
                                               in_=pt2[oi][:, 0:512].rearrange("p (a t) -> p a t", a=2)[:, :, 0:QW], func=AF.Copy),
                 (pt2_b[oi],), (oT_b[oi],))
            out_store(oT_st[oi], oT_b[oi], oT_ds[oi])
        fin_pending.append(finB)

    def fin_flush(keep):
        while len(fin_pending) > keep:
            fin_pending.pop(0)()

    for h in range(4):
        ab = h % 2
        kb, vb, qb = kvq_b[ab]
        T.dma("sp", kTt[ab][:], kT_s[h * 2:h * 2 + 2, :, 0:4096].rearrange("c d t -> d c t"), (), (kb,), kvq_ds[ab])
        T.dma("sp", vat[ab][:, :, 0:256], v_s[0:4096, h * 256:(h + 1) * 256].rearrange("(g p) e -> p g e", p=128),
              (), (vb,), kvq_ds[ab])
        T.dma("sp", qTt[ab][:], qT_s[h * 4:h * 4 + 4, :, 0:2048].rearrange("s d t -> d s t"), (), (qb,), kvq_ds[ab])
        kvb = Buf()
        for g in range(2):
            for Q in range(8):
                A = NPRE + Q * 256
                nfull = A // 128

                def kT_of(c, j, ab=ab):
                    return kTt[ab][:, c, j * 128:(j + 1) * 128]

                def v_of(j, ab=ab):
                    return vat[ab][:, j, 0:257], 128

                def q_of(c, col0, ab=ab, g=g, Q=Q):
                    return qTt[ab][:, g * 2 + c, Q * 256 + col0:(Q + 1) * 256]

                def bias_of(j):
                    return PREB if j < 16 else 0.0

                def out_store(ot, otb, ods, h=h, g=g, Q=Q):
                    ch = (h * 2 + g) * 2
                    T.dma("sp", mixT_s[ch:ch + 2, :, Q * 256:(Q + 1) * 256].rearrange("a d t -> d a t"), ot[:, :, :],
                          (otb,), (), ods)

                attn_block(kT_of, v_of, q_of, bias_of, nfull + 2, nfull, 2, (kb, vb), qb, out_store)
    fin_flush(0)
    T.barrier()
    with nc.Block() as block:
        T.replay(block)

    kTs = kTt
    for sq in range(4):
        for h in range(4):
            ab = (sq * 4 + h) % 2
            kb, vb, qb = kvq_b[ab]
            T.dma("pool", kcb[:], ck[sq, :, h * 256:(h + 1) * 256].rearrange("(g p) e -> p g e", p=128), (), (kcb_b,), kcb_ds)
            for gg in range(16):
                for c in range(2):
                    idx = gg * 2 + c
                    pti = (idx // 8) % 2
                    T.op("pe", lambda e, gg=gg, c=c, idx=idx, pti=pti: e.transpose(
                        out=pt2[pti][:, (idx % 8) * 128:(idx % 8 + 1) * 128], in_=kcb[:, gg, c * 128:(c + 1) * 128],
                        identity=ident[:]), (kcb_b, B_const), (pt2_b[pti],))
                if gg % 4 == 3:
                    pti = ((gg * 2) // 8) % 2
                    g0 = gg - 3
                    T.op("dve", lambda e, pti=pti, g0=g0, ab=ab: e.tensor_copy(
                        out=kTs[ab][:, :, g0 * 128:(g0 + 4) * 128].rearrange("p c (g t) -> p g c t", g=4),
                        in_=pt2[pti][:].rearrange("p (g c t) -> p g c t", g=4, c=2)), (pt2_b[pti],), (kb,))
            T.dma("sp", kTs[ab][:, :, 2048:2112],
                  kT_s[h * 2:h * 2 + 2, :, NPRE + NMAIN + sq * 64:NPRE + NMAIN + sq * 64 + 64].rearrange("c d t -> d c t"),
                  (), (kb,), kvq_ds[ab])
            T.dma("pool", vat[ab][:, 0:16, 0:256], cv[sq, :, h * 256:(h + 1) * 256].rearrange("(g p) e -> p g e", p=128),
                  (), (vb,), kvq_dsp[ab])
            r0 = NPRE + NMAIN + sq * 64
            T.dma("sp", vat[ab][0:64, 16, 0:256], v_s[r0:r0 + 64, h * 256:(h + 1) * 256], (), (vb,), kvq_ds[ab])
            c0 = NMAIN + sq * 64
            for g_ in range(2):
                for c_ in range(2):
                    T.dma("sp", qTt[ab][:, c_, g_ * 64:(g_ + 1) * 64], qT_s[h * 4 + g_ * 2 + c_, :, c0:c0 + 64],
                          (), (qb,), kvq_ds[ab])

            def kT_of(c, j, ab=ab):
                return kTs[ab][:, c, j * 128:min((j + 1) * 128, 2112)]

            def v_of(j, ab=ab):
                nk = 128 if j < 16 else 64
                return vat[ab][0:nk, j, 0:257], nk

            def q_of(c, col0, ab=ab):
                return qTt[ab][:, c, 0:128]

            def bias_of(j):
                return 0.0

            def out_store(ot, otb, ods, h=h, sq=sq):
                for g in range(2):
                    ch = (h * 2 + g) * 2
                    c0 = NMAIN + sq * 64
                    T.dma("sp", mixT_s[ch:ch + 2, :, c0:c0 + 64].rearrange("a d t -> d a t"), ot[:, :, g * 64:(g + 1) * 64],
                          (otb,), (), ods)

            attn_block(kT_of, v_of, q_of, bias_of, 17, None, 1, (kb, vb), qb, out_store)
    fin_flush(0)
    T.barrier()
    with nc.Block() as block:
        T.replay(block)
    st.close()
    if upto == 2:
        top.close()
        return nc

    st = ExitStack()
    ws_begin(st, 4, plan3)
    hh = sb(st, "hh", [128, 4, D], F32)
    hh_b = [Buf() for _ in range(4)]
    hh_ds = [DS() for _ in range(4)]
    actT = sb(st, "actT", [128, 32, 512], BF16)
    actT_b = [Buf() for _ in range(4)]
    actT_ds = DS()
    hsb = sb(st, "hsb", [128, D], BF16)
    hsb_b = Buf()
    gfin = sb(st, "gfin", [128, D], F32)
    ssq3 = sb(st, "ssq3", [128, 8], F32)
    ssq3_b = Buf()
    z32 = [sb(st, "z32_%d" % i, [128, 512], F32) for i in range(2)]
    z32_b = [Buf() for _ in range(2)]
    zT = [sb(st, "zT%d" % i, [128, 4, 512], BF16) for i in range(2)]
    zT_b = [Buf() for _ in range(2)]
    ptx1 = ps(st, "ptx3", [128, 1024], BF16)
    ptx = [ptx1, ptx1]
    ptx1_b = Buf()
    ptx_b = [ptx1_b, ptx1_b]
    pp = [ps(st, "pp3_%d" % i, [128, 512], F32) for i in range(4)]
    pp_b = [Buf() for _ in range(4)]
    pd = [ps(st, "pd%d" % i, [128, 512], F32) for i in range(3)]
    pd_b = [Buf() for _ in range(3)]
    T.dma("sp", gfin[:], gfin_d[:, :], (), (B_const,), ds_const)

    pcount = 0
    for kind, ti in s3_tiles:
        if kind == "main":
            xsrc, r0, NT, c0, ydst = xm, ti * 512, 512, ti * 512, y_main
        else:
            xsrc, r0, NT, c0, ydst = xs, 0, 256, NMAIN, y_smp
        NG = NT // 128
        T.dma("sp", actT[:, :, 0:NT], mixT_s[:, :, c0:c0 + NT].rearrange("k d t -> d k t"), (), tuple(actT_b[:NG]), actT_ds)
        for g in range(NG):
            T.dma("sp", hh[:, g, :], xsrc[r0 + g * 128:r0 + (g + 1) * 128, :], (), (hh_b[g],), hh_ds[g])
        for cb in range(8):
            for kh in range(2):
                slot, slb = w_next(("out", cb * 512, kh))
                sl3 = slot[:].rearrange("p (kc c) -> p kc c", kc=16)
                for g in range(NG):
                    for kc in range(16):
                        T.op("pe", lambda e, g=g, kc=kc, kh=kh, sl3=sl3: e.matmul(
                            pp[g][:, :], lhsT=actT[:, kh * 16 + kc, g * 128:(g + 1) * 128], rhs=sl3[:, kc, :],
                            start=(kh == 0 and kc == 0), stop=(kh == 1 and kc == 15)), (actT_b[g], slb), (pp_b[g],))
            for g in range(NG):
                T.op("dve", lambda e, g=g, cb=cb: e.tensor_tensor(out=hh[:, g, cb * 512:(cb + 1) * 512],
                                                                  in0=pp[g][:, :], in1=hh[:, g, cb * 512:(cb + 1) * 512], op=ALU.add),
                     (pp_b[g], hh_b[g]), (hh_b[g],))
        for g in range(NG):
            norm_transpose(hh[:, g, :], hh_b[g], actT, actT_b[g], g, 32, ptx, ptx_b, hsb, hsb_b, ssq3[:, g:g + 1], ssq3_b)
        pend = []

        def down(sbk, zi):
            nonlocal pcount
            slot_a, sla_b = w_next(("dA", sbk))
            slot_bb, slbb_b = w_next(("dB", sbk))
            sd = [slot_a[:].rearrange("p (f c) -> p f c", f=2), slot_bb[:].rearrange("p (f c) -> p f c", f=2)]
            sdb = [sla_b, slbb_b]
            for g in range(NG):
                for cb in range(8):
                    bk = pcount % 3
                    pcount += 1
                    for f in range(4):
                        T.op("pe", lambda e, g=g, cb=cb, f=f, bk=bk: e.matmul(
                            pd[bk][:, :], lhsT=zT[zi][:, f, g * 128:(g + 1) * 128], rhs=sd[f // 2][:, f % 2, cb * 512:(cb + 1) * 512],
                            start=(f == 0), stop=(f == 3)), (zT_b[zi], sdb[f // 2]), (pd_b[bk],))
                    T.op("dve", lambda e, g=g, cb=cb, bk=bk: e.tensor_tensor(
                        out=hh[:, g, cb * 512:(cb + 1) * 512], in0=pd[bk][:, :], in1=hh[:, g, cb * 512:(cb + 1) * 512], op=ALU.add),
                        (pd_b[bk], hh_b[g]), (hh_b[g],))

        for sbk in range(NSB):
            zi = sbk % 2
            for hf, nm in ((0, "upA"), (1, "upB")):
                slot_u, slu_b = w_next((nm, sbk))
                su3 = slot_u[:].rearrange("p (kc c) -> p kc c", kc=16)
                for f in range(4):
                    for kc in range(16):
                        T.op("pe", lambda e, f=f, kc=kc, su3=su3, hf=hf: e.matmul(
                            pp[f][:, 0:NT], lhsT=su3[:, kc, f * 128:(f + 1) * 128], rhs=actT[:, hf * 16 + kc, 0:NT],
                            start=(hf == 0 and kc == 0), stop=(hf == 1 and kc == 15)), tuple(actT_b[:NG]) + (slu_b,), (pp_b[f],))
            for f in range(4):
                T.op("act", lambda e, f=f: e.activation(out=z32[f % 2][:, 0:NT], in_=pp[f][:, 0:NT], func=AF.Relu),
                     (pp_b[f],), (z32_b[f % 2],))
                T.op("act", lambda e, f=f, zi=zi: e.activation(out=zT[zi][:, f, 0:NT], in_=z32[f % 2][:, 0:NT], func=AF.Square),
                     (z32_b[f % 2],), (zT_b[zi],))
            if pend:
                down(*pend.pop(0))
            pend.append((sbk, zi))
        down(*pend.pop(0))
        for g in range(NG):
            T.op("act", lambda e, g=g: e.activation(out=hsb[:], in_=hh[:, g, :], func=AF.Square, accum_out=ssq3[:, 4 + g:5 + g]),
                 (hh_b[g],), (hsb_b, ssq3_b))
            rstd_from_ss(ssq3[:, 4 + g:5 + g], float(D), (ssq3_b,))
            for hf in range(2):
                T.op("dve", lambda e, g=g, hf=hf: e.scalar_tensor_tensor(
                    out=hh[:, g, hf * 2048:(hf + 1) * 2048], in0=hh[:, g, hf * 2048:(hf + 1) * 2048], scalar=ssq3[:, 4 + g:5 + g],
                    in1=gfin[:, hf * 2048:(hf + 1) * 2048], op0=ALU.mult, op1=ALU.mult), (hh_b[g], ssq3_b, B_const), (hh_b[g],))
            T.dma("sp", ydst[r0 + g * 128:r0 + (g + 1) * 128, :], hh[:, g, :], (hh_b[g],), (), hh_ds[g])
    T.barrier()
    with nc.Block() as block:
        T.replay(block)
    st.close()
    top.close()
    return nc


_NC_CACHE = {}


def _rope_table(pos):
    half = 16
    inv = (np.float32(500000.0) ** (-np.arange(half, dtype=np.float32) / np.float32(half))).astype(np.float32)
    ang = pos.astype(np.float32)[:, None] * inv[None, :]
    cos = np.cos(ang).astype(np.float32)
    sin = np.sin(ang).astype(np.float32)
    return np.ascontiguousarray(np.concatenate([np.tile(cos, (1, 4)), np.tile(sin, (1, 4))], axis=1))


def kernel(x_prompt, x_sample, cache_k, cache_v, state_conv, state_lru,
           norm_mix, w_in, conv_w, conv_b, gate_a_w, gate_a_b, gate_x_w, gate_x_b,
           lru_lambda, lambda_q1, lambda_k1, lambda_q2, lambda_k2, subln_g,
           w_out, norm_mlp, w_up, w_down, norm_final):
    f32 = np.float32
    A = lambda a: np.ascontiguousarray(np.asarray(a, dtype=f32))
    x_prompt = A(x_prompt); x_sample = A(x_sample)
    cache_k = A(cache_k); cache_v = A(cache_v)
    state_conv = A(state_conv); state_lru = A(state_lru)
    upto = _NC_CACHE.get("upto", 3)
    if "nc" not in _NC_CACHE:
        _NC_CACHE["nc"] = build_program(upto)
    nc = _NC_CACHE["nc"]

    gvec = np.concatenate([A(norm_mix)[0].reshape(32, 128).T, A(norm_mlp)[0].reshape(32, 128).T], axis=1)
    gfin = np.tile(A(norm_final).reshape(1, D), (128, 1))
    lamv = np.tile(np.concatenate([A(lambda_q1)[0], A(lambda_k1)[0], A(lambda_q2)[0], A(lambda_k2)[0]]).reshape(1, 512), (128, 1))
    subg = np.tile(A(subln_g)[0].reshape(1, 256), (128, 1))
    ident = np.eye(128, dtype=f32)
    shared = dict(
        gvec=A(gvec), gfin=A(gfin), lamv=A(lamv), subg=A(subg), ident=ident,
        w_in=A(w_in)[0],
        w_out=A(w_out)[0] if upto >= 3 else np.zeros((128, 128), f32),
        w_up=A(w_up)[0] if upto >= 3 else np.zeros((128, 128), f32),
        w_down=A(w_down)[0] if upto >= 3 else np.zeros((128, 128), f32),
        ga_w=A(gate_a_w)[0], gx_w=A(gate_x_w)[0],
        cs_pre=_rope_table(np.arange(NPRE)),
        cs_smp=_rope_table(np.tile(PAST + np.arange(64), 4)),
    )
    in_maps = []
    for c in range(8):
        b, half = c // 2, c % 2
        rows = np.concatenate([
            A(conv_w)[0], A(conv_b), A(gate_a_b)[0].reshape(1, 2048), A(gate_x_b)[0].reshape(1, 2048), A(lru_lambda),
            state_conv[0, 4 * c:4 * c + 4].reshape(12, 2048), state_lru[0, 4 * c:4 * c + 4]], axis=0)
        constT = rows.reshape(24, 16, 128).transpose(2, 1, 0).reshape(128, 16 * 24)
        m = dict(shared)
        m.update(
            xm=x_prompt[b, half * 2048:(half + 1) * 2048],
            xpre=x_prompt[b, 0:2048],
            xs=x_sample[4 * c:4 * c + 4].reshape(256, D),
            ck=cache_k[0, 4 * c:4 * c + 4].reshape(4, PAST, 1024),
            cv=cache_v[0, 4 * c:4 * c + 4].reshape(4, PAST, 1024),
            constT=A(constT),
            valid=np.full((128, 1), float(half), dtype=f32),
            cs_main=_rope_table(half * 2048 + np.arange(NMAIN)),
        )
        in_maps.append({k: np.ascontiguousarray(v) for k, v in m.items()})
    if os.environ.get("DBG_CORES"):
        res = run_bass_kernel_spmd(nc, [in_maps[1]], core_ids=[0])
        R = [res.results[0]] * 8
    else:
        res = run_bass_kernel_spmd(nc, in_maps, core_ids=list(range(8)))
        R = res.results

    y_prompt = np.empty((4, 4096, D), f32)
    k_prompt = np.empty((1, 4, 4096, 4, 256), f32)
    v_prompt = np.empty((1, 4, 4096, 4, 256), f32)
    conv_prompt = np.empty((1, 4, 3, 2048), f32)
    lru_prompt = np.empty((1, 4, 2048), f32)
    y_sample = np.empty((32, 64, D), f32)
    k_sample = np.empty((1, 32, 64, 4, 256), f32)
    v_sample = np.empty((1, 32, 64, 4, 256), f32)
    conv_sample = np.empty((1, 32, 3, 2048), f32)
    lru_sample = np.empty((1, 32, 2048), f32)
    for c in range(8):
        b, half = c // 2, c % 2
        r = R[c]
        sl = slice(half * 2048, (half + 1) * 2048)
        y_prompt[b, sl] = r["y_main"]
        k_prompt[0, b, sl] = r["k_main"].reshape(2048, 4, 256)
        v_prompt[0, b, sl] = r["v_main"].reshape(2048, 4, 256)
        if half == 1:
            stm = r["st_main"].reshape(128, 16, 4)
            full = stm.transpose(2, 1, 0).reshape(4, 2048)
            conv_prompt[0, b] = full[0:3]
            lru_prompt[0, b] = full[3]
        y_sample[4 * c:4 * c + 4] = r["y_smp"].reshape(4, 64, D)
        k_sample[0, 4 * c:4 * c + 4] = r["k_smp"].reshape(4, 64, 4, 256)
        v_sample[0, 4 * c:4 * c + 4] = r["v_smp"].reshape(4, 64, 4, 256)
        sts = r["st_smp"].reshape(128, 16, 4, 4)
        fulls = sts.transpose(2, 3, 1, 0).reshape(4, 4, 2048)
        conv_sample[0, 4 * c:4 * c + 4] = fulls[:, 0:3]
        lru_sample[0, 4 * c:4 * c + 4] = fulls[:, 3]
    return (y_prompt, y_sample, k_prompt, v_prompt, conv_prompt, lru_prompt,
            k_sample, v_sample, conv_sample, lru_sample)
```

```python
import math
import os
from contextlib import ExitStack

import numpy as np
import concourse.bass as bass
import concourse.mybir as mybir
from concourse.bass_utils import run_bass_kernel_spmd

F32 = mybir.dt.float32
BF16 = mybir.dt.bfloat16
AF = mybir.ActivationFunctionType
ALU = mybir.AluOpType

D = 4096
NPRE = 2048
NMAIN = 2048
NSMP = 256
NQ = NMAIN + NSMP
NK = NPRE + NMAIN + NSMP
PAST = 2048
EPS = 1e-6
LAM_INIT = 0.2
SCALE = 128 ** -0.5
NSLOT = 3
SAME_ENGINE_SYNC = True
GELU_C = 2.0 * math.sqrt(2.0 / math.pi)


class Buf:
    __slots__ = ("name", "w", "r")

    def __init__(self, name=""):
        self.name = name
        self.w = None
        self.r = {}


class DSem:
    def __init__(self, sem):
        self.sem = sem
        self.count = 0


class Ent:
    __slots__ = ("fn", "waits", "inc", "val", "dsem")

    def __init__(self, fn):
        self.fn = fn
        self.waits = []
        self.inc = False
        self.val = None
        self.dsem = None


class _Rec:
    def __getattr__(self, name):
        def f(*a, **kw):
            self.__dict__["call"] = (name, a, kw)
        return f


COMPUTE = ("pe", "dve", "act", "pool")
ENGS = ("pe", "dve", "act", "pool", "sp")


class Trk:
    def __init__(self, nc, csem):
        self.nc = nc
        self.csem = csem
        self.base = {e: 0 for e in COMPUTE}
        self.dsems = []
        self.epoch = 0
        self._reset()

    def _reset(self):
        self.streams = {e: [] for e in ENGS}
        self.waited = {e: {p: -1 for p in COMPUTE} for e in ENGS}
        self.waitedD = {e: {} for e in ENGS}

    def new_dsem(self, sem):
        d = DSem(sem)
        self.dsems.append(d)
        return d

    def _add_wait(self, eng, ent, ref):
        if ref is None or ref[0] != self.epoch:
            return
        if ref[1] == "c":
            _, _, peng, idx = ref
            if peng == eng and (eng in ("pe", "sp") or not SAME_ENGINE_SYNC):
                return
            if idx <= self.waited[eng][peng]:
                return
            self.waited[eng][peng] = idx
            self.streams[peng][idx].inc = True
            ent.waits.append(("c", peng, idx))
        else:
            ds = ref[2]
            cnt = ds.count
            if self.waitedD[eng].get(ds, 0) >= cnt:
                return
            self.waitedD[eng][ds] = cnt
            ent.waits.append(("d", ds, cnt))

    def _record(self, eng, ent, reads, writes, ref_maker):
        deps = []
        for b in reads:
            if b.w is not None:
                deps.append(b.w)
        for b in writes:
            if b.w is not None:
                deps.append(b.w)
            deps.extend(b.r.values())
        for d in deps:
            self._add_wait(eng, ent, d)
        idx = len(self.streams[eng])
        self.streams[eng].append(ent)
        ref = ref_maker(idx)
        key = ref[2] if ref[1] == "d" else eng
        for b in reads:
            b.r[key] = ref
        for b in writes:
            b.w = ref
            b.r = {}
        return ref

    def op(self, eng, fn, reads=(), writes=()):
        rec = _Rec()
        fn(rec)
        name, args, kw = rec.call
        ent = Ent(lambda e: getattr(e, name)(*args, **kw))
        return self._record(eng, ent, reads, writes,
                            lambda idx: (self.epoch, "c", eng, idx))

    def dma(self, eng, out, in_, reads, writes, dsem):
        ent = Ent(lambda e: e.dma_start(out=out, in_=in_))
        ent.dsem = dsem
        ref = self._record(eng, ent, reads, writes,
                           lambda idx: (self.epoch, "d", dsem))
        dsem.count += 16
        return ref

    def barrier(self):
        last = {}
        for e in COMPUTE:
            st = self.streams[e]
            for i in range(len(st) - 1, -1, -1):
                if st[i].dsem is None:
                    last[e] = i
                    break
        for eng in ENGS:
            ent = Ent(lambda e: e.nop())
            for p, i in last.items():
                if p != eng and i > self.waited[eng][p]:
                    self.streams[p][i].inc = True
                    ent.waits.append(("c", p, i))
            for ds in self.dsems:
                if ds.count > 0 and self.waitedD[eng].get(ds, 0) < ds.count:
                    ent.waits.append(("d", ds, ds.count))
            self.streams[eng].append(ent)
        self.epoch += 1

    def replay(self, block):
        nc = self.nc
        for e in COMPUTE:
            c = self.base[e]
            for ent in self.streams[e]:
                if ent.inc:
                    c += 1
                    ent.val = c
            self.base[e] = c
        streams = self.streams
        csem = self.csem

        def run(engname):
            def f(engobj):
                for ent in streams[engname]:
                    for w in ent.waits:
                        if w[0] == "c":
                            engobj.wait_ge(csem[w[1]], streams[w[1]][w[2]].val)
                        else:
                            engobj.wait_ge(w[1].sem, w[2])
                    ins = ent.fn(engobj)
                    if ent.dsem is not None:
                        ins.then_inc(ent.dsem.sem, 16)
                    elif ent.inc:
                        ins.then_inc(csem[engname], 1)
            return f

        block.tensor(run("pe"))
        block.vector(run("dve"))
        block.scalar(run("act"))
        block.gpsimd(run("pool"))
        block.sync(run("sp"))
        self._reset()


def build_program(upto=3):
    nc = bass.Bass("TRN2", target_bir_lowering=False)

    def din(name, shape, dt=F32):
        return nc.dram_tensor(name, list(shape), dt, kind="ExternalInput").ap()

    def dout(name, shape, dt=F32):
        return nc.dram_tensor(name, list(shape), dt, kind="ExternalOutput").ap()

    def dscr(name, shape, dt):
        return nc.dram_tensor(name, list(shape), dt).ap()

    xm = din("xm", [NMAIN, D])
    xpre = din("xpre", [NPRE, D])
    xs = din("xs", [NSMP, D])
    ck = din("ck", [4, PAST, 1024])
    cv = din("cv", [4, PAST, 1024])
    constT_d = din("constT", [128, 16 * 24])
    gvec_d = din("gvec", [128, 64])
    gfin_d = din("gfin", [128, D])
    lamv_d = din("lamv", [128, 512])
    subg_d = din("subg", [128, 256])
    valid_d = din("valid", [128, 1])
    ident_d = din("ident", [128, 128])
    cs_pre = din("cs_pre", [NPRE, 128])
    cs_main = din("cs_main", [NMAIN, 128])
    cs_smp = din("cs_smp", [NSMP, 128])
    w_in = din("w_in", [D, 8192])
    w_out = din("w_out", [D, D] if upto >= 3 else [128, 128])
    w_up = din("w_up", [D, 16384] if upto >= 3 else [128, 128])
    w_down = din("w_down", [16384, D] if upto >= 3 else [128, 128])
    ga_w = din("ga_w", [16, 128, 128])
    gx_w = din("gx_w", [16, 128, 128])

    y_main = dout("y_main", [NMAIN, D])
    y_smp = dout("y_smp", [NSMP, D])
    k_main = dout("k_main", [NMAIN, 1024])
    v_main = dout("v_main", [NMAIN, 1024])
    k_smp = dout("k_smp", [NSMP, 1024])
    v_smp = dout("v_smp", [NSMP, 1024])
    st_main = dout("st_main", [128, 64])
    st_smp = dout("st_smp", [128, 256])

    qT_s = dscr("qT_s", [16, 128, NQ], BF16)
    kT_s = dscr("kT_s", [8, 128, NK], BF16)
    v_s = dscr("v_s", [NK, 1024], BF16)
    mixT_s = dscr("mixT_s", [32, 128, NQ], BF16)

    top = ExitStack()
    sems = {e: top.enter_context(nc.semaphore("c_" + e)) for e in COMPUTE}
    T = Trk(nc, sems)
    nds = [0]

    def DS():
        nds[0] += 1
        return T.new_dsem(top.enter_context(nc.semaphore("d%d" % nds[0])))

    uniq = [0]

    def sb(stack, name, shape, dt):
        uniq[0] += 1
        return stack.enter_context(nc.sbuf_tensor("sb%d_%s" % (uniq[0], name), list(shape), dt))

    def ps(stack, name, shape, dt):
        uniq[0] += 1
        return stack.enter_context(nc.psum_tensor("ps%d_%s" % (uniq[0], name), list(shape), dt))

    MAXSLOT = 4
    slot_b = [Buf("slot%d" % i) for i in range(MAXSLOT)]
    slot_ds = [DS() for _ in range(MAXSLOT)]
    WS = {"slots": None, "n": 0, "plan": None, "issued": 0, "used": 0}

    def ws_begin(stack, n, plan_):
        WS["slots"] = [sb(stack, "wslot%d" % i, [128, 8192], BF16) for i in range(n)]
        WS["n"] = n
        WS["plan"] = plan_
        WS["issued"] = 0
        WS["used"] = 0
    ident = sb(top, "ident", [128, 128], BF16)
    gvec = sb(top, "gvec", [128, 64], F32)
    constT = sb(top, "constT", [128, 16, 24], F32)
    lamv = sb(top, "lamv", [128, 4, 128], F32)
    subg = sb(top, "subg", [128, 256], F32)
    sc = sb(top, "scal", [128, 16], F32)
    c12 = sb(top, "c12", [128, 2, 16], F32)
    B_const = Buf("consts")
    ds_const = DS()
    ds_const_p = DS()
    VALID = sc[:, 0:1]
    PREB = sc[:, 1:2]
    NLAM = sc[:, 3:4]

    IN_MAIN = ([("q", i) for i in range(4)] + [("k", i) for i in range(2)] + [("v", i) for i in range(2)]
               + [x for i in range(4) for x in (("xr", i), ("gr", i))])
    IN_PRE = [("k", 0), ("k", 1), ("v", 0), ("v", 1)] + [("xr", i) for i in range(4)]
    COL0 = {"q": 0, "k": 2048, "v": 3072, "xr": 4096, "gr": 6144}
    s1_tiles = ([("pre", i) for i in range(4)] + [("main", i) for i in range(4)] + [("smp", 0)])
    plan1 = []
    for kind, _ in s1_tiles:
        for (typ, bi) in (IN_PRE if kind == "pre" else IN_MAIN):
            for kh in range(2):
                plan1.append(("in", COL0[typ] + bi * 512, kh))
    s3_tiles = [("main", i) for i in range(4)] + [("smp", 0)]
    NSB = 32
    plan3 = []
    for _ in s3_tiles:
        for cb in range(8):
            for kh in range(2):
                plan3.append(("out", cb * 512, kh))
        for sbk in range(NSB):
            plan3.append(("upA", sbk))
            plan3.append(("upB", sbk))
            if sbk > 0:
                plan3.append(("dA", sbk - 1))
                plan3.append(("dB", sbk - 1))
        plan3.append(("dA", NSB - 1))
        plan3.append(("dB", NSB - 1))

    def w_issue(i):
        d = WS["plan"][i]
        s_ = i % WS["n"]
        slot = WS["slots"][s_]
        if d[0] == "in" or d[0] == "out":
            w = w_in if d[0] == "in" else w_out
            src = w[d[2] * 2048:(d[2] + 1) * 2048, d[1]:d[1] + 512].rearrange("(kc p) c -> p kc c", p=128)
            dst = slot[:].rearrange("p (kc c) -> p kc c", kc=16)
        elif d[0] in ("upA", "upB"):
            r0_ = 0 if d[0] == "upA" else 2048
            src = w_up[r0_:r0_ + 2048, d[1] * 512:(d[1] + 1) * 512].rearrange("(kc p) c -> p kc c", p=128)
            dst = slot[:].rearrange("p (kc c) -> p kc c", kc=16)
        else:
            r0_ = d[1] * 512 + (0 if d[0] == "dA" else 256)
            src = w_down[r0_:r0_ + 256, :].rearrange("(f p) c -> p f c", p=128)
            dst = slot[:].rearrange("p (f c) -> p f c", f=2)
        T.dma("pool", dst, src, (), (slot_b[s_],), slot_ds[s_])

    def w_next(desc):
        i = WS["used"]
        plan_ = WS["plan"]
        if os.environ.get("DBG_TILES") is None:
            assert plan_[i] == desc, (plan_[i], desc)
        else:
            plan_[i] = desc
        while WS["issued"] < min(len(plan_), i + 3):
            w_issue(WS["issued"])
            WS["issued"] += 1
        WS["used"] += 1
        s_ = i % WS["n"]
        return WS["slots"][s_], slot_b[s_]

    st = ExitStack()
    ws_begin(st, 3, plan1)
    xbuf = [sb(st, "xbuf%d" % i, [128, D], F32) for i in range(2)]
    xbuf_b = [Buf("xbuf%d" % i) for i in range(2)]
    xbuf_ds = [DS() for _ in range(2)]
    xsb = sb(st, "xsb", [128, D], BF16)
    xsb_b = Buf("xsb")
    xnT = sb(st, "xnT", [128, 32, 512], BF16)
    xnT_b = [Buf("xnT%d" % g) for g in range(4)]
    ssq = sb(st, "ssq", [128, 4], F32)
    ssq_b = Buf("ssq")
    cst = sb(st, "cst", [128, 4, 128], F32)
    cst_b = Buf("cst")
    cst_ds = DS()
    qk32 = [sb(st, "qk32_%d" % i, [128, 512], F32) for i in range(4)]
    qk32_b = [Buf() for _ in range(4)]
    qk32_ds = [DS() for _ in range(4)]
    rtmp = sb(st, "rtmp", [128, 4, 4, 16], F32)
    rtmp_b = Buf()
    qkbf = [sb(st, "qkbf%d" % i, [128, 512], BF16) for i in range(4)]
    qkbf_b = [Buf() for _ in range(4)]
    qkT_st = [sb(st, "qkTst%d" % i, [128, 4, 128], BF16) for i in range(4)]
    qkT_b = [Buf() for _ in range(4)]
    qkT_ds = [DS() for _ in range(4)]
    v32, v32_b, v32_ds = qk32, qk32_b, qk32_ds
    vbf_ds = [DS() for _ in range(4)]
    gaw = sb(st, "gaw", [128, 16, 128], BF16)
    gxw = sb(st, "gxw", [128, 16, 128], BF16)
    HL = sb(st, "HL", [128, 16, 4, 3], F32)
    HST = sb(st, "HST", [128, 16, 4], F32)
    HL_b = [Buf() for _ in range(16)]
    HST_b = [Buf() for _ in range(16)]
    stout = sb(st, "stout", [128, 16, 4], F32)
    stout_s = sb(st, "stout_s", [128, 16, 4, 4], F32)
    stout_b = Buf()
    stout_ds = DS()

    def L(name, shape, dt=F32):
        return sb(st, name, shape, dt), Buf(name)

    xh4 = sb(st, "xh4", [128, 4, 528], F32)
    xh4_b = [Buf() for _ in range(4)]
    xc4 = sb(st, "xc4", [128, 4, 512], F32)
    xc4_b = [Buf() for _ in range(4)]
    xcb4 = sb(st, "xcb4", [128, 4, 512], BF16)
    xcb4_b = [Buf() for _ in range(4)]
    rr, rr_b = L("rr", [128, 512])
    ii, ii_b = L("ii", [128, 512])
    aa, aa_b = L("aa", [128, 512])
    mm_, mm_b = L("mm", [128, 512])
    uu, uu_b = L("uu", [128, 512])
    hs4 = sb(st, "hs4", [128, 4, 512], F32)
    hs4_b = [Buf() for _ in range(4)]
    gr4 = sb(st, "gr4", [128, 4, 512], F32)
    gr4_b = [Buf() for _ in range(4)]
    g1, g1_b = ii, ii_b
    g2, g2_b = aa, aa_b
    ylb = [sb(st, "ylb%d" % i, [128, 512], BF16) for i in range(2)]
    ylb_b = [Buf() for _ in range(2)]
    ylb_ds = [DS() for _ in range(2)]

    ptx = [ps(st, "ptx%d" % i, [128, 1024], BF16) for i in range(2)]
    ptx_b = [Buf() for _ in range(2)]
    pp = [ps(st, "pp%d" % i, [128, 512], F32) for i in range(4)]
    pp_b = [Buf() for _ in range(4)]
    pg = [ps(st, "pg%d" % i, [128, 512], F32) for i in range(2)]
    pg_b = [Buf() for _ in range(2)]

    T.dma("sp", gvec[:], gvec_d[:, :], (), (B_const,), ds_const)
    T.dma("sp", constT[:].rearrange("p c k -> p (c k)"), constT_d[:, :], (), (B_const,), ds_const)
    T.dma("sp", lamv[:].rearrange("p a d -> p (a d)"), lamv_d[:, :], (), (B_const,), ds_const)
    T.dma("sp", subg[:], subg_d[:, :], (), (B_const,), ds_const)
    T.dma("sp", sc[:, 0:1], valid_d[:, :], (), (B_const,), ds_const)
    T.dma("pool", ident[:], ident_d[:, :], (), (B_const,), ds_const_p)
    T.dma("pool", gaw[:], ga_w.rearrange("n c d -> c n d"), (), (B_const,), ds_const_p)
    T.dma("pool", gxw[:], gx_w.rearrange("n c d -> c n d"), (), (B_const,), ds_const_p)
    CB = (B_const,)
    T.op("dve", lambda e: e.tensor_scalar(out=sc[:, 1:2], in0=sc[:, 0:1], scalar1=-1.0, scalar2=30000.0,
                                          op0=ALU.add, op1=ALU.mult), CB, CB)
    T.op("dve", lambda e: e.tensor_tensor(out=lamv[:, 0, :], in0=lamv[:, 0, :], in1=lamv[:, 1, :], op=ALU.mult), CB, CB)
    T.op("dve", lambda e: e.tensor_tensor(out=lamv[:, 2, :], in0=lamv[:, 2, :], in1=lamv[:, 3, :], op=ALU.mult), CB, CB)
    T.op("dve", lambda e: e.tensor_reduce(out=sc[:, 4:5], in_=lamv[:, 0, :], axis=mybir.AxisListType.X, op=ALU.add), CB, CB)
    T.op("dve", lambda e: e.tensor_reduce(out=sc[:, 5:6], in_=lamv[:, 2, :], axis=mybir.AxisListType.X, op=ALU.add), CB, CB)
    T.op("act", lambda e: e.activation(out=sc[:, 4:6], in_=sc[:, 4:6], func=AF.Exp), CB, CB)
    T.op("dve", lambda e: e.tensor_tensor(out=sc[:, 2:3], in0=sc[:, 4:5], in1=sc[:, 5:6], op=ALU.subtract), CB, CB)
    T.op("dve", lambda e: e.tensor_scalar(out=sc[:, 2:3], in0=sc[:, 2:3], scalar1=LAM_INIT, scalar2=None, op0=ALU.add), CB, CB)
    T.op("dve", lambda e: e.tensor_scalar(out=sc[:, 3:4], in0=sc[:, 2:3], scalar1=-1.0, scalar2=None, op0=ALU.mult), CB, CB)
    T.op("dve", lambda e: e.tensor_scalar(out=subg[:], in0=subg[:], scalar1=1.0 - LAM_INIT, scalar2=None, op0=ALU.mult), CB, CB)
    T.op("act", lambda e: e.activation(out=c12[:, 0, :], in_=constT[:, :, 7], func=AF.Exp, scale=-1.0), CB, CB)
    T.op("act", lambda e: e.activation(out=c12[:, 0, :], in_=c12[:, 0, :], func=AF.Ln, bias=1.0), CB, CB)
    T.op("dve", lambda e: e.tensor_scalar(out=c12[:, 1, :], in0=c12[:, 0, :], scalar1=-16.0, scalar2=None, op0=ALU.mult), CB, CB)
    T.op("dve", lambda e: e.tensor_scalar(out=c12[:, 0, :], in0=c12[:, 0, :], scalar1=-8.0, scalar2=None, op0=ALU.mult), CB, CB)
    for n in range(16):
        T.op("dve", lambda e, n=n: e.memset(HL[:, n, 0, :], 0.0), (), (HL_b[n],))
        T.op("dve", lambda e, n=n: e.memset(HST[:, n, 0:1], 0.0), (), (HST_b[n],))

    def tile_info(kind, ti):
        if kind == "pre":
            return dict(x=xpre, r0=ti * 512, NT=512, cs=cs_pre, qc=None, kc=ti * 512, ko=None, vo=None)
        if kind == "main":
            return dict(x=xm, r0=ti * 512, NT=512, cs=cs_main, qc=ti * 512, kc=NPRE + ti * 512, ko=k_main, vo=v_main)
        return dict(x=xs, r0=0, NT=256, cs=cs_smp, qc=NMAIN, kc=NPRE + NMAIN, ko=k_smp, vo=v_smp)

    xloads = []
    for kind, ti in s1_tiles:
        inf = tile_info(kind, ti)
        for g in range(inf["NT"] // 128):
            xloads.append((inf["x"], inf["r0"] + g * 128))
    xl = {"i": 0}

    def x_prefetch(upto):
        while xl["i"] <= min(upto, len(xloads) - 1):
            i = xl["i"]
            src, r = xloads[i]
            T.dma("sp", xbuf[i % 2][:], src[r:r + 128, :], (), (xbuf_b[i % 2],), xbuf_ds[i % 2])
            xl["i"] += 1

    def rstd_from_ss(ss_ap, n, bufs):
        T.op("dve", lambda e: e.tensor_scalar(out=ss_ap, in0=ss_ap, scalar1=1.0 / n, scalar2=EPS,
                                              op0=ALU.mult, op1=ALU.add), bufs, bufs)
        T.op("act", lambda e: e.activation(out=ss_ap, in_=ss_ap, func=AF.Sqrt), bufs, bufs)
        T.op("dve", lambda e: e.reciprocal(out=ss_ap, in_=ss_ap), bufs, bufs)

    def norm_transpose(src_tile, src_b, dstT, dst_b, g, gcol0, ptxs, ptxs_b, junk, junk_b, ssq_ap, ssq_buf):
        T.op("act", lambda e: e.activation(out=junk[:], in_=src_tile, func=AF.Square, accum_out=ssq_ap),
             (src_b,), (junk_b, ssq_buf))
        rstd_from_ss(ssq_ap, float(D), (ssq_buf,))
        T.op("dve", lambda e: e.tensor_scalar_mul(out=junk[:], in0=src_tile, scalar1=ssq_ap),
             (src_b, ssq_buf), (junk_b,))
        for q4 in range(4):
            pt = ptxs[q4 % 2]
            ptb = ptxs_b[q4 % 2]
            for j in range(8):
                kc = q4 * 8 + j
                T.op("pe", lambda e, kc=kc, j=j, pt=pt: e.transpose(out=pt[:, j * 128:(j + 1) * 128],
                                                                    in_=junk[:, kc * 128:(kc + 1) * 128], identity=ident[:]),
                     (junk_b, B_const), (ptb,))
            gsl = gvec[:, gcol0 + q4 * 8:gcol0 + q4 * 8 + 8].unsqueeze(2).to_broadcast([128, 8, 128])
            T.op("dve", lambda e, pt=pt, q4=q4, gsl=gsl: e.tensor_tensor(
                out=dstT[:, q4 * 8:(q4 + 1) * 8, g * 128:(g + 1) * 128],
                in0=pt[:].rearrange("p (k t) -> p k t", k=8), in1=gsl, op=ALU.mult),
                (ptb, B_const), (dst_b,))

    deferred = []

    def flush_deferred():
        while deferred:
            deferred.pop(0)()

    first_main = [True]
    gcount = 0
    qkT_i = 0
    alt = {"qk": 0, "v": 0, "y": 0}
    dbg_tiles = int(os.environ.get("DBG_TILES", "99"))
    XR = int(os.environ.get("DBG_XR", "99"))
    dbg_blocks = int(os.environ.get("DBG_BLOCKS", "99"))
    for kind, ti in s1_tiles[:dbg_tiles]:
        inf = tile_info(kind, ti)
        NT = inf["NT"]
        NG = NT // 128
        nseg = 4 if kind == "smp" else 1
        Lseg = NT // nseg
        blocks = (IN_PRE if kind == "pre" else IN_MAIN)[:dbg_blocks]
        if kind == "main" and first_main[0]:
            first_main[0] = False
            for n in range(16):
                T.op("dve", lambda e, n=n: e.tensor_scalar_mul(out=HL[:, n, 0, :], in0=HL[:, n, 0, :], scalar1=VALID),
                     (HL_b[n], B_const), (HL_b[n],))
                T.op("dve", lambda e, n=n: e.tensor_scalar_mul(out=HST[:, n, 0:1], in0=HST[:, n, 0:1], scalar1=VALID),
                     (HST_b[n], B_const), (HST_b[n],))
        if kind == "smp":
            for n in range(16):
                T.op("dve", lambda e, n=n: e.tensor_copy(out=stout[:, n, 0:3], in_=HL[:, n, 0, :]), (HL_b[n],), (stout_b,))
                T.op("dve", lambda e, n=n: e.tensor_copy(out=stout[:, n, 3:4], in_=HST[:, n, 0:1]), (HST_b[n],), (stout_b,))
            T.dma("sp", st_main[:, :], stout[:].rearrange("p c k -> p (c k)"), (stout_b,), (), stout_ds)
            for n in range(16):
                T.op("dve", lambda e, n=n: e.tensor_copy(
                    out=HL[:, n, :, :], in_=constT[:, n, 8:20].rearrange("p (s j) -> p s j", s=4)), CB, (HL_b[n],))
                T.op("dve", lambda e, n=n: e.tensor_copy(out=HST[:, n, :], in_=constT[:, n, 20:24]), CB, (HST_b[n],))
        T.dma("sp", cst[:, 0:NG, :], inf["cs"][inf["r0"]:inf["r0"] + NT, :].rearrange("(g p) c -> p g c", p=128),
              (), (cst_b,), cst_ds)
        for g in range(NG):
            x_prefetch(gcount + 1)
            xb = xbuf[gcount % 2]
            norm_transpose(xb[:], xbuf_b[gcount % 2], xnT, xnT_b[g], g, 0, ptx, ptx_b, xsb, xsb_b,
                           ssq[:, g:g + 1], ssq_b)
            gcount += 1
        for (typ, bi) in blocks:
            col0 = COL0[typ] + bi * 512
            for kh in range(2):
                slot, slb = w_next(("in", col0, kh))
                sl3 = slot[:].rearrange("p (kc c) -> p kc c", kc=16)
                if typ in ("q", "k", "v"):
                    for g in range(NG):
                        for kc in range(16):
                            T.op("pe", lambda e, g=g, kc=kc, kh=kh, sl3=sl3: e.matmul(
                                pp[g][:, :], lhsT=xnT[:, kh * 16 + kc, g * 128:(g + 1) * 128], rhs=sl3[:, kc, :],
                                start=(kh == 0 and kc == 0), stop=(kh == 1 and kc == 15)),
                                (xnT_b[g], slb), (pp_b[g],))
                else:
                    for cc in range(4):
                        for kc in range(16):
                            T.op("pe", lambda e, cc=cc, kc=kc, kh=kh, sl3=sl3: e.matmul(
                                pp[cc][:, 0:NT], lhsT=sl3[:, kc, cc * 128:(cc + 1) * 128], rhs=xnT[:, kh * 16 + kc, 0:NT],
                                start=(kh == 0 and kc == 0), stop=(kh == 1 and kc == 15)),
                                tuple(xnT_b[:NG]) + (slb,), (pp_b[cc],))
            if typ in ("q", "k"):
                sub0 = (bi * 4) if typ == "q" else (16 + bi * 4)
                for g in range(NG):
                    T.op("act", lambda e, g=g: e.activation(out=qk32[g][:], in_=pp[g][:, :], func=AF.Copy),
                         (pp_b[g],), (qk32_b[g],))
                flush_deferred()
                for g in range(NG):
                    a_ = g
                    q32, q32b = qk32[g], qk32_b[g]
                    q3 = q32[:].rearrange("p (s d) -> p s d", s=4)
                    x1 = q3[:, :, 0:16]
                    x2 = q3[:, :, 16:32]
                    cos = cst[:, g, 0:64].rearrange("p (s d) -> p s d", s=4)
                    sin = cst[:, g, 64:128].rearrange("p (s d) -> p s d", s=4)
                    rb = (q32b, cst_b)
                    T.op("dve", lambda e, x1=x1, cos=cos: e.tensor_tensor(out=rtmp[:, 0], in0=x1, in1=cos, op=ALU.mult), rb, (rtmp_b,))
                    T.op("dve", lambda e, x2=x2, sin=sin: e.tensor_tensor(out=rtmp[:, 1], in0=x2, in1=sin, op=ALU.mult), rb, (rtmp_b,))
                    T.op("dve", lambda e, x2=x2, cos=cos: e.tensor_tensor(out=rtmp[:, 2], in0=x2, in1=cos, op=ALU.mult), rb, (rtmp_b,))
                    T.op("dve", lambda e, x1=x1, sin=sin: e.tensor_tensor(out=rtmp[:, 3], in0=x1, in1=sin, op=ALU.mult), rb, (rtmp_b,))
                    T.op("dve", lambda e, x1=x1: e.tensor_tensor(out=x1, in0=rtmp[:, 0], in1=rtmp[:, 1], op=ALU.subtract),
                         (rtmp_b,), (q32b,))
                    T.op("dve", lambda e, x2=x2: e.tensor_tensor(out=x2, in0=rtmp[:, 2], in1=rtmp[:, 3], op=ALU.add),
                         (rtmp_b,), (q32b,))
                    if typ == "k" and inf["ko"] is not None:
                        r0 = inf["r0"] + g * 128
                        T.dma("sp", inf["ko"][r0:r0 + 128, bi * 512:(bi + 1) * 512], q32[:], (q32b,), (), qk32_ds[a_])
                    qb, qbb = qkbf[g], qkbf_b[g]
                    T.op("act", lambda e, q32=q32, qb=qb: e.activation(out=qb[:], in_=q32[:], func=AF.Copy), (q32b,), (qbb,))

                    def tr(g=g, qb=qb, qbb=qbb, sub0=sub0, typ=typ, bi=bi, last=(typ == "k" and bi == 1)):
                        pt, ptb = ptx[g % 2], ptx_b[g % 2]
                        for s in range(4):
                            T.op("pe", lambda e, s=s: e.transpose(out=pt[:, s * 128:(s + 1) * 128],
                                                                  in_=qb[:, s * 128:(s + 1) * 128], identity=ident[:]),
                                 (qbb, B_const), (ptb,))
                        st_t, st_b = qkT_st[g], qkT_b[g]
                        T.op("act", lambda e: e.activation(out=st_t[:, :, :],
                                                           in_=pt[:, 0:512].rearrange("p (s t) -> p s t", s=4), func=AF.Copy),
                             (ptb,), (st_b,))
                        if typ == "q":
                            c0 = inf["qc"] + g * 128
                            T.dma("sp", qT_s[sub0:sub0 + 4, :, c0:c0 + 128].rearrange("s d t -> d s t"), st_t[:, :, :],
                                  (st_b,), (), qkT_ds[g])
                        else:
                            c0 = inf["kc"] + g * 128
                            T.dma("sp", kT_s[sub0 - 16:sub0 - 12, :, c0:c0 + 128].rearrange("s d t -> d s t"), st_t[:, :, :],
                                  (st_b,), (), qkT_ds[g])
                    deferred.append(tr)
            elif typ == "v":
                for g in range(NG):
                    T.op("act", lambda e, g=g: e.activation(out=v32[g][:], in_=pp[g][:, :], func=AF.Copy),
                         (pp_b[g],), (v32_b[g],))
                flush_deferred()
                for g in range(NG):
                    a_ = g
                    T.op("dve", lambda e, g=g, a_=a_: e.tensor_copy(out=qkbf[g][:], in_=v32[a_][:]),
                         (v32_b[a_],), (qkbf_b[g],))
                    if inf["vo"] is not None:
                        r0 = inf["r0"] + g * 128
                        T.dma("sp", inf["vo"][r0:r0 + 128, bi * 512:(bi + 1) * 512], v32[a_][:], (v32_b[a_],), (), v32_ds[a_])
                    r0 = inf["kc"] + g * 128
                    T.dma("sp", v_s[r0:r0 + 128, bi * 512:(bi + 1) * 512], qkbf[g][:], (qkbf_b[g],), (), vbf_ds[g])
            elif typ == "xr":
                W = 3 + Lseg

                def v3(t):
                    return t[:, 0:NT].rearrange("p (s l) -> p s l", s=nseg)

                for cc in range(4):
                    xh3 = xh4[:, cc, 0:nseg * W].rearrange("p (s w) -> p s w", s=nseg)
                    T.op("act", lambda e, cc=cc, xh3=xh3: e.activation(out=xh3[:, :, 3:W], in_=v3(pp[cc]), func=AF.Copy),
                         (pp_b[cc],), (xh4_b[cc],))
                flush_deferred()
                for cc in range(4):
                    n = bi * 4 + cc
                    xh3 = xh4[:, cc, 0:nseg * W].rearrange("p (s w) -> p s w", s=nseg)
                    xh_b = xh4_b[cc]
                    if XR < 1:
                        continue
                    T.op("dve", lambda e, n=n: e.tensor_copy(out=xh3[:, :, 0:3], in_=HL[:, n, 0:nseg, :]), (HL_b[n],), (xh_b,))
                    T.op("dve", lambda e, n=n: e.tensor_copy(out=HL[:, n, 0:nseg, :], in_=xh3[:, :, W - 3:W]), (xh_b,), (HL_b[n],))
                    if kind == "smp":
                        T.op("dve", lambda e, n=n: e.tensor_copy(out=stout_s[:, n, :, 0:3], in_=xh3[:, :, W - 3:W]),
                             (xh_b,), (stout_b,))
                    if XR < 2:
                        continue
                    cw = constT[:, n, :]
                    xc, xc_b, xcb, xcb_b = xc4[:, cc, :], xc4_b[cc], xcb4[:, cc, :], xcb4_b[cc]
                    T.op("dve", lambda e, cw=cw, xc=xc: e.tensor_scalar(out=v3(xc), in0=xh3[:, :, 3:W], scalar1=cw[:, 3:4],
                                                                 scalar2=cw[:, 4:5], op0=ALU.mult, op1=ALU.add),
                         (xh_b, B_const), (xc_b,))
                    for j in range(3):
                        T.op("dve", lambda e, cw=cw, j=j, xc=xc: e.scalar_tensor_tensor(
                            out=v3(xc), in0=xh3[:, :, j:j + Lseg], scalar=cw[:, j:j + 1], in1=v3(xc),
                            op0=ALU.mult, op1=ALU.add), (xh_b, xc_b, B_const), (xc_b,))
                    T.op("act", lambda e, xc=xc, xcb=xcb: e.activation(out=xcb[:, 0:NT], in_=xc[:, 0:NT], func=AF.Copy), (xc_b,), (xcb_b,))

                    if XR < 3:
                        continue
                    def gates(n=n, cc=cc, cw=cw, xc=xc, xc_b=xc_b, xcb=xcb, xcb_b=xcb_b):
                        T.op("pe", lambda e: e.matmul(pg[0][:, 0:NT], lhsT=gaw[:, n, :], rhs=xcb[:, 0:NT], start=True, stop=True),
                             (xcb_b, B_const), (pg_b[0],))
                        T.op("pe", lambda e: e.matmul(pg[1][:, 0:NT], lhsT=gxw[:, n, :], rhs=xcb[:, 0:NT], start=True, stop=True),
                             (xcb_b, B_const), (pg_b[1],))
                        T.op("act", lambda e: e.activation(out=rr[:, 0:NT], in_=pg[0][:, 0:NT], func=AF.Sigmoid, bias=cw[:, 5:6]),
                             (pg_b[0], B_const), (rr_b,))
                        T.op("act", lambda e: e.activation(out=ii[:, 0:NT], in_=pg[1][:, 0:NT], func=AF.Sigmoid, bias=cw[:, 6:7]),
                             (pg_b[1], B_const), (ii_b,))
                        if XR < 4:
                            return
                        T.op("act", lambda e: e.activation(out=aa[:, 0:NT], in_=rr[:, 0:NT], func=AF.Exp, scale=c12[:, 0, n:n + 1]),
                             (rr_b, B_const), (aa_b,))
                        T.op("act", lambda e: e.activation(out=mm_[:, 0:NT], in_=rr[:, 0:NT], func=AF.Exp, scale=c12[:, 1, n:n + 1]),
                             (rr_b, B_const), (mm_b,))
                        T.op("act", lambda e: e.activation(out=mm_[:, 0:NT], in_=mm_[:, 0:NT], func=AF.Sqrt, bias=1.0, scale=-1.0),
                             (mm_b,), (mm_b,))
                        T.op("dve", lambda e: e.tensor_tensor(out=uu[:, 0:NT], in0=ii[:, 0:NT], in1=xc[:, 0:NT], op=ALU.mult),
                             (ii_b, xc_b), (uu_b,))
                        T.op("dve", lambda e: e.tensor_tensor(out=uu[:, 0:NT], in0=uu[:, 0:NT], in1=mm_[:, 0:NT], op=ALU.mult),
                             (uu_b, mm_b), (uu_b,))
                        if XR < 5:
                            return
                        for s in range(nseg):
                            T.op("dve", lambda e, s=s: e.tensor_tensor_scan(
                                out=hs4[:, cc, s * Lseg:(s + 1) * Lseg], data0=aa[:, s * Lseg:(s + 1) * Lseg],
                                data1=uu[:, s * Lseg:(s + 1) * Lseg], initial=HST[:, n, s:s + 1], op0=ALU.mult, op1=ALU.add),
                                (aa_b, uu_b, HST_b[n]), (hs4_b[cc],))
                        if XR < 6:
                            return
                        T.op("dve", lambda e: e.tensor_copy(
                            out=HST[:, n, 0:nseg],
                            in_=hs4[:, cc, 0:NT].rearrange("p (s l) -> p s l", s=nseg)[:, :, Lseg - 1]),
                            (hs4_b[cc],), (HST_b[n],))
                        if kind == "smp":
                            T.op("dve", lambda e: e.tensor_copy(out=stout_s[:, n, :, 3], in_=HST[:, n, :]), (HST_b[n],), (stout_b,))
                    deferred.append(gates)
            else:
                for cc in range(4):
                    T.op("act", lambda e, cc=cc: e.activation(out=gr4[:, cc, 0:NT], in_=pp[cc][:, 0:NT], func=AF.Copy),
                         (pp_b[cc],), (gr4_b[cc],))
                flush_deferred()
                for cc in range(4):
                    n = bi * 4 + cc
                    gr32, gr32_b = gr4[:, cc, :], gr4_b[cc]
                    T.op("act", lambda e: e.activation(out=g1[:, 0:NT], in_=gr32[:, 0:NT], func=AF.Square), (gr32_b,), (g1_b,))
                    T.op("dve", lambda e: e.tensor_scalar(out=g1[:, 0:NT], in0=g1[:, 0:NT], scalar1=0.044715, scalar2=1.0,
                                                          op0=ALU.mult, op1=ALU.add), (g1_b,), (g1_b,))
                    T.op("dve", lambda e: e.tensor_tensor(out=g1[:, 0:NT], in0=g1[:, 0:NT], in1=gr32[:, 0:NT], op=ALU.mult),
                         (g1_b, gr32_b), (g1_b,))
                    T.op("act", lambda e: e.activation(out=g2[:, 0:NT], in_=g1[:, 0:NT], func=AF.Sigmoid, scale=GELU_C),
                         (g1_b,), (g2_b,))
                    T.op("dve", lambda e: e.tensor_tensor(out=g2[:, 0:NT], in0=g2[:, 0:NT], in1=gr32[:, 0:NT], op=ALU.mult),
                         (g2_b, gr32_b), (g2_b,))

                    def ymul(n=n, cc=cc):
                        a_ = alt["y"] % 2
                        alt["y"] += 1
                        T.op("dve", lambda e: e.tensor_tensor(out=ylb[a_][:, 0:NT], in0=g2[:, 0:NT], in1=hs4[:, cc, 0:NT], op=ALU.mult),
                             (g2_b, hs4_b[cc]), (ylb_b[a_],))
                        T.dma("sp", mixT_s[16 + n, :, inf["qc"]:inf["qc"] + NT], ylb[a_][:, 0:NT], (ylb_b[a_],), (), ylb_ds[a_])
                    flush_deferred()
                    ymul()
        flush_deferred()
    T.dma("sp", st_smp[:, :], stout_s[:].rearrange("p c s k -> p (c s k)"), (stout_b,), (), stout_ds)
    T.barrier()
    with nc.Block() as block:
        T.replay(block)
    st.close()
    if upto == 1:
        top.close()
        return nc

    st = ExitStack()
    kTt = [sb(st, "kTt%d" % i, [128, 2, 4096], BF16) for i in range(2)]
    vat = [sb(st, "vat%d" % i, [128, 32, 258], BF16) for i in range(2)]
    qTt = [sb(st, "qTt%d" % i, [128, 4, 2048], BF16) for i in range(2)]
    kvq_b = [(Buf(), Buf(), Buf()) for _ in range(2)]
    kvq_ds = [DS() for _ in range(2)]
    kvq_dsp = [DS() for _ in range(2)]
    kcbs = [sb(st, "kcb%d" % i, [128, 16, 256], BF16) for i in range(2)]
    kcbs_b = [Buf() for _ in range(2)]
    kcbs_ds = [DS() for _ in range(2)]
    qparts_b = [[Buf() for _ in range(4)] for _ in range(2)]
    pT = [sb(st, "pT%d" % i, [128, 2, 256], BF16) for i in range(3)]
    pT_b = [Buf() for _ in range(3)]
    o32 = sb(st, "o32", [128, 4, 260], F32)
    o32_b = Buf()
    d32s = [sb(st, "d32_%d" % i, [128, 2, 256], F32) for i in range(2)]
    d32s_b = [Buf() for _ in range(2)]
    rss = [sb(st, "rs%d" % i, [128, 8], F32) for i in range(2)]
    rss_b = [Buf() for _ in range(2)]
    fin_pending = []
    onb = sb(st, "onb", [128, 2, 256], BF16)
    onb_b = Buf()
    junk2 = sb(st, "junk2", [128, 256], F32)
    junk2_b = Buf()
    oT_st = [sb(st, "oTst%d" % i, [128, 2, 256], BF16) for i in range(2)]
    oT_b = [Buf() for _ in range(2)]
    oT_ds = [DS() for _ in range(2)]
    psc = [ps(st, "psc%d" % i, [128, 512], F32) for i in range(3)]
    psc_b = [Buf() for _ in range(3)]
    po = [ps(st, "po%d" % i, [128, 512], F32) for i in range(4)]
    po_b = [Buf() for _ in range(4)]
    pt2_0 = ps(st, "pt2_0", [128, 1024], BF16)
    pt2 = [pt2_0, pt2_0]
    pt2_0b = Buf()
    pt2_b = [pt2_0b, pt2_0b]
    for i in range(2):
        T.op("dve", lambda e, i=i: e.memset(vat[i][:, :, 256:258], 1.0), (), (kvq_b[i][1],))
    acount = {"o": 0, "blk": 0}

    def attn_block(kT_of, v_of, q_of, bias_of, nblocks, band0, NS, kbuf, qbuf, out_store):
        QW = NS * 128
        items = []
        for j in range(nblocks):
            spj = (j - band0) if band0 is not None else -1
            col0 = 128 * spj if spj > 0 else 0
            items.append((j, spj, col0))

        def qk(it):
            j, spj, col0 = it
            pb = acount["blk"] % 3
            for c in range(2):
                kt = kT_of(c, j)
                nk = kt.shape[1]
                T.op("pe", lambda e, c=c, kt=kt, nk=nk, col0=col0, pb=pb: e.matmul(
                    psc[pb][0:nk, c * 256 + col0:c * 256 + QW], lhsT=kt, rhs=q_of(c, col0), start=True, stop=True),
                    tuple(kbuf) + tuple(qbuf), (psc_b[pb],))
            nk = kT_of(0, j).shape[1]
            bj = bias_of(j)
            T.op("act", lambda e, pb=pb, nk=nk, col0=col0, bj=bj: e.activation(
                out=pT[pb][0:nk, :, col0:QW],
                in_=psc[pb][0:nk, :].rearrange("p (c t) -> p c t", c=2)[:, :, col0:QW],
                func=AF.Exp, scale=SCALE, bias=bj), (psc_b[pb], B_const), (pT_b[pb],))
            if spj >= 0:
                T.op("dve", lambda e, pb=pb, spj=spj: e.memset(pT[pb][64:128, :, spj * 128:spj * 128 + 64], 0.0),
                     (), (pT_b[pb],))
            acount["blk"] += 1
            return pb

        def pv(it, pb):
            j, spj, col0 = it
            vv, nk = v_of(j)
            for c in range(2):
                for s in range(max(spj, 0), NS):
                    last = (band0 + s) if band0 is not None else nblocks - 1
                    T.op("pe", lambda e, c=c, s=s, vv=vv, nk=nk, pb=pb, j=j, last=last: e.matmul(
                        po[c * 2 + s][:, 0:257], lhsT=pT[pb][0:nk, c, s * 128:(s + 1) * 128], rhs=vv,
                        start=(j == 0), stop=(j == last)), (pT_b[pb],) + tuple(kbuf), (po_b[c * 2 + s],))

        pend_pv = []
        for it in items:
            pb = qk(it)
            pend_pv.append((it, pb))
            if len(pend_pv) > 2:
                pv(*pend_pv.pop(0))
        while pend_pv:
            pv(*pend_pv.pop(0))
        fin_flush(0)
        for c in range(2):
            for s in range(NS):
                i4 = c * 2 + s
                if i4 % 2 == 0:
                    T.op("act", lambda e, i4=i4: e.activation(out=o32[:, i4, 0:257], in_=po[i4][:, 0:257], func=AF.Copy),
                         (po_b[i4],), (o32_b,))
                else:
                    T.op("dve", lambda e, i4=i4: e.tensor_copy(out=o32[:, i4, 0:257], in_=po[i4][:, 0:257]),
                         (po_b[i4],), (o32_b,))
        fi = acount["o"] % 2
        d32, d32_b, rs, rs_b = d32s[fi], d32s_b[fi], rss[fi], rss_b[fi]
        ob = (o32_b, rs_b)
        for s in range(NS):
            T.op("dve", lambda e, s=s: e.reciprocal(out=rs[:, 0:1], in_=o32[:, s, 256:257]), (o32_b,), (rs_b,))
            T.op("dve", lambda e, s=s: e.reciprocal(out=rs[:, 1:2], in_=o32[:, 2 + s, 256:257]), (o32_b,), (rs_b,))
            T.op("dve", lambda e: e.tensor_tensor(out=rs[:, 1:2], in0=rs[:, 1:2], in1=NLAM, op=ALU.mult), (rs_b, B_const), (rs_b,))
            T.op("dve", lambda e, s=s: e.tensor_scalar_mul(out=d32[:, s, :], in0=o32[:, s, 0:256], scalar1=rs[:, 0:1]), ob, (d32_b,))
            T.op("dve", lambda e, s=s: e.scalar_tensor_tensor(out=d32[:, s, :], in0=o32[:, 2 + s, 0:256], scalar=rs[:, 1:2],
                                                              in1=d32[:, s, :], op0=ALU.mult, op1=ALU.add),
                 (o32_b, rs_b, d32_b), (d32_b,))
            T.op("dve", lambda e, s=s: e.scalar_tensor_tensor(out=junk2[:], in0=d32[:, s, :], scalar=1.0, in1=d32[:, s, :],
                                                              op0=ALU.mult, op1=ALU.mult, accum_out=rs[:, 4 + s:5 + s]),
                 (d32_b,), (junk2_b, rs_b))
        T.op("dve", lambda e: e.tensor_scalar(out=rs[:, 4:4 + NS], in0=rs[:, 4:4 + NS], scalar1=1.0 / 256.0, scalar2=EPS,
                                              op0=ALU.mult, op1=ALU.add), (rs_b,), (rs_b,))
        oi = acount["o"] % 2
        acount["o"] += 1

        def finB(d32=d32, d32_b=d32_b, rs=rs, rs_b=rs_b, oi=oi, NS=NS, QW=QW, out_store=out_store):
            T.op("act", lambda e: e.activation(out=rs[:, 4:4 + NS], in_=rs[:, 4:4 + NS], func=AF.Ln), (rs_b,), (rs_b,))
            T.op("act", lambda e: e.activation(out=rs[:, 4:4 + NS], in_=rs[:, 4:4 + NS], func=AF.Exp, scale=-0.5), (rs_b,), (rs_b,))
            for s in range(NS):
                T.op("dve", lambda e, s=s: e.scalar_tensor_tensor(out=onb[:, s, :], in0=d32[:, s, :], scalar=rs[:, 4 + s:5 + s], in1=subg[:],
                                                                  op0=ALU.mult, op1=ALU.mult), (d32_b, rs_b, B_const), (onb_b,))
            for s in range(NS):
                for eh in range(2):
                    T.op("pe", lambda e, s=s, eh=eh: e.transpose(out=pt2[oi][:, (eh * 2 + s) * 128:(eh * 2 + s + 1) * 128],
                                                                 in_=onb[:, s, eh * 128:(eh + 1) * 128], identity=ident[:]),
                         (onb_b, B_const), (pt2_b[oi],))
            T.op("act", lambda e: e.activation(out=oT_st[oi][:, :, 0:QW],
                                               in_=pt2[oi][:, 0:512].rearrange("p (a t) -> p a t", a=2)[:, :, 0:QW], func=AF.Copy),
                 (pt2_b[oi],), (oT_b[oi],))
            out_store(oT_st[oi], oT_b[oi], oT_ds[oi])
        fin_pending.append(finB)

    def fin_flush(keep):
        while len(fin_pending) > keep:
            fin_pending.pop(0)()

    for h in range(4):
        ab = h % 2
        kb, vb, qb = kvq_b[ab]
        T.dma("sp", kTt[ab][:], kT_s[h * 2:h * 2 + 2, :, 0:4096].rearrange("c d t -> d c t"), (), (kb,), kvq_ds[ab])
        T.dma("sp", vat[ab][:, :, 0:256], v_s[0:4096, h * 256:(h + 1) * 256].rearrange("(g p) e -> p g e", p=128),
              (), (vb,), kvq_ds[ab])
        T.dma("sp", qTt[ab][:], qT_s[h * 4:h * 4 + 4, :, 0:2048].rearrange("s d t -> d s t"), (), (qb,), kvq_ds[ab])
        kvb = Buf()
        for g in range(2):
            for Q in range(8):
                A = NPRE + Q * 256
                nfull = A // 128

                def kT_of(c, j, ab=ab):
                    return kTt[ab][:, c, j * 128:(j + 1) * 128]

                def v_of(j, ab=ab):
                    return vat[ab][:, j, 0:257], 128

                def q_of(c, col0, ab=ab, g=g, Q=Q):
                    return qTt[ab][:, g * 2 + c, Q * 256 + col0:(Q + 1) * 256]

                def bias_of(j):
                    return PREB if j < 16 else 0.0

                def out_store(ot, otb, ods, h=h, g=g, Q=Q):
                    ch = (h * 2 + g) * 2
                    T.dma("sp", mixT_s[ch:ch + 2, :, Q * 256:(Q + 1) * 256].rearrange("a d t -> d a t"), ot[:, :, :],
                          (otb,), (), ods)

                attn_block(kT_of, v_of, q_of, bias_of, nfull + 2, nfull, 2, (kb, vb), (qb,), out_store)
    fin_flush(0)
    T.barrier()
    with nc.Block() as block:
        T.replay(block)

    kTs = kTt
    for sq in range(4):
        for h in range(4):
            ab = (sq * 4 + h) % 2
            kb, vb, qb = kvq_b[ab]
            kcb, kcb_b, kcb_ds = kcbs[ab], kcbs_b[ab], kcbs_ds[ab]
            T.dma("pool", kcb[:], ck[sq, :, h * 256:(h + 1) * 256].rearrange("(g p) e -> p g e", p=128), (), (kcb_b,), kcb_ds)
            for gg in range(16):
                for c in range(2):
                    idx = gg * 2 + c
                    pti = (idx // 8) % 2
                    T.op("pe", lambda e, gg=gg, c=c, idx=idx, pti=pti: e.transpose(
                        out=pt2[pti][:, (idx % 8) * 128:(idx % 8 + 1) * 128], in_=kcb[:, gg, c * 128:(c + 1) * 128],
                        identity=ident[:]), (kcb_b, B_const), (pt2_b[pti],))
                if gg % 4 == 3:
                    pti = ((gg * 2) // 8) % 2
                    g0 = gg - 3
                    T.op("dve", lambda e, pti=pti, g0=g0, ab=ab: e.tensor_copy(
                        out=kTs[ab][:, :, g0 * 128:(g0 + 4) * 128].rearrange("p c (g t) -> p g c t", g=4),
                        in_=pt2[pti][:].rearrange("p (g c t) -> p g c t", g=4, c=2)), (pt2_b[pti],), (kb,))
            T.dma("sp", kTs[ab][:, :, 2048:2112],
                  kT_s[h * 2:h * 2 + 2, :, NPRE + NMAIN + sq * 64:NPRE + NMAIN + sq * 64 + 64].rearrange("c d t -> d c t"),
                  (), (kb,), kvq_ds[ab])
            T.dma("pool", vat[ab][:, 0:16, 0:256], cv[sq, :, h * 256:(h + 1) * 256].rearrange("(g p) e -> p g e", p=128),
                  (), (vb,), kvq_dsp[ab])
            r0 = NPRE + NMAIN + sq * 64
            T.dma("sp", vat[ab][0:64, 16, 0:256], v_s[r0:r0 + 64, h * 256:(h + 1) * 256], (), (vb,), kvq_ds[ab])
            c0 = NMAIN + sq * 64
            for g_ in range(2):
                for c_ in range(2):
                    T.dma("sp", qTt[ab][:, c_, g_ * 64:(g_ + 1) * 64], qT_s[h * 4 + g_ * 2 + c_, :, c0:c0 + 64],
                          (), (qparts_b[ab][g_ * 2 + c_],), kvq_ds[ab])

            def kT_of(c, j, ab=ab):
                return kTs[ab][:, c, j * 128:min((j + 1) * 128, 2112)]

            def v_of(j, ab=ab):
                nk = 128 if j < 16 else 64
                return vat[ab][0:nk, j, 0:257], nk

            def q_of(c, col0, ab=ab):
                return qTt[ab][:, c, 0:128]

            def bias_of(j):
                return 0.0

            def out_store(ot, otb, ods, h=h, sq=sq):
                for g in range(2):
                    ch = (h * 2 + g) * 2
                    c0 = NMAIN + sq * 64
                    T.dma("sp", mixT_s[ch:ch + 2, :, c0:c0 + 64].rearrange("a d t -> d a t"), ot[:, :, g * 64:(g + 1) * 64],
                          (otb,), (), ods)

            attn_block(kT_of, v_of, q_of, bias_of, 17, None, 1, (kb, vb), tuple(qparts_b[ab]), out_store)
    fin_flush(0)
    T.barrier()
    with nc.Block() as block:
        T.replay(block)
    st.close()
    if upto == 2:
        top.close()
        return nc

    st = ExitStack()
    ws_begin(st, 4, plan3)
    hh = sb(st, "hh", [128, 4, D], F32)
    hh_b = [Buf() for _ in range(4)]
    hh_ds = [DS() for _ in range(4)]
    actT = sb(st, "actT", [128, 32, 512], BF16)
    actT_b = [Buf() for _ in range(4)]
    actT_ds = DS()
    hsb = sb(st, "hsb", [128, D], BF16)
    hsb_b = Buf()
    gfin = sb(st, "gfin", [128, D], F32)
    ssq3 = sb(st, "ssq3", [128, 8], F32)
    ssq3_b = Buf()
    z32 = [sb(st, "z32_%d" % i, [128, 512], F32) for i in range(2)]
    z32_b = [Buf() for _ in range(2)]
    zT = [sb(st, "zT%d" % i, [128, 4, 512], BF16) for i in range(2)]
    zT_b = [Buf() for _ in range(2)]
    ptx1 = ps(st, "ptx3", [128, 1024], BF16)
    ptx = [ptx1, ptx1]
    ptx1_b = Buf()
    ptx_b = [ptx1_b, ptx1_b]
    pp = [ps(st, "pp3_%d" % i, [128, 512], F32) for i in range(4)]
    pp_b = [Buf() for _ in range(4)]
    pd = [ps(st, "pd%d" % i, [128, 512], F32) for i in range(3)]
    pd_b = [Buf() for _ in range(3)]
    T.dma("sp", gfin[:], gfin_d[:, :], (), (B_const,), ds_const)

    pcount = 0
    for kind, ti in s3_tiles:
        if kind == "main":
            xsrc, r0, NT, c0, ydst = xm, ti * 512, 512, ti * 512, y_main
        else:
            xsrc, r0, NT, c0, ydst = xs, 0, 256, NMAIN, y_smp
        NG = NT // 128
        T.dma("sp", actT[:, :, 0:NT], mixT_s[:, :, c0:c0 + NT].rearrange("k d t -> d k t"), (), tuple(actT_b[:NG]), actT_ds)
        for g in range(NG):
            T.dma("sp", hh[:, g, :], xsrc[r0 + g * 128:r0 + (g + 1) * 128, :], (), (hh_b[g],), hh_ds[g])
        for cb in range(8):
            for kh in range(2):
                slot, slb = w_next(("out", cb * 512, kh))
                sl3 = slot[:].rearrange("p (kc c) -> p kc c", kc=16)
                for g in range(NG):
                    for kc in range(16):
                        T.op("pe", lambda e, g=g, kc=kc, kh=kh, sl3=sl3: e.matmul(
                            pp[g][:, :], lhsT=actT[:, kh * 16 + kc, g * 128:(g + 1) * 128], rhs=sl3[:, kc, :],
                            start=(kh == 0 and kc == 0), stop=(kh == 1 and kc == 15)), (actT_b[g], slb), (pp_b[g],))
            for g in range(NG):
                T.op("dve", lambda e, g=g, cb=cb: e.tensor_tensor(out=hh[:, g, cb * 512:(cb + 1) * 512],
                                                                  in0=pp[g][:, :], in1=hh[:, g, cb * 512:(cb + 1) * 512], op=ALU.add),
                     (pp_b[g], hh_b[g]), (hh_b[g],))
        for g in range(NG):
            norm_transpose(hh[:, g, :], hh_b[g], actT, actT_b[g], g, 32, ptx, ptx_b, hsb, hsb_b, ssq3[:, g:g + 1], ssq3_b)
        pend = []

        def down(sbk, zi):
            nonlocal pcount
            slot_a, sla_b = w_next(("dA", sbk))
            slot_bb, slbb_b = w_next(("dB", sbk))
            sd = [slot_a[:].rearrange("p (f c) -> p f c", f=2), slot_bb[:].rearrange("p (f c) -> p f c", f=2)]
            sdb = [sla_b, slbb_b]
            for g in range(NG):
                for cb in range(8):
                    bk = pcount % 3
                    pcount += 1
                    for f in range(4):
                        T.op("pe", lambda e, g=g, cb=cb, f=f, bk=bk: e.matmul(
                            pd[bk][:, :], lhsT=zT[zi][:, f, g * 128:(g + 1) * 128], rhs=sd[f // 2][:, f % 2, cb * 512:(cb + 1) * 512],
                            start=(f == 0), stop=(f == 3)), (zT_b[zi], sdb[f // 2]), (pd_b[bk],))
                    T.op("dve", lambda e, g=g, cb=cb, bk=bk: e.tensor_tensor(
                        out=hh[:, g, cb * 512:(cb + 1) * 512], in0=pd[bk][:, :], in1=hh[:, g, cb * 512:(cb + 1) * 512], op=ALU.add),
                        (pd_b[bk], hh_b[g]), (hh_b[g],))

        for sbk in range(NSB):
            zi = sbk % 2
            for hf, nm in ((0, "upA"), (1, "upB")):
                slot_u, slu_b = w_next((nm, sbk))
                su3 = slot_u[:].rearrange("p (kc c) -> p kc c", kc=16)
                for f in range(4):
                    for kc in range(16):
                        T.op("pe", lambda e, f=f, kc=kc, su3=su3, hf=hf: e.matmul(
                            pp[f][:, 0:NT], lhsT=su3[:, kc, f * 128:(f + 1) * 128], rhs=actT[:, hf * 16 + kc, 0:NT],
                            start=(hf == 0 and kc == 0), stop=(hf == 1 and kc == 15)), tuple(actT_b[:NG]) + (slu_b,), (pp_b[f],))
            for f in range(4):
                T.op("act", lambda e, f=f: e.activation(out=z32[f % 2][:, 0:NT], in_=pp[f][:, 0:NT], func=AF.Relu),
                     (pp_b[f],), (z32_b[f % 2],))
                T.op("act", lambda e, f=f, zi=zi: e.activation(out=zT[zi][:, f, 0:NT], in_=z32[f % 2][:, 0:NT], func=AF.Square),
                     (z32_b[f % 2],), (zT_b[zi],))
            if pend:
                down(*pend.pop(0))
            pend.append((sbk, zi))
        down(*pend.pop(0))
        for g in range(NG):
            T.op("act", lambda e, g=g: e.activation(out=hsb[:], in_=hh[:, g, :], func=AF.Square, accum_out=ssq3[:, 4 + g:5 + g]),
                 (hh_b[g],), (hsb_b, ssq3_b))
            rstd_from_ss(ssq3[:, 4 + g:5 + g], float(D), (ssq3_b,))
            for hf in range(2):
                T.op("dve", lambda e, g=g, hf=hf: e.scalar_tensor_tensor(
                    out=hh[:, g, hf * 2048:(hf + 1) * 2048], in0=hh[:, g, hf * 2048:(hf + 1) * 2048], scalar=ssq3[:, 4 + g:5 + g],
                    in1=gfin[:, hf * 2048:(hf + 1) * 2048], op0=ALU.mult, op1=ALU.mult), (hh_b[g], ssq3_b, B_const), (hh_b[g],))
            T.dma("sp", ydst[r0 + g * 128:r0 + (g + 1) * 128, :], hh[:, g, :], (hh_b[g],), (), hh_ds[g])
    T.barrier()
    with nc.Block() as block:
        T.replay(block)
    st.close()
    top.close()
    return nc


_NC_CACHE = {}


def _rope_table(pos):
    half = 16
    inv = (np.float32(500000.0) ** (-np.arange(half, dtype=np.float32) / np.float32(half))).astype(np.float32)
    ang = pos.astype(np.float32)[:, None] * inv[None, :]
    cos = np.cos(ang).astype(np.float32)
    sin = np.sin(ang).astype(np.float32)
    return np.ascontiguousarray(np.concatenate([np.tile(cos, (1, 4)), np.tile(sin, (1, 4))], axis=1))


def kernel(x_prompt, x_sample, cache_k, cache_v, state_conv, state_lru,
           norm_mix, w_in, conv_w, conv_b, gate_a_w, gate_a_b, gate_x_w, gate_x_b,
           lru_lambda, lambda_q1, lambda_k1, lambda_q2, lambda_k2, subln_g,
           w_out, norm_mlp, w_up, w_down, norm_final):
    f32 = np.float32
    A = lambda a: np.ascontiguousarray(np.asarray(a, dtype=f32))
    x_prompt = A(x_prompt); x_sample = A(x_sample)
    cache_k = A(cache_k); cache_v = A(cache_v)
    state_conv = A(state_conv); state_lru = A(state_lru)
    upto = _NC_CACHE.get("upto", 3)
    if "nc" not in _NC_CACHE:
        _NC_CACHE["nc"] = build_program(upto)
    nc = _NC_CACHE["nc"]

    gvec = np.concatenate([A(norm_mix)[0].reshape(32, 128).T, A(norm_mlp)[0].reshape(32, 128).T], axis=1)
    gfin = np.tile(A(norm_final).reshape(1, D), (128, 1))
    lamv = np.tile(np.concatenate([A(lambda_q1)[0], A(lambda_k1)[0], A(lambda_q2)[0], A(lambda_k2)[0]]).reshape(1, 512), (128, 1))
    subg = np.tile(A(subln_g)[0].reshape(1, 256), (128, 1))
    ident = np.eye(128, dtype=f32)
    shared = dict(
        gvec=A(gvec), gfin=A(gfin), lamv=A(lamv), subg=A(subg), ident=ident,
        w_in=A(w_in)[0],
        w_out=A(w_out)[0] if upto >= 3 else np.zeros((128, 128), f32),
        w_up=A(w_up)[0] if upto >= 3 else np.zeros((128, 128), f32),
        w_down=A(w_down)[0] if upto >= 3 else np.zeros((128, 128), f32),
        ga_w=A(gate_a_w)[0], gx_w=A(gate_x_w)[0],
        cs_pre=_rope_table(np.arange(NPRE)),
        cs_smp=_rope_table(np.tile(PAST + np.arange(64), 4)),
    )
    in_maps = []
    for c in range(8):
        b, half = c // 2, c % 2
        rows = np.concatenate([
            A(conv_w)[0], A(conv_b), A(gate_a_b)[0].reshape(1, 2048), A(gate_x_b)[0].reshape(1, 2048), A(lru_lambda),
            state_conv[0, 4 * c:4 * c + 4].reshape(12, 2048), state_lru[0, 4 * c:4 * c + 4]], axis=0)
        constT = rows.reshape(24, 16, 128).transpose(2, 1, 0).reshape(128, 16 * 24)
        m = dict(shared)
        m.update(
            xm=x_prompt[b, half * 2048:(half + 1) * 2048],
            xpre=x_prompt[b, 0:2048],
            xs=x_sample[4 * c:4 * c + 4].reshape(256, D),
            ck=cache_k[0, 4 * c:4 * c + 4].reshape(4, PAST, 1024),
            cv=cache_v[0, 4 * c:4 * c + 4].reshape(4, PAST, 1024),
            constT=A(constT),
            valid=np.full((128, 1), float(half), dtype=f32),
            cs_main=_rope_table(half * 2048 + np.arange(NMAIN)),
        )
        in_maps.append({k: np.ascontiguousarray(v) for k, v in m.items()})
    if os.environ.get("DBG_CORES"):
        res = run_bass_kernel_spmd(nc, [in_maps[1]], core_ids=[0])
        R = [res.results[0]] * 8
    else:
        res = run_bass_kernel_spmd(nc, in_maps, core_ids=list(range(8)))
        R = res.results

    y_prompt = np.empty((4, 4096, D), f32)
    k_prompt = np.empty((1, 4, 4096, 4, 256), f32)
    v_prompt = np.empty((1, 4, 4096, 4, 256), f32)
    conv_prompt = np.empty((1, 4, 3, 2048), f32)
    lru_prompt = np.empty((1, 4, 2048), f32)
    y_sample = np.empty((32, 64, D), f32)
    k_sample = np.empty((1, 32, 64, 4, 256), f32)
    v_sample = np.empty((1, 32, 64, 4, 256), f32)
    conv_sample = np.empty((1, 32, 3, 2048), f32)
    lru_sample = np.empty((1, 32, 2048), f32)
    for c in range(8):
        b, half = c // 2, c % 2
        r = R[c]
        sl = slice(half * 2048, (half + 1) * 2048)
        y_prompt[b, sl] = r["y_main"]
        k_prompt[0, b, sl] = r["k_main"].reshape(2048, 4, 256)
        v_prompt[0, b, sl] = r["v_main"].reshape(2048, 4, 256)
        if half == 1:
            stm = r["st_main"].reshape(128, 16, 4)
            full = stm.transpose(2, 1, 0).reshape(4, 2048)
            conv_prompt[0, b] = full[0:3]
            lru_prompt[0, b] = full[3]
        y_sample[4 * c:4 * c + 4] = r["y_smp"].reshape(4, 64, D)
        k_sample[0, 4 * c:4 * c + 4] = r["k_smp"].reshape(4, 64, 4, 256)
        v_sample[0, 4 * c:4 * c + 4] = r["v_smp"].reshape(4, 64, 4, 256)
        sts = r["st_smp"].reshape(128, 16, 4, 4)
        fulls = sts.transpose(2, 3, 1, 0).reshape(4, 4, 2048)
        conv_sample[0, 4 * c:4 * c + 4] = fulls[:, 0:3]
        lru_sample[0, 4 * c:4 * c + 4] = fulls[:, 3]
    return (y_prompt, y_sample, k_prompt, v_prompt, conv_prompt, lru_prompt,
            k_sample, v_sample, conv_sample, lru_sample)
```
